# Optimizing a Trainium2 kernel written in Bass

```python
import math
import jax, jax.numpy as jnp
from jax import lax
import numpy as np

D_MODEL = 1024
BATCH = 8
SEQ = 4096
DEPTH = 1
DEC_BATCH = 8
DEC_SEQ = 2048
PAST_LEN = 128

D_FF = 2816
D_SSM = D_MODEL // 2
SSM_GROUP = 16
N_SSM_GROUPS = D_SSM // SSM_GROUP
SSM_STATE = 64
N_HEADS = 8
QK_NOPE = 64
QK_ROPE = 32
V_HEAD = 64
D_ATTN = N_HEADS * V_HEAD
Q_RANK = (3 * D_MODEL) // 8
KV_RANK = D_MODEL // 4
D_IN = D_SSM + Q_RANK + KV_RANK + QK_ROPE
D_MIX = D_SSM + D_ATTN
Q_BLOCK = 128
ROPE_THETA = 10000.0
EPS = 1e-6
DT_MIN = 1e-3
DT_MAX = 1e-1

kernel_name = 'hybrid_s5_mla_macaron_encoder'

F32 = jnp.float32


def rms_norm(x, g):
    xf = x.astype(F32)
    y = xf * lax.rsqrt(jnp.mean(xf * xf, axis=-1, keepdims=True) + EPS)
    return (y * g.astype(F32)).astype(x.dtype)


def swiglu(x, w_gate, w_up, w_down):
    return (jax.nn.silu(x @ w_gate) * (x @ w_up)) @ w_down


def rope_tables(length):
    inv = 1.0 / (ROPE_THETA ** (jnp.arange(0, QK_ROPE, 2, dtype=F32) / QK_ROPE))
    ang = jnp.arange(length, dtype=F32)[:, None] * inv[None, :]
    return jnp.cos(ang), jnp.sin(ang)


def apply_rope(x, cos, sin):
    x1, x2 = jnp.split(x.astype(F32), 2, axis=-1)
    return jnp.concatenate([x1 * cos - x2 * sin, x1 * sin + x2 * cos], axis=-1).astype(x.dtype)


def _linear_recurrence(e1, e2):
    a1, b1 = e1
    a2, b2 = e2
    return a1 * a2, a2 * b1 + b2


def ssm_direction(u_c, lam_re, lam_im, log_dt, b_re, b_im, c_re, c_im, reverse):
    lam = lax.complex(lam_re.astype(F32), lam_im.astype(F32))
    dt = jnp.exp(log_dt.astype(F32))[:, None]
    lam_bar = jnp.exp(lam * dt)
    b_mat = lax.complex(b_re.astype(F32), b_im.astype(F32))
    b_bar = ((lam_bar - 1.0) / lam)[..., None] * b_mat
    bu = jnp.einsum('blgc,gpc->blgp', u_c, b_bar)
    a = jnp.broadcast_to(lam_bar, bu.shape)
    _, s = lax.associative_scan(_linear_recurrence, (a, bu), axis=1, reverse=reverse)
    c_mat = lax.complex(c_re.astype(F32), c_im.astype(F32))
    return jnp.real(jnp.einsum('blgp,gcp->blgc', s, c_mat))


def s5_mixer(u, fwd, bwd, d_skip, w_glu, b_glu):
    bsz, length, _ = u.shape
    ug = u.reshape(bsz, length, N_SSM_GROUPS, SSM_GROUP).astype(F32)
    u_c = ug.astype(jnp.complex64)
    y = (ssm_direction(u_c, *fwd, reverse=False)
         + ssm_direction(u_c, *bwd, reverse=True)
         + d_skip.astype(F32).reshape(N_SSM_GROUPS, SSM_GROUP) * ug)
    y = jax.nn.gelu(y.reshape(bsz, length, D_SSM)).astype(u.dtype)
    return y * jax.nn.sigmoid(y @ w_glu + b_glu)


def mla_mixer(q_c, kv_c, k_rope, g_q, w_uq, g_kv, w_ukv):
    bsz, length, _ = q_c.shape
    q = (rms_norm(q_c, g_q) @ w_uq).reshape(bsz, length, N_HEADS, QK_NOPE + QK_ROPE)
    kv = (rms_norm(kv_c, g_kv) @ w_ukv).reshape(bsz, length, N_HEADS, QK_NOPE + V_HEAD)
    cos, sin = rope_tables(length)
    scale = (QK_NOPE + QK_ROPE) ** -0.5
    q_nope = q[..., :QK_NOPE] * scale
    q_pe = apply_rope(q[..., QK_NOPE:], cos[:, None, :], sin[:, None, :]) * scale
    k_pe = apply_rope(k_rope, cos, sin)
    k_nope, v = kv[..., :QK_NOPE], kv[..., QK_NOPE:]
    n_blk = length // Q_BLOCK
    qn_b = q_nope.reshape(bsz, n_blk, Q_BLOCK, N_HEADS, QK_NOPE).transpose(1, 0, 2, 3, 4)
    qp_b = q_pe.reshape(bsz, n_blk, Q_BLOCK, N_HEADS, QK_ROPE).transpose(1, 0, 2, 3, 4)

    def attend(blk):
        qn, qp = blk
        s = (jnp.einsum('bqhd,bkhd->bhqk', qn, k_nope, preferred_element_type=F32)
             + jnp.einsum('bqhr,bkr->bhqk', qp, k_pe, preferred_element_type=F32))
        p = jax.nn.softmax(s, axis=-1).astype(v.dtype)
        return jnp.einsum('bhqk,bkhd->bqhd', p, v)

    o = lax.map(attend, (qn_b, qp_b))
    return o.transpose(1, 0, 2, 3, 4).reshape(bsz, length, D_ATTN)


def setup_inputs(seed: int = 0) -> dict:
    key = jax.random.key(seed)
    ks = iter(jax.random.split(key, 64))

    def nrm(shape, scale):
        return scale * jax.random.normal(next(ks), shape, dtype=F32)

    def gain(n):
        return 1.0 + nrm((DEPTH, n), 0.02)

    G, P, C = N_SSM_GROUPS, SSM_STATE, SSM_GROUP

    def ssm_dir():
        lam_re = -0.5 + nrm((DEPTH, G, P), 0.01)
        lam_im = math.pi * jnp.arange(P, dtype=F32)[None, None, :] + nrm((DEPTH, G, P), 0.01)
        log_dt = jax.random.uniform(next(ks), (DEPTH, G), F32, math.log(DT_MIN), math.log(DT_MAX))
        b_re = nrm((DEPTH, G, P, C), (2.0 * C) ** -0.5)
        b_im = nrm((DEPTH, G, P, C), (2.0 * C) ** -0.5)
        c_re = nrm((DEPTH, G, C, P), (2.0 * P) ** -0.5)
        c_im = nrm((DEPTH, G, C, P), (2.0 * P) ** -0.5)
        return lam_re, lam_im, log_dt, b_re, b_im, c_re, c_im

    x_prompt = nrm((BATCH, SEQ, D_MODEL), 1.0)
    x_sample = nrm((DEC_BATCH, DEC_SEQ, D_MODEL), 1.0)
    g_ffn1_pre = gain(D_MODEL)
    w_ffn1_gate = nrm((DEPTH, D_MODEL, D_FF), D_MODEL ** -0.5)
    w_ffn1_up = nrm((DEPTH, D_MODEL, D_FF), D_MODEL ** -0.5)
    w_ffn1_down = nrm((DEPTH, D_FF, D_MODEL), D_FF ** -0.5)
    g_ffn1_post = gain(D_MODEL)
    g_mix_pre = gain(D_MODEL)
    w_in = nrm((DEPTH, D_MODEL, D_IN), D_MODEL ** -0.5)
    lam_re_fwd, lam_im_fwd, log_dt_fwd, b_re_fwd, b_im_fwd, c_re_fwd, c_im_fwd = ssm_dir()
    lam_re_bwd, lam_im_bwd, log_dt_bwd, b_re_bwd, b_im_bwd, c_re_bwd, c_im_bwd = ssm_dir()
    d_skip = nrm((DEPTH, D_SSM), 1.0)
    w_glu = nrm((DEPTH, D_SSM, D_SSM), D_SSM ** -0.5)
    b_glu = nrm((DEPTH, D_SSM), 0.02)
    g_ssm_out = gain(D_SSM)
    g_q = gain(Q_RANK)
    w_uq = nrm((DEPTH, Q_RANK, N_HEADS * (QK_NOPE + QK_ROPE)), Q_RANK ** -0.5)
    g_kv = gain(KV_RANK)
    w_ukv = nrm((DEPTH, KV_RANK, N_HEADS * (QK_NOPE + V_HEAD)), KV_RANK ** -0.5)
    g_att_out = gain(D_ATTN)
    w_out = nrm((DEPTH, D_MIX, D_MODEL), D_MIX ** -0.5)
    g_mix_post = gain(D_MODEL)
    g_ffn2_pre = gain(D_MODEL)
    w_ffn2_gate = nrm((DEPTH, D_MODEL, D_FF), D_MODEL ** -0.5)
    w_ffn2_up = nrm((DEPTH, D_MODEL, D_FF), D_MODEL ** -0.5)
    w_ffn2_down = nrm((DEPTH, D_FF, D_MODEL), D_FF ** -0.5)
    g_ffn2_post = gain(D_MODEL)
    return {
        'x_prompt': x_prompt, 'x_sample': x_sample,
        'g_ffn1_pre': g_ffn1_pre, 'w_ffn1_gate': w_ffn1_gate, 'w_ffn1_up': w_ffn1_up,
        'w_ffn1_down': w_ffn1_down, 'g_ffn1_post': g_ffn1_post,
        'g_mix_pre': g_mix_pre, 'w_in': w_in,
        'lam_re_fwd': lam_re_fwd, 'lam_im_fwd': lam_im_fwd, 'log_dt_fwd': log_dt_fwd,
        'b_re_fwd': b_re_fwd, 'b_im_fwd': b_im_fwd, 'c_re_fwd': c_re_fwd, 'c_im_fwd': c_im_fwd,
        'lam_re_bwd': lam_re_bwd, 'lam_im_bwd': lam_im_bwd, 'log_dt_bwd': log_dt_bwd,
        'b_re_bwd': b_re_bwd, 'b_im_bwd': b_im_bwd, 'c_re_bwd': c_re_bwd, 'c_im_bwd': c_im_bwd,
        'd_skip': d_skip, 'w_glu': w_glu, 'b_glu': b_glu, 'g_ssm_out': g_ssm_out,
        'g_q': g_q, 'w_uq': w_uq, 'g_kv': g_kv, 'w_ukv': w_ukv, 'g_att_out': g_att_out,
        'w_out': w_out, 'g_mix_post': g_mix_post,
        'g_ffn2_pre': g_ffn2_pre, 'w_ffn2_gate': w_ffn2_gate, 'w_ffn2_up': w_ffn2_up,
        'w_ffn2_down': w_ffn2_down, 'g_ffn2_post': g_ffn2_post,
    }


def reference(x_prompt, x_sample,
              g_ffn1_pre, w_ffn1_gate, w_ffn1_up, w_ffn1_down, g_ffn1_post,
              g_mix_pre, w_in,
              lam_re_fwd, lam_im_fwd, log_dt_fwd, b_re_fwd, b_im_fwd, c_re_fwd, c_im_fwd,
              lam_re_bwd, lam_im_bwd, log_dt_bwd, b_re_bwd, b_im_bwd, c_re_bwd, c_im_bwd,
              d_skip, w_glu, b_glu, g_ssm_out,
              g_q, w_uq, g_kv, w_ukv, g_att_out,
              w_out, g_mix_post,
              g_ffn2_pre, w_ffn2_gate, w_ffn2_up, w_ffn2_down, g_ffn2_post):

    def trunk(x):
        for l in range(DEPTH):
            h = rms_norm(x, g_ffn1_pre[l])
            x = x + 0.5 * rms_norm(swiglu(h, w_ffn1_gate[l], w_ffn1_up[l], w_ffn1_down[l]), g_ffn1_post[l])
            h = rms_norm(x, g_mix_pre[l])
            z = h @ w_in[l]
            u, q_c, kv_c, k_rope = jnp.split(
                z, [D_SSM, D_SSM + Q_RANK, D_SSM + Q_RANK + KV_RANK], axis=-1)
            fwd = (lam_re_fwd[l], lam_im_fwd[l], log_dt_fwd[l], b_re_fwd[l], b_im_fwd[l],
                   c_re_fwd[l], c_im_fwd[l])
            bwd = (lam_re_bwd[l], lam_im_bwd[l], log_dt_bwd[l], b_re_bwd[l], b_im_bwd[l],
                   c_re_bwd[l], c_im_bwd[l])
            y_ssm = rms_norm(s5_mixer(u, fwd, bwd, d_skip[l], w_glu[l], b_glu[l]), g_ssm_out[l])
            y_att = rms_norm(mla_mixer(q_c, kv_c, k_rope, g_q[l], w_uq[l], g_kv[l], w_ukv[l]),
                             g_att_out[l])
            m = jnp.concatenate([y_ssm, y_att], axis=-1) @ w_out[l]
            x = x + rms_norm(m, g_mix_post[l])
            h = rms_norm(x, g_ffn2_pre[l])
            x = x + 0.5 * rms_norm(swiglu(h, w_ffn2_gate[l], w_ffn2_up[l], w_ffn2_down[l]), g_ffn2_post[l])
        return x

    y_prompt = trunk(x_prompt)
    y_sample = trunk(x_sample)
    return (y_prompt, y_sample)
```

```python
import math
from contextlib import ExitStack

import numpy as np
import concourse.bass as bass
import concourse.mybir as mybir
from concourse.bass_utils import run_bass_kernel_spmd

F32 = mybir.dt.float32
BF16 = mybir.dt.bfloat16
I32 = mybir.dt.int32
ALU = mybir.AluOpType
AF = mybir.ActivationFunctionType
AX = mybir.AxisListType

D = 1024
DFF = 2816
NFF = DFF // 128
DIN = 1184
LP = 4096
LS = 2048
NTOK = LP + LS
EPS = 1e-6
TWO_PI = 2.0 * math.pi


class Buf:
    __slots__ = ("name", "last_w", "readers")

    def __init__(self, name):
        self.name = name
        self.last_w = None
        self.readers = []


class Op:
    __slots__ = ("eng", "name", "args", "kw", "dma", "lane", "laneval", "deps", "sig", "sigval")


class Sched:
    ENGS = ("pe", "act", "dve", "pool", "sp")

    def __init__(self, nc, lanes_per_queue=8):
        self.nc = nc
        self.ops = {e: [] for e in self.ENGS}
        self.nl = lanes_per_queue
        self.lane_rr = {e: 0 for e in self.ENGS}
        self.lane_last = {}
        self.lane_cnt = {}
        self.out_dmas = []

    def op(self, eng, name, *args, reads=(), writes=(), dma=False, final=False, **kw):
        o = Op()
        o.eng = eng
        o.name = name
        o.args = args
        o.kw = kw
        o.dma = dma
        o.sig = False
        o.sigval = 0
        o.deps = []
        o.lane = None
        o.laneval = 0
        deps = o.deps
        for b in reads:
            w = b.last_w
            if w is not None and (w.dma or not (w.eng == eng == "pe")):
                deps.append(w)
        for b in writes:
            w = b.last_w
            if w is not None and (w.dma or not (w.eng == eng == "pe")):
                deps.append(w)
            for r in b.readers:
                if r.dma or not (r.eng == eng == "pe"):
                    deps.append(r)
        if dma:
            l = self.lane_rr[eng]
            self.lane_rr[eng] = (l + 1) % self.nl
            key = (eng, l)
            prev = self.lane_last.get(key)
            if prev is not None:
                deps.append(prev)
            self.lane_last[key] = o
            c = self.lane_cnt.get(key, 0) + 1
            self.lane_cnt[key] = c
            o.lane = key
            o.laneval = 16 * c
            o.sig = True
            if final:
                self.out_dmas.append(o)
        for d in deps:
            d.sig = True
        for b in reads:
            b.readers.append(o)
        for b in writes:
            b.last_w = o
            b.readers = []
        self.ops[eng].append(o)
        return o

    def barrier(self):
        lasts = []
        for e in self.ENGS:
            for o in reversed(self.ops[e]):
                if not o.dma:
                    lasts.append(o)
                    break
        lasts += list(self.lane_last.values())
        for e in self.ENGS:
            o = Op()
            o.eng = e
            o.name = "nop"
            o.args = ()
            o.kw = {}
            o.dma = False
            o.sig = False
            o.sigval = 0
            o.lane = None
            o.laneval = 0
            o.deps = list(lasts)
            for d in o.deps:
                d.sig = True
            self.ops[e].append(o)

    def emit(self, stack):
        nc = self.nc
        sems = {e: stack.enter_context(nc.semaphore("s_" + e)) for e in self.ENGS}
        lsems = {}
        for key in self.lane_cnt:
            lsems[key] = stack.enter_context(nc.semaphore("l_%s%d" % key))
        for e in self.ENGS:
            c = 0
            for o in self.ops[e]:
                if not o.dma and o.sig:
                    c += 1
                    o.sigval = c
        fw = {}
        for o in self.out_dmas:
            if o.lane not in fw or fw[o.lane] < o.laneval:
                fw[o.lane] = o.laneval
        final_waits = [(lsems[k], v) for k, v in fw.items()]

        def replay(e, h):
            waited = {}
            for o in self.ops[e]:
                for d in o.deps:
                    if d.dma:
                        s, v = lsems[d.lane], d.laneval
                        k = ("l",) + d.lane
                    else:
                        s, v = sems[d.eng], d.sigval
                        k = d.eng
                    if waited.get(k, 0) >= v:
                        continue
                    waited[k] = v
                    h.wait_ge(s, v)
                ins = getattr(h, o.name)(*o.args, **o.kw)
                if o.dma:
                    ins.then_inc(lsems[o.lane], 16)
                elif o.sig:
                    ins.then_inc(sems[e], 1)
            if e == "sp":
                for s, v in final_waits:
                    h.wait_ge(s, v)

        with nc.Block() as block:
            @block.tensor
            def _(h):
                replay("pe", h)

            @block.scalar
            def _(h):
                replay("act", h)

            @block.vector
            def _(h):
                replay("dve", h)

            @block.gpsimd
            def _(h):
                replay("pool", h)

            @block.sync
            def _(h):
                replay("sp", h)


class Arena:
    def __init__(self, base_ap, nwords):
        self.base = base_ap
        self.n = nwords
        self.off = 0
        self.maxoff = 0

    def alloc(self, free_shape, dt):
        n = 1
        for s in free_shape:
            n *= s
        nw = n if dt in (F32, I32) else (n + 1) // 2
        assert self.off + nw <= self.n, ("arena overflow", self.off, nw, self.n)
        a = self.base[:, self.off:self.off + nw]
        self.off += nw
        self.maxoff = max(self.maxoff, self.off)
        if dt != F32:
            a = a.bitcast(dt)
            if dt == BF16 and n != 2 * nw:
                a = a[:, 0:n]
        if len(free_shape) == 2:
            a = a.rearrange("p (a b) -> p a b", a=free_shape[0])
        elif len(free_shape) == 3:
            a = a.rearrange("p (a b c) -> p a b c", a=free_shape[0], b=free_shape[1])
        return a


def build(stage="full", LB=1024, bstop=9):
    nc = bass.Bass("TRN2", target_bir_lowering=False)

    def din(name, shape):
        return nc.dram_tensor(name, list(shape), F32, kind="ExternalInput").ap()

    x_in = din("x", [NTOK, D]) if stage != "B" else None
    W = {}
    for f in ((1, 2) if stage != "B" else ()):
        W["g_pre%d" % f] = din("g_ffn%d_pre" % f, [1, D])
        W["wg%d" % f] = din("w_ffn%d_gate" % f, [D, DFF])
        W["wu%d" % f] = din("w_ffn%d_up" % f, [D, DFF])
        W["wd%d" % f] = din("w_ffn%d_down" % f, [DFF, D])
        W["g_post%d" % f] = din("g_ffn%d_post" % f, [1, D])
    if stage != "B":
        W["g_mix_pre"] = din("g_mix_pre", [1, D])
        W["w_in"] = din("w_in", [D, DIN])
    y_out = nc.dram_tensor("y", [NTOK, D], F32, kind="ExternalOutput").ap()
    x1_d = nc.dram_tensor("x1_scr", [NTOK, D], F32, kind="Internal").ap()
    x2_d = nc.dram_tensor("x2_scr", [NTOK, D], F32, kind="Internal").ap()
    zq_d = nc.dram_tensor("zq_scr", [NTOK, 672], F32, kind="Internal").ap()
    uT_d = nc.dram_tensor("uT_scr", [512, NTOK], BF16, kind="Internal").ap()
    M = {}
    for nm, shp in (("w_glu", [512, 512]), ("w_uq", [384, 768]), ("w_ukv", [256, 1024]), ("w_out", [1024, 1024]),
                    ("d_skip", [1, 512]), ("b_glu", [1, 512]), ("g_ssm_out", [1, 512]), ("g_q", [1, 384]),
                    ("g_kv", [1, 256]), ("g_att_out", [1, 512]), ("g_mix_post", [1, 1024])):
        M[nm] = din(nm, shp)
    for dn in ("fwd", "bwd"):
        M["lam_re_" + dn] = din("lam_re_" + dn, [2048])
        M["lam_im_" + dn] = din("lam_im_" + dn, [2048])
        M["log_dt_" + dn] = din("log_dt_" + dn, [32])
        for pn in ("re", "im"):
            M["b_%s_%s" % (pn, dn)] = din("b_%s_%s" % (pn, dn), [32768])
            M["c_%s_%s" % (pn, dn)] = din("c_%s_%s" % (pn, dn), [32768])
    ys_d = nc.dram_tensor("ys_scr", [512, NTOK], BF16, kind="Internal").ap()
    b_ysd = Buf("ysd")
    b_x2d = Buf("x2d")
    if stage == "B":
        zq_src = din("t_zq", [LB, 672])
        uT_src = nc.dram_tensor("t_uT", [512, LB], BF16, kind="ExternalInput").ap()
        x1_src = din("t_x1", [LB, D])
        x2_dst = nc.dram_tensor("t_x2", [LB, D], F32, kind="ExternalOutput").ap()
    else:
        zq_src, uT_src, x1_src, x2_dst = zq_d, uT_d, x1_d, x2_d
    dbg = {}
    if stage == "A":
        dbg["x1"] = nc.dram_tensor("dbg_x1", [NTOK, D], F32, kind="ExternalOutput").ap()
        dbg["zq"] = nc.dram_tensor("dbg_zq", [NTOK, 672], F32, kind="ExternalOutput").ap()
        dbg["uT"] = nc.dram_tensor("dbg_uT", [512, NTOK], BF16, kind="ExternalOutput").ap()

    S = Sched(nc)
    op = S.op
    with ExitStack() as st:
        NW = 53000 - 1024
        arena_t = st.enter_context(nc.sbuf_tensor("arena", [128, 53000], F32))
        CT_g = arena_t[:, 53000 - 1024:53000].rearrange("p (a b c) -> p a b c", a=4, b=16)
        b_CT_g = Buf("CT")
        pall = st.enter_context(nc.psum_tensor("pall", [128, 4096], F32))
        banks = [pall[:, i * 512:(i + 1) * 512] for i in range(8)]

        ct_pending = []
        for di, dn in enumerate(("fwd", "bwd")):
            for part, pn in enumerate(("re", "im")):
                for gi in range(2):
                    for P in range(16):
                        ct_pending.append((di, dn, part, pn, gi, P))

        def issue_ct_loads(n=None):
            k = len(ct_pending) if n is None else min(n, len(ct_pending))
            for _ in range(k):
                di, dn, part, pn, gi, P = ct_pending.pop(0)
                src = M["c_%s_%s" % (pn, dn)]
                op("sp", "dma_start", out=CT_g[gi * 64:(gi + 1) * 64, di * 2 + part, P, :],
                   in_=bass.AP(src.tensor, gi * 1024 + P * 2048, [[1, 64], [64, 16]]),
                   writes=[b_CT_g], dma=True, allow_slow_non_contiguous=True)

        def ffn_phase(f, src_d, dst_d, with_win, ntiles=NTOK // 256):
            ar = Arena(arena_t[:, :], NW)
            wg = ar.alloc([8, DFF], BF16)
            wu = ar.alloc([8, DFF], BF16)
            wd = ar.alloc([NFF, D], BF16)
            b_wg = [[Buf("wg%d_%d" % (k, h)) for h in range(2)] for k in range(8)]
            b_wu = [[Buf("wu%d_%d" % (k, h)) for h in range(2)] for k in range(8)]
            b_wd = [Buf("wd%d" % k) for k in range(NFF)]
            if with_win:
                win = ar.alloc([8, DIN], BF16)
                b_win = [Buf("win%d" % k) for k in range(8)]
            gpost = ar.alloc([D], F32)
            b_gpost = Buf("gpost")
            gpreT = ar.alloc([8], F32)
            b_gpreT = Buf("gpreT")
            gmixT = ar.alloc([8], F32)
            b_gmixT = Buf("gmixT")
            idf = ar.alloc([128], F32)
            b_idf = Buf("idf")
            NPAR = 1 if with_win else 2
            xts = [[ar.alloc([D], F32) for _ in range(2)] for _ in range(NPAR)]
            b_xts = [[Buf("xt%d_%d" % (p, i)) for i in range(2)] for p in range(NPAR)]
            xt, b_xt = xts[0], b_xts[0]
            xn = ar.alloc([D], F32)
            b_xn = Buf("xn")
            hTs = [ar.alloc([8, 256], BF16) for _ in range(NPAR)]
            b_hTs = [[[Buf("hT%d_%d_%d" % (p, s, k)) for k in range(8)] for s in range(2)] for p in range(NPAR)]
            hT, b_hT = hTs[0], b_hTs[0]
            actT = ar.alloc([NFF, 256], BF16)
            b_actT = [Buf("actT%d" % c) for c in range(NFF)]
            sg = [ar.alloc([256], F32) for _ in range(2)]
            b_sg = [Buf("sg%d" % i) for i in range(2)]
            tt = [ar.alloc([512], F32) for _ in range(2)]
            b_tt = [Buf("tt%d" % i) for i in range(2)]
            junk = ar.alloc([D], BF16)
            b_junk = Buf("junkA")
            st_ss = ar.alloc([8], F32)
            b_ss = [Buf("ss%d" % i) for i in range(8)]
            if with_win:
                zqt = ar.alloc([672], F32)
                b_zqt = Buf("zqt")
                uTt = ar.alloc([4, 256], BF16)
                b_uTt = Buf("uTt")

            HC = DFF // 2
            for hf in range(2):
                for k in range(8):
                    op("pool", "dma_start", out=wg[:, k, hf * HC:(hf + 1) * HC],
                       in_=W["wg%d" % f][k * 128:(k + 1) * 128, hf * HC:(hf + 1) * HC], writes=[b_wg[k][hf]], dma=True)
                    op("pool", "dma_start", out=wu[:, k, hf * HC:(hf + 1) * HC],
                       in_=W["wu%d" % f][k * 128:(k + 1) * 128, hf * HC:(hf + 1) * HC], writes=[b_wu[k][hf]], dma=True)
            for c in range(NFF):
                op("pool", "dma_start", out=wd[:, c, :], in_=W["wd%d" % f][c * 128:(c + 1) * 128, :],
                   writes=[b_wd[c]], dma=True)
            if with_win:
                for k in range(8):
                    op("pool", "dma_start", out=win[:, k, :], in_=W["w_in"][k * 128:(k + 1) * 128, :],
                       writes=[b_win[k]], dma=True)
            op("sp", "dma_start", out=gpost, in_=W["g_post%d" % f].partition_broadcast(128),
               writes=[b_gpost], dma=True)
            op("sp", "dma_start", out=gpreT, in_=W["g_pre%d" % f][0, :].rearrange("(k p) -> p k", p=128),
               writes=[b_gpreT], dma=True, allow_slow_non_contiguous=True)
            if with_win:
                op("sp", "dma_start", out=gmixT, in_=W["g_mix_pre"][0, :].rearrange("(k p) -> p k", p=128),
                   writes=[b_gmixT], dma=True, allow_slow_non_contiguous=True)
            idi = ar.alloc([128], I32)
            op("pool", "iota", idi, [[1, 128]], base=0, channel_multiplier=-1, writes=[b_idf])
            op("dve", "tensor_copy", idf, idi, reads=[b_idf], writes=[b_idf])
            op("dve", "tensor_scalar", idf, idf, 0.0, None, ALU.is_equal, reads=[b_idf], writes=[b_idf])

            def rstd_from_ss(col, n_feat, extra_scale=1.0):
                c = st_ss[:, col:col + 1]
                op("dve", "tensor_scalar", c, c, 1.0 / n_feat, EPS, ALU.mult, ALU.add,
                   reads=[b_ss[col]], writes=[b_ss[col]])
                op("act", "sqrt", c, c, reads=[b_ss[col]], writes=[b_ss[col]])
                op("dve", "reciprocal", c, c, reads=[b_ss[col]], writes=[b_ss[col]])
                if extra_scale != 1.0:
                    op("dve", "tensor_scalar", c, c, extra_scale, None, ALU.mult,
                       reads=[b_ss[col]], writes=[b_ss[col]])

            def norm_transpose(s, gT, b_gT, sscol):
                op("act", "activation", junk, xt[s], AF.Square, accum_out=st_ss[:, sscol:sscol + 1],
                   reads=[b_xt[s]], writes=[b_junk, b_ss[sscol]])
                rstd_from_ss(sscol, D)
                op("dve", "tensor_scalar", xn, xt[s], st_ss[:, sscol:sscol + 1], None, ALU.mult,
                   reads=[b_xt[s], b_ss[sscol]], writes=[b_xn])
                for k in range(8):
                    bk = k // 4
                    sl = (k % 4) * 128
                    op("pe", "transpose", banks[bk][:, sl:sl + 128], xn[:, k * 128:(k + 1) * 128], idf,
                       reads=[b_xn, b_idf], writes=[b_tp[bk]])
                for k in range(8):
                    bk = k // 4
                    sl = (k % 4) * 128
                    if bk == 0:
                        op("act", "activation", hT[:, k, s * 128:(s + 1) * 128], banks[bk][:, sl:sl + 128],
                           AF.Copy, scale=gT[:, k:k + 1], reads=[b_tp[bk], b_gT], writes=[b_hT[s][k]])
                    else:
                        op("dve", "tensor_scalar", hT[:, k, s * 128:(s + 1) * 128], banks[bk][:, sl:sl + 128],
                           gT[:, k:k + 1], None, ALU.mult, reads=[b_tp[bk], b_gT], writes=[b_hT[s][k]])

            b_tp = [Buf("tp%d" % k) for k in range(2)]
            b_pg = [Buf("pg0"), Buf("pg1")]
            b_pd = [[Buf("pd%d%d" % (s, n)) for n in range(2)] for s in range(2)]

            def load_and_norm(ti):
                r0_ = ti * 256
                for s in range(2):
                    op("sp", "dma_start", out=xt[s], in_=src_d[r0_ + s * 128:r0_ + (s + 1) * 128, :],
                       writes=[b_xt[s]], dma=True)
                if with_win and stage == "full" and ti >= 1:
                    issue_ct_loads(8)
                for s in range(2):
                    norm_transpose(s, gpreT, b_gpreT, s)

            for ti in range(ntiles):
                r0 = ti * 256
                par_ = ti % NPAR
                xt, b_xt, hT, b_hT = xts[par_], b_xts[par_], hTs[par_], b_hTs[par_]
                if NPAR == 1 or ti == 0:
                    load_and_norm(ti)
                hreads = lambda k: [b_hT[0][k], b_hT[1][k]]
                for c in range(NFF):
                    pg = banks[2 + (c % 2)]
                    bpg = b_pg[c % 2]
                    for k in range(8):
                        op("pe", "matmul", pg[:, 0:256], wg[:, k, c * 128:(c + 1) * 128], hT[:, k, :],
                           start=(k == 0), stop=(k == 7), reads=[b_wg[k][c // 11]] + hreads(k), writes=[bpg])
                    for k in range(8):
                        op("pe", "matmul", pg[:, 256:512], wu[:, k, c * 128:(c + 1) * 128], hT[:, k, :],
                           start=(k == 0), stop=(k == 7), reads=[b_wu[k][c // 11]] + hreads(k), writes=[bpg])
                    op("act", "activation", sg[c % 2], pg[:, 0:256], AF.Silu, reads=[bpg], writes=[b_sg[c % 2]])
                    op("dve", "tensor_tensor", actT[:, c, :], sg[c % 2], pg[:, 256:512], ALU.mult,
                       reads=[b_sg[c % 2], bpg], writes=[b_actT[c]])
                if NPAR == 2 and ti + 1 < ntiles:
                    pn_ = (ti + 1) % 2
                    xt, b_xt, hT, b_hT = xts[pn_], b_xts[pn_], hTs[pn_], b_hTs[pn_]
                    load_and_norm(ti + 1)
                    xt, b_xt, hT, b_hT = xts[par_], b_xts[par_], hTs[par_], b_hTs[par_]
                for s in range(2):
                    for n in range(2):
                        pd = banks[4 + 2 * s + n]
                        for c in range(NFF):
                            op("pe", "matmul", pd, actT[:, c, s * 128:(s + 1) * 128],
                               wd[:, c, n * 512:(n + 1) * 512], start=(c == 0), stop=(c == NFF - 1),
                               reads=[b_actT[c], b_wd[c]], writes=[b_pd[s][n]])
                for s in range(2):
                    for n in range(2):
                        pd = banks[4 + 2 * s + n]
                        col = 2 + 2 * s + n
                        op("act", "activation", junk[:, 0:512], pd, AF.Square,
                           accum_out=st_ss[:, col:col + 1], reads=[b_pd[s][n]], writes=[b_junk, b_ss[col]])
                    c0 = 2 + 2 * s
                    op("dve", "tensor_tensor", st_ss[:, c0:c0 + 1], st_ss[:, c0:c0 + 1], st_ss[:, c0 + 1:c0 + 2],
                       ALU.add, reads=[b_ss[c0], b_ss[c0 + 1]], writes=[b_ss[c0]])
                    rstd_from_ss(c0, D, extra_scale=0.5)
                    for n in range(2):
                        pd = banks[4 + 2 * s + n]
                        op("dve", "scalar_tensor_tensor", tt[n], pd, st_ss[:, c0:c0 + 1],
                           gpost[:, n * 512:(n + 1) * 512], ALU.mult, ALU.mult,
                           reads=[b_pd[s][n], b_ss[c0], b_gpost], writes=[b_tt[n]])
                        op("pool", "tensor_tensor", xt[s][:, n * 512:(n + 1) * 512], tt[n],
                           xt[s][:, n * 512:(n + 1) * 512], ALU.add, reads=[b_tt[n], b_xt[s]], writes=[b_xt[s]])
                    op("pool", "dma_start", out=dst_d[r0 + s * 128:r0 + (s + 1) * 128, :], in_=xt[s],
                       reads=[b_xt[s]], dma=True, final=(not with_win))
                    if stage == "A" and with_win:
                        op("pool", "dma_start", out=dbg["x1"][r0 + s * 128:r0 + (s + 1) * 128, :], in_=xt[s],
                           reads=[b_xt[s]], dma=True, final=True)
                if not with_win:
                    continue
                for s in range(2):
                    norm_transpose(s, gmixT, b_gmixT, 6 + s)
                for uc in range(4):
                    pu = banks[2 + (uc % 2)]
                    bpu = b_pg[uc % 2]
                    for k in range(8):
                        op("pe", "matmul", pu[:, 0:256], win[:, k, uc * 128:(uc + 1) * 128], hT[:, k, :],
                           start=(k == 0), stop=(k == 7), reads=[b_win[k]] + hreads(k), writes=[bpu])
                    op("act", "copy", uTt[:, uc, :], pu[:, 0:256], reads=[bpu], writes=[b_uTt])
                op("pool", "dma_start", out=uT_d[:, r0:r0 + 256].rearrange("(c p) t -> p c t", p=128), in_=uTt,
                   reads=[b_uTt], dma=True)
                if stage == "A":
                    op("pool", "dma_start", out=dbg["uT"][:, r0:r0 + 256].rearrange("(c p) t -> p c t", p=128),
                       in_=uTt, reads=[b_uTt], dma=True, final=True)
                for s in range(2):
                    for (n0, nn, bi) in ((512, 512, 4 + 2 * s), (1024, 160, 5 + 2 * s)):
                        pz = banks[bi]
                        bpz = b_pd[s][bi - 4 - 2 * s]
                        for k in range(8):
                            op("pe", "matmul", pz[:, 0:nn], hT[:, k, s * 128:(s + 1) * 128], win[:, k, n0:n0 + nn],
                               start=(k == 0), stop=(k == 7), reads=[b_win[k], b_hT[s][k]], writes=[bpz])
                        if nn == 512:
                            op("act", "copy", zqt[:, 0:512], pz[:, 0:512], reads=[bpz], writes=[b_zqt])
                        else:
                            op("dve", "tensor_copy", zqt[:, 512:672], pz[:, 0:160], reads=[bpz], writes=[b_zqt])
                    op("pool", "dma_start", out=zq_d[r0 + s * 128:r0 + (s + 1) * 128, :], in_=zqt,
                       reads=[b_zqt], dma=True)
                    if stage == "A":
                        op("pool", "dma_start", out=dbg["zq"][r0 + s * 128:r0 + (s + 1) * 128, :], in_=zqt,
                           reads=[b_zqt], dma=True, final=True)

        def dap(ap, offset, dims):
            return bass.AP(ap.tensor, offset, [list(d) for d in dims])

        def mixer_phase(seqs):
            ar = Arena(arena_t[:, :], NW)
            QSCALE = 96.0 ** -0.5
            idf = ar.alloc([128], F32); b_idf = Buf("idfB")
            idi = ar.alloc([128], I32)
            idb = ar.alloc([128], BF16); b_idb = Buf("idb")
            ones = ar.alloc([128], F32); b_ones = Buf("ones")
            op("pool", "iota", idi, [[1, 128]], base=0, channel_multiplier=-1, writes=[b_idf])
            op("dve", "tensor_copy", idf, idi, reads=[b_idf], writes=[b_idf])
            op("dve", "tensor_scalar", idf, idf, 0.0, None, ALU.is_equal, reads=[b_idf], writes=[b_idf])
            op("dve", "tensor_copy", idb, idf, reads=[b_idf], writes=[b_idb])
            op("dve", "memset", ones, 1.0, writes=[b_ones])

            wglu = ar.alloc([4, 512], BF16); b_wglu = Buf("wglu")
            for k in range(4):
                op("pool", "dma_start", out=wglu[:, k, :], in_=M["w_glu"][k * 128:(k + 1) * 128, :],
                   writes=[b_wglu], dma=True)
            smallT = ar.alloc([3, 4], F32); b_smallT = Buf("smallT")
            for i, nm in enumerate(("d_skip", "b_glu", "g_ssm_out")):
                op("sp", "dma_start", out=smallT[:, i, :], in_=M[nm][0, :].rearrange("(k p) -> p k", p=128),
                   writes=[b_smallT], dma=True, allow_slow_non_contiguous=True)
            dskipT, bgluT, gssmT = smallT[:, 0, :], smallT[:, 1, :], smallT[:, 2, :]

            mark_params = ar.off
            NPT = 72
            PT = ar.alloc([NPT, 32], F32)
            b_PT = [Buf("PT%d" % i) for i in range(NPT)]
            pi_ = [0]

            def newp():
                i = pi_[0]; pi_[0] += 1
                return i
            def P_(i):
                return PT[:, i, :]
            LR, LI, LD = newp(), newp(), newp()
            for di, dn in enumerate(("fwd", "bwd")):
                op("sp", "dma_start", out=PT[:, LR, di * 16:(di + 1) * 16],
                   in_=M["lam_re_" + dn].rearrange("(P q) -> q P", q=128), writes=[b_PT[LR]], dma=True,
                   allow_slow_non_contiguous=True)
                op("sp", "dma_start", out=PT[:, LI, di * 16:(di + 1) * 16],
                   in_=M["lam_im_" + dn].rearrange("(P q) -> q P", q=128), writes=[b_PT[LI]], dma=True,
                   allow_slow_non_contiguous=True)
                for gi in range(2):
                    op("sp", "dma_start", out=PT[gi * 64:(gi + 1) * 64, LD, di * 16:(di + 1) * 16],
                       in_=dap(M["log_dt_" + dn], gi, [[0, 64], [2, 16]]), writes=[b_PT[LD]], dma=True,
                       allow_slow_non_contiguous=True)

            def tt_(o, a, b, alu):
                op("dve", "tensor_tensor", P_(o), P_(a), P_(b), alu, reads=[b_PT[a], b_PT[b]], writes=[b_PT[o]])
            def ts_(o, a, s1, s2, o0, o1=None):
                if o1 is None:
                    op("dve", "tensor_scalar", P_(o), P_(a), s1, None, o0, reads=[b_PT[a]], writes=[b_PT[o]])
                else:
                    op("dve", "tensor_scalar", P_(o), P_(a), s1, s2, o0, o1, reads=[b_PT[a]], writes=[b_PT[o]])
            def act_(o, a, fn, **kw):
                op("act", "activation", P_(o), P_(a), fn, reads=[b_PT[a]], writes=[b_PT[o]], **kw)

            KI = ar.alloc([32], I32); b_KI = Buf("KI")
            def sin_of(o, a, shift):
                m, kf = newp(), newp()
                ts_(m, a, shift, None, ALU.add)
                ts_(kf, m, 1.0 / TWO_PI, None, ALU.mult)
                op("dve", "tensor_copy", KI, P_(kf), reads=[b_PT[kf]], writes=[b_KI])
                op("dve", "tensor_copy", P_(kf), KI, reads=[b_KI], writes=[b_PT[kf]])
                op("dve", "scalar_tensor_tensor", P_(m), P_(kf), -TWO_PI, P_(m), ALU.mult, ALU.add,
                   reads=[b_PT[kf], b_PT[m]], writes=[b_PT[m]])
                ts_(kf, m, math.pi, -TWO_PI, ALU.is_gt, ALU.mult)
                tt_(m, m, kf, ALU.add)
                ts_(kf, m, -math.pi, TWO_PI, ALU.is_lt, ALU.mult)
                tt_(m, m, kf, ALU.add)
                act_(o, m, AF.Sin)

            DT, AR_, TH, R, SN, CS = newp(), newp(), newp(), newp(), newp(), newp()
            act_(DT, LD, AF.Exp)
            tt_(AR_, LR, DT, ALU.mult)
            tt_(TH, LI, DT, ALU.mult)
            act_(R, AR_, AF.Exp)
            sin_of(SN, TH, 0.0)
            sin_of(CS, TH, math.pi / 2)
            ABR, ABI, DEN, T1, T2, COR, COI = newp(), newp(), newp(), newp(), newp(), newp(), newp()
            tt_(ABR, R, CS, ALU.mult)
            tt_(ABI, R, SN, ALU.mult)
            NRT = newp()
            ts_(NRT, ABR, -1.0, None, ALU.add)
            tt_(DEN, LR, LR, ALU.mult)
            tt_(T1, LI, LI, ALU.mult)
            tt_(DEN, DEN, T1, ALU.add)
            op("dve", "reciprocal", P_(DEN), P_(DEN), reads=[b_PT[DEN]], writes=[b_PT[DEN]])
            tt_(T1, NRT, LR, ALU.mult)
            tt_(T2, ABI, LI, ALU.mult)
            tt_(T1, T1, T2, ALU.add)
            tt_(COR, T1, DEN, ALU.mult)
            tt_(T1, ABI, LR, ALU.mult)
            tt_(T2, NRT, LI, ALU.mult)
            tt_(T1, T1, T2, ALU.subtract)
            tt_(COI, T1, DEN, ALU.mult)
            MC, MS = [CS], [SN]
            for k in range(11):
                c2, s2 = newp(), newp()
                tt_(T1, MC[k], MC[k], ALU.mult)
                tt_(T2, MS[k], MS[k], ALU.mult)
                tt_(c2, T1, T2, ALU.subtract)
                tt_(T1, MC[k], MS[k], ALU.mult)
                ts_(s2, T1, 2.0, None, ALU.mult)
                MC.append(c2); MS.append(s2)
            AW_r, AW_i, NAW_i = [None, ABR], [None, ABI], [None]
            def cmul(ar_, ai_, br_, bi_):
                o_r, o_i = newp(), newp()
                tt_(T1, ar_, br_, ALU.mult)
                tt_(T2, ai_, bi_, ALU.mult)
                tt_(o_r, T1, T2, ALU.subtract)
                tt_(T1, ar_, bi_, ALU.mult)
                tt_(T2, ai_, br_, ALU.mult)
                tt_(o_i, T1, T2, ALU.add)
                return o_r, o_i
            a2 = cmul(ABR, ABI, ABR, ABI)
            a3 = cmul(a2[0], a2[1], ABR, ABI)
            a4 = cmul(a2[0], a2[1], a2[0], a2[1])
            for (r_, i_) in (a2, a3, a4):
                AW_r.append(r_); AW_i.append(i_)
            for pw in range(1, 5):
                n_ = newp()
                ts_(n_, AW_i[pw], -1.0, None, ALU.mult)
                NAW_i.append(n_)
            R4 = newp()
            tt_(R4, R, R, ALU.mult)
            tt_(R4, R4, R4, ALU.mult)
            assert pi_[0] <= NPT, pi_[0]

            def combo(P, di, part):
                return (di * 16 + P) * 2 + part
            def w1i(q, di, part, tap):
                return ((q * 2 + di) * 2 + part) * 4 + tap
            def cab(j, pw, var):
                return (j * 4 + (pw - 1)) * 2 + var
            def ca32(j, pw, var):
                return (j * 5 + pw) * 2 + var
            def kdi(q, di, d):
                return (q * 2 + di) * 4 + d
            W1c = ar.alloc([64, 128], BF16); b_W1c = Buf("W1c")
            CAc = ar.alloc([256, 32], BF16); b_CAc = Buf("CAc")
            Kd = ar.alloc([32, 128], BF16); b_Kd = Buf("Kd")
            mark_b1 = ar.off
            BnP = ar.alloc([64, 128], F32); b_BnP = Buf("BnP")
            op("pool", "memset", BnP, 0.0, writes=[b_BnP])
            for di, dn in enumerate(("fwd", "bwd")):
                for part, pn in enumerate(("re", "im")):
                    src = M["b_%s_%s" % (pn, dn)]
                    for g in range(32):
                        P, gi = g // 2, g % 2
                        c0 = 32 * (P % 4) + gi * 16
                        op("sp", "dma_start", out=BnP[gi * 64:(gi + 1) * 64, combo(P, di, part), c0:c0 + 16],
                           in_=dap(src, g * 1024, [[16, 64], [1, 16]]), writes=[b_BnP], dma=True)
            issue_ct_loads()
            CT, b_CT = CT_g, b_CT_g
            ctmp = ar.alloc([8, 16], F32); b_ctmp = Buf("ctmp")
            CA32 = ar.alloc([320, 32], F32); b_CA32 = Buf("CA32")
            op("pool", "memset", CA32, 0.0, writes=[b_CA32])
            cw_ = [ar.alloc([16, 16], F32) for _ in range(6)]; b_cw = [Buf("cw%d" % i) for i in range(6)]
            cpr, cpi, ct0, ct1, car, cai = cw_
            b_cpr, b_cpi, b_ct0, b_ct1, b_car, b_cai = b_cw

            def bcP(tile_idx, di):
                t = PT[:, tile_idx, di * 16:(di + 1) * 16]
                return bass.AP(t.tensor, t.offset, [list(t.ap[0]), [1, 16], [0, 16]])

            for di in range(2):
                cr, ci = CT[:, di * 2, :, :], CT[:, di * 2 + 1, :, :]
                rd = [b_CT, b_PT[COR], b_PT[COI]]
                op("dve", "tensor_tensor", ct0, ci, bcP(COI, di), ALU.mult, reads=rd, writes=[b_ct0])
                op("dve", "tensor_tensor", cpr, cr, bcP(COR, di), ALU.mult, reads=rd, writes=[b_cpr])
                op("dve", "tensor_tensor", cpr, cpr, ct0, ALU.subtract, reads=[b_cpr, b_ct0], writes=[b_cpr])
                op("dve", "tensor_tensor", ct1, cr, bcP(COI, di), ALU.mult, reads=rd, writes=[b_ct1])
                op("dve", "tensor_tensor", cpi, ci, bcP(COR, di), ALU.mult, reads=rd, writes=[b_cpi])
                op("dve", "tensor_tensor", cpi, cpi, ct1, ALU.add, reads=[b_cpi, b_ct1], writes=[b_cpi])
                for pw in range(5):
                    if pw == 0:
                        src_r, b_sr = cpr, b_cpr
                        op("dve", "tensor_scalar", cai, cpi, -1.0, None, ALU.mult, reads=[b_cpi], writes=[b_cai])
                    else:
                        rdp = [b_cpr, b_cpi, b_PT[AW_r[pw]], b_PT[AW_i[pw]], b_PT[NAW_i[pw]]]
                        op("dve", "tensor_tensor", ct0, cpi, bcP(AW_i[pw], di), ALU.mult, reads=rdp, writes=[b_ct0])
                        op("dve", "tensor_tensor", car, cpr, bcP(AW_r[pw], di), ALU.mult, reads=rdp, writes=[b_car])
                        op("dve", "tensor_tensor", car, car, ct0, ALU.subtract, reads=[b_car, b_ct0], writes=[b_car])
                        op("dve", "tensor_tensor", ct1, cpi, bcP(AW_r[pw], di), ALU.mult, reads=rdp, writes=[b_ct1])
                        op("dve", "tensor_tensor", cai, cpr, bcP(NAW_i[pw], di), ALU.mult, reads=rdp, writes=[b_cai])
                        op("dve", "tensor_tensor", cai, cai, ct1, ALU.subtract, reads=[b_cai, b_ct1], writes=[b_cai])
                        src_r, b_sr = car, b_car
                    for var, (src_, bsrc_) in enumerate(((src_r, b_sr), (cai, b_cai))):
                        base = ca32(di * 16, pw, var)
                        for gi in range(2):
                            ps_ = slice(gi * 64, (gi + 1) * 64)
                            op("act" if gi == 0 else "dve", "copy" if gi == 0 else "tensor_copy",
                               CA32[ps_, base:base + 151:10, gi * 16:gi * 16 + 16], src_[ps_, :, :],
                               reads=[bsrc_], writes=[b_CA32])
            CA32v = CA32.rearrange("p (j w v) c -> p j w v c", j=32, w=5)
            CAcv = CAc.rearrange("p (j w v) c -> p j w v c", j=32, w=4)
            for j in range(32):
                op("act", "copy", CAcv[:, j, :, :, :], CA32v[:, j, 1:5, :, :], reads=[b_CA32], writes=[b_CAc])
            dg = ar.alloc([48, 128], F32); b_dg = Buf("dg")
            def dgi(P4, pw, v):
                return (P4 * 4 + pw) * 3 + v
            b_sb = [Buf("sbank%d" % i) for i in range(8)]
            wtmp = ar.alloc([128], F32); b_wtmp = Buf("wtmp")
            sbi = 0
            for q in range(4):
                for di in range(2):
                    for P4 in range(4):
                        j = di * 16 + q * 4 + P4
                        for pw in range(4):
                            if pw == 0:
                                op("dve", "tensor_copy", dg[:, dgi(P4, 0, 0), :], idf, reads=[b_idf], writes=[b_dg])
                                continue
                            for v, tl in enumerate((AW_r[pw], AW_i[pw], NAW_i[pw])):
                                if v == 1:
                                    op("act", "activation", dg[:, dgi(P4, pw, v), :], idf, AF.Copy, scale=PT[:, tl, j:j + 1],
                                       reads=[b_idf, b_PT[tl]], writes=[b_dg])
                                else:
                                    op("dve", "tensor_scalar", dg[:, dgi(P4, pw, v), :], idf, PT[:, tl, j:j + 1], None, ALU.mult,
                                       reads=[b_idf, b_PT[tl]], writes=[b_dg])
                    for part in range(2):
                        for tap in range(4):
                            pw = (3 - tap) if di == 0 else tap
                            bk = banks[sbi % 8]; bbk = b_sb[sbi % 8]; sbi += 1
                            for P4 in range(4):
                                P = q * 4 + P4
                                o_ = bk[:, P4 * 128:(P4 + 1) * 128]
                                Bre, Bim = BnP[:, combo(P, di, 0), :], BnP[:, combo(P, di, 1), :]
                                rdm = [b_BnP, b_dg]
                                if pw == 0:
                                    op("pe", "matmul", o_, Bre if part == 0 else Bim, dg[:, dgi(P4, 0, 0), :], start=True, stop=True,
                                       skip_group_check=True, reads=rdm, writes=[bbk])
                                elif part == 0:
                                    op("pe", "matmul", o_, Bre, dg[:, dgi(P4, pw, 0), :], start=True, stop=False,
                                       skip_group_check=True, reads=rdm, writes=[bbk])
                                    op("pe", "matmul", o_, Bim, dg[:, dgi(P4, pw, 2), :], start=False, stop=True,
                                       skip_group_check=True, reads=rdm, writes=[bbk])
                                else:
                                    op("pe", "matmul", o_, Bre, dg[:, dgi(P4, pw, 1), :], start=True, stop=False,
                                       skip_group_check=True, reads=rdm, writes=[bbk])
                                    op("pe", "matmul", o_, Bim, dg[:, dgi(P4, pw, 0), :], start=False, stop=True,
                                       skip_group_check=True, reads=rdm, writes=[bbk])
                            op("dve", "tensor_reduce", wtmp, bk.rearrange("p (b s) -> p s b", b=4), AX.X, ALU.add,
                               reads=[bbk], writes=[b_wtmp])
                            op("act", "copy", W1c[:, w1i(q, di, part, tap), :], wtmp, reads=[b_wtmp], writes=[b_W1c])
            dsk = ar.alloc([4, 128], F32); b_dsk = Buf("dsk")
            for q in range(4):
                op("dve", "tensor_scalar", dsk[:, q, :], idf, dskipT[:, q:q + 1], None, ALU.mult,
                   reads=[b_idf, b_smallT], writes=[b_dsk])
            for q in range(4):
                for di in range(2):
                    bk = banks[sbi % 8]; bbk = b_sb[sbi % 8]; sbi += 1
                    first = True
                    for d in range(4):
                        if di == 0 and d == 0:
                            op("pe", "matmul", bk[:, 0:128], idf, dsk[:, q, :], start=first, stop=False,
                               skip_group_check=True, reads=[b_idf, b_dsk], writes=[bbk])
                            first = False
                        for P4 in range(4):
                            P = q * 4 + P4
                            j = di * 16 + P
                            o_ = bk[:, d * 128 + 32 * P4:d * 128 + 32 * P4 + 32]
                            for part in range(2):
                                op("pe", "matmul", o_, BnP[:, combo(P, di, part), :], CA32[:, ca32(j, d, part), :],
                                   start=first, stop=False, skip_group_check=True, reads=[b_BnP, b_CA32], writes=[bbk])
                                first = False
                    op("act", "copy", Kd[:, kdi(q, di, 0):kdi(q, di, 0) + 4, :], bk.rearrange("p (d c) -> p d c", d=4),
                       reads=[bbk], writes=[b_Kd])

            if bstop == 0:
                op("dve", "tensor_copy", ctmp[:, 0, :], Kd[:, 5, 0:16], reads=[b_Kd, b_W1c, b_CAc] + b_PT, writes=[b_ctmp])
                op("pool", "dma_start", out=x2_dst[0:128, 0:128], in_=ctmp, reads=[b_ctmp], dma=True, final=True)
                return
            for (s0, L) in seqs:
                S.barrier()
                ar.off = mark_b1
                nch = L // 512
                uT = ar.alloc([4, L], BF16); b_uT = Buf("uT")
                yacc = ar.alloc([4, L], F32)
                b_yacc = [[Buf("yacc%d_%d" % (q, c)) for c in range(nch)] for q in range(4)]
                for q in range(4):
                    op("sp", "dma_start", out=uT[:, q, :], in_=uT_src[q * 128:(q + 1) * 128, s0:s0 + L],
                       writes=[b_uT], dma=True)
                mark_loop = ar.off
                cosT = [ar.alloc([512], F32) for _ in range(4)]
                sinT = [ar.alloc([512], F32) for _ in range(4)]
                b_tab = [Buf("tab%d" % i) for i in range(4)]
                tmpT = ar.alloc([256], F32); b_tmpT = Buf("tmpT")
                bus = [[ar.alloc([512], F32) for _ in range(2)] for _ in range(2)]
                b_bus = [[Buf("bus%d_%d" % (i, k)) for k in range(2)] for i in range(2)]
                T_ = [ar.alloc([512], F32) for _ in range(4)]; bT_ = [Buf("T%d" % k) for k in range(4)]
                bre = ar.alloc([512], F32); bim = ar.alloc([512], F32); b_bre = Buf("bre"); b_bim = Buf("bim")
                wre = ar.alloc([512], F32); wim = ar.alloc([512], F32); b_wre = Buf("wre"); b_wim = Buf("wim")
                XS = [[ar.alloc([514], BF16) for _ in range(4)] for _ in range(2)]
                b_XS = [[Buf("XS%d_%d" % (i, k)) for k in range(4)] for i in range(2)]
                cX = ar.alloc([4, 4], BF16); b_cX = [Buf("cX%d" % i) for i in range(4)]
                init = ar.alloc([4, 4], F32); b_init = [Buf("init%d" % i) for i in range(4)]
                wl = ar.alloc([4, 2], F32); b_wl = [Buf("wl%d" % i) for i in range(4)]
                b_y = [Buf("yb%d" % i) for i in range(4)]
                b_bx = [Buf("bx%d" % i) for i in range(4)]
                nchb = L // 2048
                it = 0
                for di in range(2):
                    for q in range(4):
                        for P4 in range(4):
                            j = di * 16 + q * 4 + P4
                            cT, sT, bt = cosT[P4], sinT[P4], b_tab[P4]
                            op("dve", "memset", cT[:, 0:1], 1.0, writes=[bt])
                            op("dve", "memset", sT[:, 0:1], 0.0, writes=[bt])
                            for k in range(9):
                                n = 1 << k
                                c_, s_ = PT[:, MC[k + 2], j:j + 1], PT[:, MS[k + 2], j:j + 1]
                                rd = [bt, b_PT[MC[k + 2]], b_PT[MS[k + 2]]]
                                op("dve", "tensor_scalar", tmpT[:, 0:n], sT[:, 0:n], s_, None, ALU.mult,
                                   reads=rd, writes=[b_tmpT])
                                op("dve", "scalar_tensor_tensor", cT[:, n:2 * n], cT[:, 0:n], c_, tmpT[:, 0:n],
                                   ALU.mult, ALU.subtract, reads=rd + [b_tmpT], writes=[bt])
                                op("dve", "tensor_scalar", tmpT[:, 0:n], cT[:, 0:n], s_, None, ALU.mult,
                                   reads=rd, writes=[b_tmpT])
                                op("dve", "scalar_tensor_tensor", sT[:, n:2 * n], sT[:, 0:n], c_, tmpT[:, 0:n],
                                   ALU.mult, ALU.add, reads=rd + [b_tmpT], writes=[bt])
                        chunks = list(range(nchb)) if di == 0 else list(range(nchb - 1, -1, -1))
                        items = [(ci_, ch, P4) for ci_, ch in enumerate(chunks) for P4 in range(4)]

                        def R_(ap, di=di):
                            return ap if di == 0 else ap[:, ::-1]

                        def emit_BX(n, di=di, q=q):
                            ci_, ch, P4 = items[n]
                            c0 = ch * 2048
                            z_ = n % 2
                            rows = slice(32 * P4, 32 * P4 + 32)
                            for part in range(2):
                                pb, bpb = banks[4 + z_ * 2 + part], b_bx[z_ * 2 + part]
                                for tap in range(4):
                                    op("pe", "matmul", pb, W1c[rows, w1i(q, di, part, tap), :],
                                       uT[rows, q, c0 + tap:c0 + 2048:4], start=(tap == 0), stop=(tap == 3),
                                       tile_position=(32 * P4, 0), reads=[b_W1c, b_uT], writes=[bpb])
                                op("act", "copy", bus[z_][part], R_(pb), reads=[bpb], writes=[b_bus[z_][part]])

                        def emit_FIR(ch, di=di, q=q):
                            c0 = ch * 2048
                            for b_ in range(4):
                                first = True
                                for tau in range(4):
                                    taps = range(0, tau + 1) if di == 0 else range(tau, 4)
                                    for tp in taps:
                                        t0_ = c0 + b_ * 512 + tp
                                        op("pe", "matmul", banks[b_][:, tau::4], Kd[:, kdi(q, di, abs(tau - tp)), :],
                                           uT[:, q, t0_:c0 + (b_ + 1) * 512:4], start=first, stop=False,
                                           skip_group_check=True, reads=[b_Kd, b_uT], writes=[b_y[b_]])
                                        first = False

                        def emit_DVE(n, di=di, q=q):
                            ci_, ch, P4 = items[n]
                            j = di * 16 + q * 4 + P4
                            cT, sT, bt = cosT[P4], sinT[P4], b_tab[P4]
                            rr = PT[:, R4, j:j + 1]
                            c512, s512 = PT[:, MC[11], j:j + 1], PT[:, MS[11], j:j + 1]
                            z_ = n % 2
                            ur, ui = bus[z_][0], bus[z_][1]
                            bur, bui = b_bus[z_][0], b_bus[z_][1]
                            op("dve", "tensor_tensor", T_[0], ur, cT, ALU.mult, reads=[bur, bt], writes=[bT_[0]])
                            op("dve", "tensor_tensor", T_[1], ui, sT, ALU.mult, reads=[bui, bt], writes=[bT_[1]])
                            op("dve", "tensor_tensor", T_[2], ui, cT, ALU.mult, reads=[bui, bt], writes=[bT_[2]])
                            op("dve", "tensor_tensor", T_[3], ur, sT, ALU.mult, reads=[bur, bt], writes=[bT_[3]])
                            op("dve", "tensor_tensor", bre, T_[0], T_[1], ALU.add, reads=[bT_[0], bT_[1]], writes=[b_bre])
                            op("dve", "tensor_tensor", bim, T_[2], T_[3], ALU.subtract, reads=[bT_[2], bT_[3]], writes=[b_bim])
                            X_, bX_ = XS[z_], b_XS[z_]
                            col = 0 if di == 0 else 513
                            if ci_ == 0:
                                i_re, i_im = 0.0, 0.0
                                ird = []
                                for k in range(4):
                                    op("pool", "memset", X_[k][:, col:col + 1], 0.0, writes=[bX_[k]])
                            else:
                                iv = init[:, P4, :]
                                wlr, wli = wl[:, P4, 0:1], wl[:, P4, 1:2]
                                op("dve", "tensor_scalar", iv[:, 2:3], wli, s512, None, ALU.mult,
                                   reads=[b_wl[P4], b_PT[MS[11]]], writes=[b_init[P4]])
                                op("dve", "scalar_tensor_tensor", iv[:, 0:1], wlr, c512, iv[:, 2:3],
                                   ALU.mult, ALU.subtract, reads=[b_wl[P4], b_PT[MC[11]], b_init[P4]], writes=[b_init[P4]])
                                op("dve", "tensor_scalar", iv[:, 3:4], wlr, s512, None, ALU.mult,
                                   reads=[b_wl[P4], b_PT[MS[11]]], writes=[b_init[P4]])
                                op("dve", "scalar_tensor_tensor", iv[:, 1:2], wli, c512, iv[:, 3:4],
                                   ALU.mult, ALU.add, reads=[b_wl[P4], b_PT[MC[11]], b_init[P4]], writes=[b_init[P4]])
                                i_re, i_im = iv[:, 0:1], iv[:, 1:2]
                                ird = [b_init[P4]]
                                for k in range(4):
                                    op("pool", "tensor_copy", X_[k][:, col:col + 1], cX[:, P4, k:k + 1],
                                       reads=[b_cX[P4]], writes=[bX_[k]])
                            rbc = rr.to_broadcast([128, 512])
                            op("dve", "tensor_tensor_scan", wre, rbc, bre, i_re, ALU.mult, ALU.add,
                               reads=[b_bre, b_PT[R4]] + ird, writes=[b_wre])
                            op("dve", "tensor_tensor_scan", wim, rbc, bim, i_im, ALU.mult, ALU.add,
                               reads=[b_bim, b_PT[R4]] + ird, writes=[b_wim])
                            if ci_ < nchb - 1:
                                op("act", "copy", wl[:, P4, 0:1], wre[:, 511:512], reads=[b_wre], writes=[b_wl[P4]])
                                op("act", "copy", wl[:, P4, 1:2], wim[:, 511:512], reads=[b_wim], writes=[b_wl[P4]])
                            Xw = [R_(X_[k][:, 1:513]) for k in range(4)]
                            op("dve", "tensor_tensor", Xw[0], wre, cT, ALU.mult, reads=[b_wre, bt], writes=[bX_[0]])
                            op("dve", "scalar_tensor_tensor", Xw[1], wim, -1.0, sT, ALU.mult, ALU.mult,
                               reads=[b_wim, bt], writes=[bX_[1]])
                            op("dve", "tensor_tensor", Xw[2], wim, cT, ALU.mult, reads=[b_wim, bt], writes=[bX_[2]])
                            op("dve", "tensor_tensor", Xw[3], wre, sT, ALU.mult, reads=[b_wre, bt], writes=[bX_[3]])
                            if ci_ < nchb - 1:
                                colc = 512 if di == 0 else 1
                                for k in range(4):
                                    op("pool", "tensor_copy", cX[:, P4, k:k + 1], X_[k][:, colc:colc + 1],
                                       reads=[bX_[k]], writes=[b_cX[P4]])

                        def emit_OUT(n, di=di, q=q):
                            ci_, ch, P4 = items[n]
                            j = di * 16 + q * 4 + P4
                            z_ = n % 2
                            X_, bX_ = XS[z_], b_XS[z_]
                            rows = slice(32 * P4, 32 * P4 + 32)
                            sh = 0 if di == 0 else 2
                            for b_ in range(4):
                                for tau in range(4):
                                    pw = tau + 1 if di == 0 else 4 - tau
                                    for k in range(4):
                                        last = (P4 == 3 and tau == 3 and k == 3)
                                        op("pe", "matmul", banks[b_][rows, tau::4], CAc[:, cab(j, pw, 0 if k < 2 else 1), :],
                                           X_[k][:, b_ * 128 + sh:b_ * 128 + sh + 128], start=False, stop=last,
                                           skip_group_check=True, tile_position=(0, 32 * P4),
                                           reads=[b_CAc, bX_[k]], writes=[b_y[b_]])

                        def emit_yevac(ch, di=di, q=q):
                            c0 = ch * 2048
                            for b_ in range(4):
                                ya = yacc[:, q, c0 + b_ * 512:c0 + (b_ + 1) * 512]
                                byq = b_yacc[q][(c0 + b_ * 512) // 512]
                                if di == 0:
                                    op("act", "copy", ya, banks[b_], reads=[b_y[b_]], writes=[byq])
                                else:
                                    op("dve", "tensor_tensor", ya, ya, banks[b_], ALU.add, reads=[b_y[b_], byq], writes=[byq])

                        emit_BX(0)
                        for n, (ci_, ch, P4) in enumerate(items):
                            if n + 1 < len(items):
                                emit_BX(n + 1)
                            if P4 == 0:
                                emit_FIR(ch)
                            emit_DVE(n)
                            emit_OUT(n)
                            if P4 == 3:
                                emit_yevac(ch)
                S.barrier()
                ar.off = mark_loop
                Y2 = ar.alloc([4, 512], F32); b_Y2 = [Buf("Y2_%d" % q) for q in range(4)]
                YG = ar.alloc([4, 512], F32); b_YG = [Buf("YG%d" % q) for q in range(4)]
                YGB = ar.alloc([4, 512], BF16); b_YGB = [Buf("YGB%d" % q) for q in range(4)]
                SQ = ar.alloc([4, 512], F32); b_SQ = [Buf("SQ%d" % q) for q in range(4)]
                GT = ar.alloc([4, 512], F32); b_GT = [Buf("GT%d" % q) for q in range(4)]
                RS = ar.alloc([512], F32); b_RS = Buf("RS")
                YS = ar.alloc([4, 512], BF16); b_YS = Buf("YS")
                b_gl = [Buf("glps%d" % i) for i in range(3)]; b_ms = Buf("msps")
                glb = (4, 5, 6)
                gi_ = 0
                for ch in range(nch):
                    t0 = ch * 512
                    for q in range(4):
                        Yq = yacc[:, q, t0:t0 + 512]
                        byq = b_yacc[q][ch]
                        op("act", "activation", Y2[:, q, :], Yq, AF.Square, reads=[byq], writes=[b_Y2[q]])
                        op("dve", "tensor_scalar", Y2[:, q, :], Y2[:, q, :], 0.044715, 1.0, ALU.mult, ALU.add,
                           reads=[b_Y2[q]], writes=[b_Y2[q]])
                        op("dve", "tensor_tensor", Y2[:, q, :], Y2[:, q, :], Yq, ALU.mult, reads=[b_Y2[q], byq], writes=[b_Y2[q]])
                        op("act", "activation", Y2[:, q, :], Y2[:, q, :], AF.Sigmoid, scale=2.0 * math.sqrt(2.0 / math.pi),
                           reads=[b_Y2[q]], writes=[b_Y2[q]])
                        op("dve", "tensor_tensor", YG[:, q, :], Yq, Y2[:, q, :], ALU.mult, reads=[byq, b_Y2[q]], writes=[b_YG[q]])
                        op("act", "copy", YGB[:, q, :], YG[:, q, :], reads=[b_YG[q]], writes=[b_YGB[q]])
                    for qo in range(4):
                        gl = banks[glb[gi_ % 3]]
                        bgl = b_gl[gi_ % 3]
                        gi_ += 1
                        for qi in range(4):
                            op("pe", "matmul", gl, wglu[:, qi, qo * 128:(qo + 1) * 128], YGB[:, qi, :],
                               start=(qi == 0), stop=(qi == 3), reads=[b_wglu, b_YGB[qi]], writes=[bgl])
                        op("act", "activation", GT[:, qo, :], gl, AF.Sigmoid, bias=bgluT[:, qo:qo + 1],
                           reads=[bgl, b_smallT], writes=[b_GT[qo]])
                        op("dve", "tensor_tensor", YG[:, qo, :], YG[:, qo, :], GT[:, qo, :], ALU.mult,
                           reads=[b_GT[qo], b_YG[qo]], writes=[b_YG[qo]])
                        op("act", "activation", SQ[:, qo, :], YG[:, qo, :], AF.Square, reads=[b_YG[qo]], writes=[b_SQ[qo]])
                    ms = banks[7]
                    for qo in range(4):
                        op("pe", "matmul", ms, ones, SQ[:, qo, :], start=(qo == 0), stop=(qo == 3),
                           reads=[b_ones, b_SQ[qo]], writes=[b_ms])
                    op("dve", "tensor_scalar", RS, ms, 1.0 / 512, EPS, ALU.mult, ALU.add, reads=[b_ms], writes=[b_RS])
                    op("act", "sqrt", RS, RS, reads=[b_RS], writes=[b_RS])
                    op("dve", "reciprocal", RS, RS, reads=[b_RS], writes=[b_RS])
                    for qo in range(4):
                        op("dve", "scalar_tensor_tensor", YS[:, qo, :], YG[:, qo, :], gssmT[:, qo:qo + 1], RS,
                           ALU.mult, ALU.mult, reads=[b_YG[qo], b_smallT, b_RS], writes=[b_YS])
                    op("pool", "dma_start", out=ys_d[:, s0 + t0:s0 + t0 + 512].rearrange("(c p) t -> p c t", p=128),
                       in_=YS, reads=[b_YS], writes=[b_ysd], dma=True)
            ar.off = mark_params
            S.barrier()
            if bstop == 1:
                op("dve", "memset", ones, 1.0, writes=[b_ones])
                op("pool", "dma_start", out=x2_dst[0:128, 0:128], in_=ones, reads=[b_ones, b_ysd], dma=True, final=True)
                return

            wuq = ar.alloc([3, 768], BF16); b_wuq = Buf("wuq")
            wukv = ar.alloc([2, 1024], BF16); b_wukv = Buf("wukv")
            wout = ar.alloc([8, 1024], BF16); b_wout = Buf("wout")
            for k in range(3):
                op("pool", "dma_start", out=wuq[:, k, :], in_=M["w_uq"][k * 128:(k + 1) * 128, :],
                   reads=[b_ysd], writes=[b_wuq], dma=True)
            for k in range(2):
                op("pool", "dma_start", out=wukv[:, k, :], in_=M["w_ukv"][k * 128:(k + 1) * 128, :],
                   reads=[b_ysd], writes=[b_wukv], dma=True)
            for k in range(8):
                op("pool", "dma_start", out=wout[:, k, :], in_=M["w_out"][k * 128:(k + 1) * 128, :],
                   reads=[b_ysd], writes=[b_wout], dma=True)
            gq = ar.alloc([384], F32); gkv = ar.alloc([256], F32); gatt = ar.alloc([512], F32); gmp = ar.alloc([D], F32)
            b_gb = Buf("gbc")
            for t_ap, nm in ((gq, "g_q"), (gkv, "g_kv"), (gatt, "g_att_out"), (gmp, "g_mix_post")):
                op("pool", "dma_start", out=t_ap, in_=M[nm].partition_broadcast(128), reads=[b_ysd], writes=[b_gb], dma=True)
            NT = max(L for _, L in seqs) // 128
            rc = ar.alloc([NT, 16], F32); rs = ar.alloc([NT, 16], F32)
            rcq = ar.alloc([NT, 16], F32); rsq = ar.alloc([NT, 16], F32)
            b_rope = Buf("rope")
            mark_b2 = ar.off
            ri = ar.alloc([NT, 16], I32); rf = ar.alloc([NT, 16], F32); rg = ar.alloc([NT, 16], F32)
            rk = ar.alloc([NT, 16], F32)
            rki = ar.alloc([NT, 16], I32)
            b_r = Buf("ropetmp")
            op("pool", "iota", ri, [[0, NT], [1, 16]], base=0, channel_multiplier=0, reads=[b_ysd], writes=[b_r])
            op("dve", "tensor_copy", rf, ri, reads=[b_r], writes=[b_r])
            op("act", "activation", rf, rf, AF.Exp, scale=-math.log(10000.0) / 16.0, reads=[b_r], writes=[b_r])
            op("pool", "iota", ri, [[128, NT], [0, 16]], base=0, channel_multiplier=1, reads=[b_r], writes=[b_r])
            op("dve", "tensor_copy", rg, ri, reads=[b_r], writes=[b_r])
            op("dve", "tensor_tensor", rg, rg, rf, ALU.mult, reads=[b_r], writes=[b_r])
            for (dst, shift) in ((rs, 0.0), (rc, math.pi / 2)):
                op("dve", "tensor_scalar", rf, rg, shift, None, ALU.add, reads=[b_r], writes=[b_r])
                op("dve", "tensor_scalar", rk, rf, 1.0 / TWO_PI, None, ALU.mult, reads=[b_r], writes=[b_r])
                op("dve", "tensor_copy", rki, rk, reads=[b_r], writes=[b_r])
                op("dve", "tensor_copy", rk, rki, reads=[b_r], writes=[b_r])
                op("dve", "scalar_tensor_tensor", rf, rk, -TWO_PI, rf, ALU.mult, ALU.add, reads=[b_r], writes=[b_r])
                op("dve", "tensor_scalar", rk, rf, math.pi, -TWO_PI, ALU.is_gt, ALU.mult, reads=[b_r], writes=[b_r])
                op("dve", "tensor_tensor", rf, rf, rk, ALU.add, reads=[b_r], writes=[b_r])
                op("dve", "tensor_scalar", rk, rf, -math.pi, TWO_PI, ALU.is_lt, ALU.mult, reads=[b_r], writes=[b_r])
                op("dve", "tensor_tensor", rf, rf, rk, ALU.add, reads=[b_r], writes=[b_r])
                op("act", "activation", dst, rf, AF.Sin, reads=[b_r], writes=[b_rope])
            op("dve", "tensor_scalar", rcq, rc, QSCALE, None, ALU.mult, reads=[b_rope], writes=[b_rope])
            op("dve", "tensor_scalar", rsq, rs, QSCALE, None, ALU.mult, reads=[b_rope], writes=[b_rope])

            def bc8(ap16):
                return bass.AP(ap16.tensor, ap16.offset, [list(ap16.ap[0]), [0, 8], [1, 16]])

            for (s0, L) in seqs:
                S.barrier()
                ar.off = mark_b2
                nkt = L // 128
                KT = ar.alloc([8, L], BF16); b_KT = [Buf("KT%d" % i) for i in range(nkt)]
                VA = ar.alloc([nkt, 8, 65], BF16); b_VA = [Buf("VA%d" % i) for i in range(nkt)]
                zq_ = [ar.alloc([672], F32) for _ in range(2)]; b_zq_ = [Buf("zq0"), Buf("zq1")]
                st2 = ar.alloc([8], F32); b_st = [Buf("st%d" % i) for i in range(8)]
                junk = ar.alloc([D], BF16)
                b_junk = Buf("junkB")
                kvn_ = [ar.alloc([384], BF16) for _ in range(2)]; b_kvn_ = [Buf("kvn0"), Buf("kvn1")]
                kvnT_ = [ar.alloc([3, 128], BF16) for _ in range(2)]; b_kvnT_ = [Buf("kvnT0"), Buf("kvnT1")]
                Kf_ = [ar.alloc([8, 96], BF16) for _ in range(2)]; b_Kf_ = [Buf("Kf0"), Buf("Kf1")]
                rt_ = [[ar.alloc([8, 16], F32) for _ in range(2)] for _ in range(2)]; b_rt_ = [Buf("rt0"), Buf("rt1")]
                QT = ar.alloc([8, 512], BF16); b_QT = Buf("QT")
                PTb = [ar.alloc([512], BF16) for _ in range(3)]; b_PTb = [Buf("PTb0"), Buf("PTb1"), Buf("PTb2")]
                att = ar.alloc([4, 512], F32); b_att = [Buf("att%d" % i) for i in range(4)]
                rec = ar.alloc([4], F32); b_rec = Buf("rec")
                yat = ar.alloc([512], BF16); b_yat = Buf("yat")
                ymT = ar.alloc([8, 128], BF16); b_ymT = Buf("ymT")
                x1t = ar.alloc([D], F32); b_x1t = Buf("x1t")
                qs_ = [ar.alloc([768], F32) for _ in range(2)]; b_qs_ = [Buf("qs0"), Buf("qs1")]
                tt2 = [ar.alloc([512], F32) for _ in range(2)]; b_tt2 = [Buf("tt2_0"), Buf("tt2_1")]
                bb = [Buf("b2bank%d" % i) for i in range(8)]
                b_S = [bb[3], bb[4], bb[7]]
                b_O = [bb[5], bb[6]]
                tpb_ = [banks[0].bitcast(BF16), banks[0].bitcast(BF16)]
                b_tpb_ = [bb[0], bb[0]]
                mmb_ = [(1, 2), (5, 6)]

                class _Par:
                    v = 0
                par = _Par()

                def rstd_col(col, n_feat):
                    c = st2[:, col:col + 1]
                    op("dve", "tensor_scalar", c, c, 1.0 / n_feat, EPS, ALU.mult, ALU.add, reads=[b_st[col]], writes=[b_st[col]])
                    op("act", "sqrt", c, c, reads=[b_st[col]], writes=[b_st[col]])
                    op("dve", "reciprocal", c, c, reads=[b_st[col]], writes=[b_st[col]])

                def norm_T(src_ap, n_feat, g_ap, col):
                    p_ = par.v
                    col = col + 5 * p_
                    b_zq, kvn, b_kvn, kvnT, b_kvnT, tpb, b_tpb = b_zq_[p_], kvn_[p_], b_kvn_[p_], kvnT_[p_], b_kvnT_[p_], tpb_[p_], b_tpb_[p_]
                    nk = n_feat // 128
                    op("act", "activation", junk[:, 0:n_feat], src_ap, AF.Square, accum_out=st2[:, col:col + 1],
                       reads=[b_zq], writes=[b_junk, b_st[col]])
                    rstd_col(col, n_feat)
                    op("dve", "scalar_tensor_tensor", kvn[:, 0:n_feat], src_ap, st2[:, col:col + 1], g_ap,
                       ALU.mult, ALU.mult, reads=[b_zq, b_st[col], b_gb], writes=[b_kvn])
                    for k in range(nk):
                        op("pe", "transpose", tpb[:, k * 128:(k + 1) * 128], kvn[:, k * 128:(k + 1) * 128], idb,
                           reads=[b_kvn, b_idb], writes=[b_tpb])
                    op("dve", "tensor_copy", kvnT[:, 0:nk, :], tpb[:, 0:nk * 128].rearrange("p (a b) -> p a b", a=nk),
                       reads=[b_tpb], writes=[b_kvnT])

                for kt in range(nkt):
                    r0 = s0 + kt * 128
                    par.v = kt % 2
                    p_ = par.v
                    zq, b_zq, kvnT, b_kvnT, Kf, b_Kf, rt, b_rt = zq_[p_], b_zq_[p_], kvnT_[p_], b_kvnT_[p_], Kf_[p_], b_Kf_[p_], rt_[p_], b_rt_[p_]
                    tpb, b_tpb = tpb_[p_], b_tpb_[p_]
                    mbk = mmb_[p_]
                    b_mm = [bb[mbk[0]], bb[mbk[1]]]
                    op("sp", "dma_start", out=zq, in_=zq_src[r0:r0 + 128, :], writes=[b_zq], dma=True)
                    norm_T(zq[:, 384:640], 256, gkv, 0)
                    for n in range(2):
                        for kc in range(2):
                            op("pe", "matmul", banks[mbk[n]], kvnT[:, kc, :], wukv[:, kc, n * 512:(n + 1) * 512],
                               start=(kc == 0), stop=(kc == 1), reads=[b_kvnT, b_wukv], writes=[b_mm[n]])
                    x1_, x2_ = zq[:, 640:656], zq[:, 656:672]
                    cs_, sn_ = rc[:, kt, :], rs[:, kt, :]
                    r1, r2 = rt[0][:, 0, :], rt[1][:, 0, :]
                    op("dve", "tensor_tensor", r1, x1_, cs_, ALU.mult, reads=[b_zq, b_rope], writes=[b_rt])
                    op("dve", "tensor_tensor", r2, x2_, sn_, ALU.mult, reads=[b_zq, b_rope], writes=[b_rt])
                    op("dve", "tensor_tensor", Kf[:, 0, 64:80], r1, r2, ALU.subtract, reads=[b_rt], writes=[b_Kf])
                    op("dve", "tensor_tensor", r1, x1_, sn_, ALU.mult, reads=[b_zq, b_rope, b_Kf], writes=[b_rt])
                    op("dve", "tensor_tensor", r2, x2_, cs_, ALU.mult, reads=[b_zq, b_rope], writes=[b_rt])
                    op("dve", "tensor_tensor", Kf[:, 0, 80:96], r1, r2, ALU.add, reads=[b_rt], writes=[b_Kf])
                    for h in range(1, 8):
                        op("pool", "tensor_copy", Kf[:, h, 64:96], Kf[:, 0, 64:96], reads=[b_Kf], writes=[b_Kf])
                    for n in range(2):
                        bv = banks[mbk[n]].rearrange("p (h d) -> p h d", h=4)
                        op("act", "copy", Kf[:, 4 * n:4 * n + 4, 0:64], bv[:, :, 0:64], reads=[b_mm[n]], writes=[b_Kf])
                        op("dve", "tensor_copy", VA[:, kt, 4 * n:4 * n + 4, 0:64], bv[:, :, 64:128],
                           reads=[b_mm[n]], writes=[b_VA[kt]])
                    op("pool", "memset", VA[:, kt, :, 64:65], 1.0, writes=[b_VA[kt]])
                    for h in range(8):
                        op("pe", "transpose", tpb[0:96, h * 128:(h + 1) * 128], Kf[:, h, :], idb,
                           reads=[b_Kf, b_idb], writes=[b_tpb])
                    op("act", "copy", KT[0:96, :, kt * 128:(kt + 1) * 128],
                       tpb[0:96, :].rearrange("p (h t) -> p h t", h=8), reads=[b_tpb], writes=[b_KT[kt]])

                if bstop == 2:
                    op("dve", "memset", ones, 1.0, reads=b_KT[0:nkt] + b_VA[0:nkt], writes=[b_ones])
                    op("pool", "dma_start", out=x2_dst[0:128, 0:128], in_=ones, reads=[b_ones], dma=True, final=True)
                    return
                for qb in range(L // 512):
                    for sub in range(4):
                        tix = qb * 4 + sub
                        r0 = s0 + tix * 128
                        par.v = tix % 2
                        p_ = par.v
                        zq, b_zq, kvnT, b_kvnT, Qf, b_Qf, rt, b_rt = zq_[p_], b_zq_[p_], kvnT_[p_], b_kvnT_[p_], Kf_[p_], b_Kf_[p_], rt_[p_], b_rt_[p_]
                        tpb, b_tpb = tpb_[p_], b_tpb_[p_]
                        qs, b_qs = qs_[p_], b_qs_[p_]
                        mbk = mmb_[p_]
                        b_mm = [bb[mbk[0]], bb[mbk[1]]]
                        mm2 = pall[:, mbk[0] * 512:mbk[0] * 512 + 1024]
                        op("sp", "dma_start", out=zq, in_=zq_src[r0:r0 + 128, :], writes=[b_zq], dma=True)
                        norm_T(zq[:, 0:384], 384, gq, 1)
                        for (n0, nn) in ((0, 512), (512, 256)):
                            n = n0 // 512
                            for kc in range(3):
                                op("pe", "matmul", mm2[:, n0:n0 + nn], kvnT[:, kc, :], wuq[:, kc, n0:n0 + nn],
                                   start=(kc == 0), stop=(kc == 2), reads=[b_kvnT, b_wuq], writes=[b_mm[n]])
                        op("dve", "tensor_copy", qs[:, 0:512], banks[mbk[0]], reads=[b_mm[0]], writes=[b_qs])
                        op("dve", "tensor_copy", qs[:, 512:768], banks[mbk[1]][:, 0:256], reads=[b_mm[1]], writes=[b_qs])
                        qv = qs.rearrange("p (h d) -> p h d", h=8)
                        rdq = [b_qs]
                        op("dve", "tensor_scalar", Qf[:, :, 0:64], qv[:, :, 0:64], QSCALE, None, ALU.mult,
                           reads=rdq, writes=[b_Qf])
                        cq, sq_ = bc8(rcq[:, tix, :]), bc8(rsq[:, tix, :])
                        op("dve", "tensor_tensor", rt[0], qv[:, :, 64:80], cq, ALU.mult, reads=rdq + [b_rope], writes=[b_rt])
                        op("dve", "tensor_tensor", rt[1], qv[:, :, 80:96], sq_, ALU.mult, reads=rdq + [b_rope], writes=[b_rt])
                        op("dve", "tensor_tensor", Qf[:, :, 64:80], rt[0], rt[1], ALU.subtract, reads=[b_rt], writes=[b_Qf])
                        op("dve", "tensor_tensor", rt[0], qv[:, :, 64:80], sq_, ALU.mult, reads=rdq + [b_rope, b_Qf], writes=[b_rt])
                        op("dve", "tensor_tensor", rt[1], qv[:, :, 80:96], cq, ALU.mult, reads=rdq + [b_rope], writes=[b_rt])
                        op("dve", "tensor_tensor", Qf[:, :, 80:96], rt[0], rt[1], ALU.add, reads=[b_rt], writes=[b_Qf])
                        for h in range(8):
                            op("pe", "transpose", tpb[0:96, h * 128:(h + 1) * 128], Qf[:, h, :], idb,
                               reads=[b_Qf, b_idb], writes=[b_tpb])
                        op("dve", "tensor_copy", QT[0:96, :, sub * 128:(sub + 1) * 128],
                           tpb[0:96, :].rearrange("p (h t) -> p h t", h=8), reads=[b_tpb], writes=[b_QT])
                    if bstop == 3:
                        op("dve", "memset", ones, 1.0, reads=[b_QT], writes=[b_ones])
                        op("pool", "dma_start", out=x2_dst[0:128, 0:128], in_=ones, reads=[b_ones], dma=True, final=True)
                        return
                    items = [(h, kt) for h in range(8) for kt in range(nkt)]
                    sbank = (3, 4, 7)

                    def emit_S(i):
                        h, kt = items[i]
                        op("pe", "matmul", banks[sbank[i % 3]], KT[0:96, h, kt * 128:(kt + 1) * 128], QT[0:96, h, :],
                           start=True, stop=True, reads=[b_KT[kt], b_QT], writes=[b_S[i % 3]])

                    emit_S(0)
                    if len(items) > 1:
                        emit_S(1)
                    for i, (h, kt) in enumerate(items):
                        if i + 2 < len(items):
                            emit_S(i + 2)
                        Ob = banks[5 + h % 2]
                        bO = b_O[h % 2]
                        z_ = i % 3
                        op("act", "activation", PTb[z_], banks[sbank[z_]], AF.Exp, reads=[b_S[z_]], writes=[b_PTb[z_]])
                        for sub in range(4):
                            op("pe", "matmul", Ob[:, sub * 65:(sub + 1) * 65], PTb[z_][:, sub * 128:(sub + 1) * 128],
                               VA[:, kt, h, :], start=(kt == 0 and sub == 0), stop=(kt == nkt - 1),
                               skip_group_check=True, reads=[b_PTb[z_], b_VA[kt]], writes=[bO])
                        if kt == nkt - 1:
                            Ov = Ob[:, 0:260].rearrange("p (s d) -> p s d", s=4)
                            op("dve", "reciprocal", rec, Ov[:, :, 64], reads=[bO], writes=[b_rec])
                            for sub in range(4):
                                op("dve", "tensor_scalar", att[:, sub, h * 64:(h + 1) * 64], Ov[:, sub, 0:64],
                                   rec[:, sub:sub + 1], None, ALU.mult, reads=[bO, b_rec], writes=[b_att[sub]])
                    if bstop == 4:
                        op("dve", "memset", ones, 1.0, reads=b_att, writes=[b_ones])
                        op("pool", "dma_start", out=x2_dst[0:128, 0:128], in_=ones, reads=[b_ones], dma=True, final=True)
                        return
                    tpb, b_tpb = tpb_[0], b_tpb_[0]
                    b_mm = [bb[1], bb[2]]
                    for sub in range(4):
                        tix = qb * 4 + sub
                        r0 = s0 + tix * 128
                        op("act", "activation", junk[:, 0:512], att[:, sub, :], AF.Square, accum_out=st2[:, 2:3],
                           reads=[b_att[sub]], writes=[b_junk, b_st[2]])
                        rstd_col(2, 512)
                        op("dve", "scalar_tensor_tensor", yat, att[:, sub, :], st2[:, 2:3], gatt, ALU.mult, ALU.mult,
                           reads=[b_att[sub], b_st[2], b_gb], writes=[b_yat])
                        for k in range(4):
                            op("pe", "transpose", tpb[:, k * 128:(k + 1) * 128], yat[:, k * 128:(k + 1) * 128], idb,
                               reads=[b_yat, b_idb], writes=[b_tpb])
                        op("dve", "tensor_copy", ymT[:, 4:8, :], tpb[:, 0:512].rearrange("p (a b) -> p a b", a=4),
                           reads=[b_tpb], writes=[b_ymT])
                        op("sp", "dma_start", out=ymT[:, 0:4, :],
                           in_=ys_d[:, r0:r0 + 128].rearrange("(c p) t -> p c t", p=128),
                           reads=[b_ysd], writes=[b_ymT], dma=True)
                        op("sp", "dma_start", out=x1t, in_=x1_src[r0:r0 + 128, :], writes=[b_x1t], dma=True)
                        for n in range(2):
                            for k in range(8):
                                op("pe", "matmul", banks[1 + n], ymT[:, k, :], wout[:, k, n * 512:(n + 1) * 512],
                                   start=(k == 0), stop=(k == 7), reads=[b_ymT, b_wout], writes=[b_mm[n]])
                        for n in range(2):
                            op("act", "activation", junk[:, 0:512], banks[1 + n], AF.Square,
                               accum_out=st2[:, 3 + n:4 + n], reads=[b_mm[n]], writes=[b_junk, b_st[3 + n]])
                        op("dve", "tensor_tensor", st2[:, 3:4], st2[:, 3:4], st2[:, 4:5], ALU.add,
                           reads=[b_st[3], b_st[4]], writes=[b_st[3]])
                        rstd_col(3, D)
                        for n in range(2):
                            op("dve", "scalar_tensor_tensor", tt2[n], banks[1 + n], st2[:, 3:4],
                               gmp[:, n * 512:(n + 1) * 512], ALU.mult, ALU.mult,
                               reads=[b_mm[n], b_st[3], b_gb], writes=[b_tt2[n]])
                            op("pool", "tensor_tensor", x1t[:, n * 512:(n + 1) * 512], tt2[n],
                               x1t[:, n * 512:(n + 1) * 512], ALU.add, reads=[b_tt2[n], b_x1t], writes=[b_x1t])
                        op("pool", "dma_start", out=x2_dst[r0:r0 + 128, :], in_=x1t, reads=[b_x1t],
                           writes=[b_x2d], dma=True, final=(stage == "B"))

        if stage == "A":
            ffn_phase(1, x_in, x1_d, True, ntiles=2)
        elif stage == "B":
            mixer_phase([(0, LB)])
        elif stage == "C":
            ffn_phase(2, x_in, y_out, False, ntiles=3)
        else:
            ffn_phase(1, x_in, x1_d, True)
            S.barrier()
            mixer_phase([(0, LP), (LP, LS)])
            S.barrier()
            ffn_phase(2, x2_d, y_out, False)
        S.emit(st)
    return nc


_NC_CACHE = {}


def kernel(**inputs):
    xp = np.asarray(inputs["x_prompt"], dtype=np.float32)
    xs = np.asarray(inputs["x_sample"], dtype=np.float32)
    if "nc" not in _NC_CACHE:
        _NC_CACHE["nc"] = build("full")
    nc = _NC_CACHE["nc"]
    shared = {}
    shared.update(ffn_inputs(inputs))
    shared.update(mixer_inputs(inputs))
    in_maps = []
    for b in range(8):
        m = dict(shared)
        m["x"] = np.ascontiguousarray(np.concatenate([xp[b], xs[b]], axis=0))
        in_maps.append(m)
    res = run_bass_kernel_spmd(nc, in_maps, core_ids=list(range(8)))
    y_prompt = np.stack([res.results[b]["y"][:LP] for b in range(8)], axis=0).astype(np.float32)
    y_sample = np.stack([res.results[b]["y"][LP:] for b in range(8)], axis=0).astype(np.float32)
    return (y_prompt, y_sample)


def ffn_inputs(d):
    m = {}
    _names = ("g_ffn1_pre", "w_ffn1_gate", "w_ffn1_up", "w_ffn1_down", "g_ffn1_post",
              "g_ffn2_pre", "w_ffn2_gate", "w_ffn2_up", "w_ffn2_down", "g_ffn2_post", "g_mix_pre", "w_in")
    assert all(n in d for n in _names)
    for f in (1, 2):
        m["g_ffn%d_pre" % f] = np.ascontiguousarray(d["g_ffn%d_pre" % f], dtype=np.float32).reshape(1, -1)
        m["w_ffn%d_gate" % f] = np.ascontiguousarray(d["w_ffn%d_gate" % f][0])
        m["w_ffn%d_up" % f] = np.ascontiguousarray(d["w_ffn%d_up" % f][0])
        m["w_ffn%d_down" % f] = np.ascontiguousarray(d["w_ffn%d_down" % f][0])
        m["g_ffn%d_post" % f] = np.ascontiguousarray(d["g_ffn%d_post" % f]).reshape(1, -1)
    m["g_mix_pre"] = np.ascontiguousarray(d["g_mix_pre"]).reshape(1, -1)
    m["w_in"] = np.ascontiguousarray(d["w_in"][0])
    return m


def mixer_inputs(d):
    m = {}
    for nm in ("w_glu", "w_uq", "w_ukv", "w_out"):
        m[nm] = np.ascontiguousarray(d[nm][0])
    for nm in ("d_skip", "b_glu", "g_ssm_out", "g_q", "g_kv", "g_att_out", "g_mix_post"):
        m[nm] = np.ascontiguousarray(d[nm]).reshape(1, -1)
    for dn in ("fwd", "bwd"):
        for nm in ("lam_re_", "lam_im_", "log_dt_", "b_re_", "b_im_", "c_re_", "c_im_"):
            m[nm + dn] = np.ascontiguousarray(d[nm + dn]).reshape(-1)
    return m
```

```python
import math
from contextlib import ExitStack

import numpy as np
import concourse.bass as bass
import concourse.mybir as mybir
from concourse.bass_utils import run_bass_kernel_spmd

F32 = mybir.dt.float32
BF16 = mybir.dt.bfloat16
I32 = mybir.dt.int32
ALU = mybir.AluOpType
AF = mybir.ActivationFunctionType
AX = mybir.AxisListType

D = 1024
DFF = 2816
NFF = DFF // 128
DIN = 1184
LP = 4096
LS = 2048
NTOK = LP + LS
EPS = 1e-6
TWO_PI = 2.0 * math.pi


class Buf:
    __slots__ = ("name", "last_w", "readers")

    def __init__(self, name):
        self.name = name
        self.last_w = None
        self.readers = []


class Op:
    __slots__ = ("eng", "name", "args", "kw", "dma", "lane", "laneval", "deps", "sig", "sigval")


class Sched:
    ENGS = ("pe", "act", "dve", "pool", "sp")

    def __init__(self, nc, lanes_per_queue=8):
        self.nc = nc
        self.ops = {e: [] for e in self.ENGS}
        self.nl = lanes_per_queue
        self.lane_rr = {e: 0 for e in self.ENGS}
        self.lane_last = {}
        self.lane_cnt = {}
        self.out_dmas = []

    def op(self, eng, name, *args, reads=(), writes=(), dma=False, final=False, **kw):
        o = Op()
        o.eng = eng
        o.name = name
        o.args = args
        o.kw = kw
        o.dma = dma
        o.sig = False
        o.sigval = 0
        o.deps = []
        o.lane = None
        o.laneval = 0
        deps = o.deps
        for b in reads:
            w = b.last_w
            if w is not None and (w.dma or not (w.eng == eng == "pe")):
                deps.append(w)
        for b in writes:
            w = b.last_w
            if w is not None and (w.dma or not (w.eng == eng == "pe")):
                deps.append(w)
            for r in b.readers:
                if r.dma or not (r.eng == eng == "pe"):
                    deps.append(r)
        if dma:
            l = self.lane_rr[eng]
            self.lane_rr[eng] = (l + 1) % self.nl
            key = (eng, l)
            prev = self.lane_last.get(key)
            if prev is not None:
                deps.append(prev)
            self.lane_last[key] = o
            c = self.lane_cnt.get(key, 0) + 1
            self.lane_cnt[key] = c
            o.lane = key
            o.laneval = 16 * c
            o.sig = True
            if final:
                self.out_dmas.append(o)
        for d in deps:
            d.sig = True
        for b in reads:
            b.readers.append(o)
        for b in writes:
            b.last_w = o
            b.readers = []
        self.ops[eng].append(o)
        return o

    def barrier(self):
        lasts = []
        for e in self.ENGS:
            for o in reversed(self.ops[e]):
                if not o.dma:
                    lasts.append(o)
                    break
        lasts += list(self.lane_last.values())
        for e in self.ENGS:
            o = Op()
            o.eng = e
            o.name = "nop"
            o.args = ()
            o.kw = {}
            o.dma = False
            o.sig = False
            o.sigval = 0
            o.lane = None
            o.laneval = 0
            o.deps = list(lasts)
            for d in o.deps:
                d.sig = True
            self.ops[e].append(o)

    def emit(self, stack):
        nc = self.nc
        sems = {e: stack.enter_context(nc.semaphore("s_" + e)) for e in self.ENGS}
        lsems = {}
        for key in self.lane_cnt:
            lsems[key] = stack.enter_context(nc.semaphore("l_%s%d" % key))
        for e in self.ENGS:
            c = 0
            for o in self.ops[e]:
                if not o.dma and o.sig:
                    c += 1
                    o.sigval = c
        fw = {}
        for o in self.out_dmas:
            if o.lane not in fw or fw[o.lane] < o.laneval:
                fw[o.lane] = o.laneval
        final_waits = [(lsems[k], v) for k, v in fw.items()]

        def replay(e, h):
            waited = {}
            for o in self.ops[e]:
                for d in o.deps:
                    if d.dma:
                        s, v = lsems[d.lane], d.laneval
                        k = ("l",) + d.lane
                    else:
                        s, v = sems[d.eng], d.sigval
                        k = d.eng
                    if waited.get(k, 0) >= v:
                        continue
                    waited[k] = v
                    h.wait_ge(s, v)
                ins = getattr(h, o.name)(*o.args, **o.kw)
                if o.dma:
                    ins.then_inc(lsems[o.lane], 16)
                elif o.sig:
                    ins.then_inc(sems[e], 1)
            if e == "sp":
                for s, v in final_waits:
                    h.wait_ge(s, v)

        with nc.Block() as block:
            @block.tensor
            def _(h):
                replay("pe", h)

            @block.scalar
            def _(h):
                replay("act", h)

            @block.vector
            def _(h):
                replay("dve", h)

            @block.gpsimd
            def _(h):
                replay("pool", h)

            @block.sync
            def _(h):
                replay("sp", h)


class Arena:
    def __init__(self, base_ap, nwords):
        self.base = base_ap
        self.n = nwords
        self.off = 0
        self.maxoff = 0

    def alloc(self, free_shape, dt):
        n = 1
        for s in free_shape:
            n *= s
        nw = n if dt in (F32, I32) else (n + 1) // 2
        assert self.off + nw <= self.n, ("arena overflow", self.off, nw, self.n)
        a = self.base[:, self.off:self.off + nw]
        self.off += nw
        self.maxoff = max(self.maxoff, self.off)
        if dt != F32:
            a = a.bitcast(dt)
            if dt == BF16 and n != 2 * nw:
                a = a[:, 0:n]
        if len(free_shape) == 2:
            a = a.rearrange("p (a b) -> p a b", a=free_shape[0])
        elif len(free_shape) == 3:
            a = a.rearrange("p (a b c) -> p a b c", a=free_shape[0], b=free_shape[1])
        return a


def build(stage="full", LB=1024, bstop=9):
    nc = bass.Bass("TRN2", target_bir_lowering=False)

    def din(name, shape):
        return nc.dram_tensor(name, list(shape), F32, kind="ExternalInput").ap()

    x_in = din("x", [NTOK, D]) if stage != "B" else None
    W = {}
    for f in ((1, 2) if stage != "B" else ()):
        W["g_pre%d" % f] = din("g_ffn%d_pre" % f, [1, D])
        W["wg%d" % f] = din("w_ffn%d_gate" % f, [D, DFF])
        W["wu%d" % f] = din("w_ffn%d_up" % f, [D, DFF])
        W["wd%d" % f] = din("w_ffn%d_down" % f, [DFF, D])
        W["g_post%d" % f] = din("g_ffn%d_post" % f, [1, D])
    if stage != "B":
        W["g_mix_pre"] = din("g_mix_pre", [1, D])
        W["w_in"] = din("w_in", [D, DIN])
    y_out = nc.dram_tensor("y", [NTOK, D], F32, kind="ExternalOutput").ap()
    x1_d = nc.dram_tensor("x1_scr", [NTOK, D], F32, kind="Internal").ap()
    x2_d = nc.dram_tensor("x2_scr", [NTOK, D], F32, kind="Internal").ap()
    zq_d = nc.dram_tensor("zq_scr", [NTOK, 672], F32, kind="Internal").ap()
    uT_d = nc.dram_tensor("uT_scr", [512, NTOK], BF16, kind="Internal").ap()
    M = {}
    for nm, shp in (("w_glu", [512, 512]), ("w_uq", [384, 768]), ("w_ukv", [256, 1024]), ("w_out", [1024, 1024]),
                    ("d_skip", [1, 512]), ("b_glu", [1, 512]), ("g_ssm_out", [1, 512]), ("g_q", [1, 384]),
                    ("g_kv", [1, 256]), ("g_att_out", [1, 512]), ("g_mix_post", [1, 1024])):
        M[nm] = din(nm, shp)
    for dn in ("fwd", "bwd"):
        M["lam_re_" + dn] = din("lam_re_" + dn, [2048])
        M["lam_im_" + dn] = din("lam_im_" + dn, [2048])
        M["log_dt_" + dn] = din("log_dt_" + dn, [32])
        for pn in ("re", "im"):
            M["b_%s_%s" % (pn, dn)] = din("b_%s_%s" % (pn, dn), [32768])
            M["c_%s_%s" % (pn, dn)] = din("c_%s_%s" % (pn, dn), [32768])
    ys_d = nc.dram_tensor("ys_scr", [512, NTOK], BF16, kind="Internal").ap()
    b_ysd = Buf("ysd")
    b_x2d = Buf("x2d")
    if stage == "B":
        zq_src = din("t_zq", [LB, 672])
        uT_src = nc.dram_tensor("t_uT", [512, LB], BF16, kind="ExternalInput").ap()
        x1_src = din("t_x1", [LB, D])
        x2_dst = nc.dram_tensor("t_x2", [LB, D], F32, kind="ExternalOutput").ap()
    else:
        zq_src, uT_src, x1_src, x2_dst = zq_d, uT_d, x1_d, x2_d
    dbg = {}
    if stage == "A":
        dbg["x1"] = nc.dram_tensor("dbg_x1", [NTOK, D], F32, kind="ExternalOutput").ap()
        dbg["zq"] = nc.dram_tensor("dbg_zq", [NTOK, 672], F32, kind="ExternalOutput").ap()
        dbg["uT"] = nc.dram_tensor("dbg_uT", [512, NTOK], BF16, kind="ExternalOutput").ap()

    S = Sched(nc)
    op = S.op
    with ExitStack() as st:
        NW = 53000 - 1024
        arena_t = st.enter_context(nc.sbuf_tensor("arena", [128, 53000], F32))
        CT_g = arena_t[:, 53000 - 1024:53000].rearrange("p (a b c) -> p a b c", a=4, b=16)
        b_CT_g = Buf("CT")
        pall = st.enter_context(nc.psum_tensor("pall", [128, 4096], F32))
        banks = [pall[:, i * 512:(i + 1) * 512] for i in range(8)]

        ct_pending = []
        for di, dn in enumerate(("fwd", "bwd")):
            for part, pn in enumerate(("re", "im")):
                for gi in range(2):
                    for P in range(16):
                        ct_pending.append((di, dn, part, pn, gi, P))

        def issue_ct_loads(n=None):
            k = len(ct_pending) if n is None else min(n, len(ct_pending))
            for _ in range(k):
                di, dn, part, pn, gi, P = ct_pending.pop(0)
                src = M["c_%s_%s" % (pn, dn)]
                op("sp", "dma_start", out=CT_g[gi * 64:(gi + 1) * 64, di * 2 + part, P, :],
                   in_=bass.AP(src.tensor, gi * 1024 + P * 2048, [[1, 64], [64, 16]]),
                   writes=[b_CT_g], dma=True, allow_slow_non_contiguous=True)

        def ffn_phase(f, src_d, dst_d, with_win, ntiles=NTOK // 256):
            ar = Arena(arena_t[:, :], NW)
            wg = ar.alloc([8, DFF], BF16)
            wu = ar.alloc([8, DFF], BF16)
            wd = ar.alloc([NFF, D], BF16)
            b_wg = [[Buf("wg%d_%d" % (k, h)) for h in range(2)] for k in range(8)]
            b_wu = [[Buf("wu%d_%d" % (k, h)) for h in range(2)] for k in range(8)]
            b_wd = [Buf("wd%d" % k) for k in range(NFF)]
            if with_win:
                win = ar.alloc([8, DIN], BF16)
                b_win = [Buf("win%d" % k) for k in range(8)]
            gpost = ar.alloc([D], F32)
            b_gpost = Buf("gpost")
            gpreT = ar.alloc([8], F32)
            b_gpreT = Buf("gpreT")
            gmixT = ar.alloc([8], F32)
            b_gmixT = Buf("gmixT")
            idf = ar.alloc([128], F32)
            b_idf = Buf("idf")
            NPAR = 1 if with_win else 2
            xts = [[ar.alloc([D], F32) for _ in range(2)] for _ in range(NPAR)]
            b_xts = [[Buf("xt%d_%d" % (p, i)) for i in range(2)] for p in range(NPAR)]
            xt, b_xt = xts[0], b_xts[0]
            xn = ar.alloc([D], F32)
            b_xn = Buf("xn")
            hTs = [ar.alloc([8, 256], BF16) for _ in range(NPAR)]
            b_hTs = [[[Buf("hT%d_%d_%d" % (p, s, k)) for k in range(8)] for s in range(2)] for p in range(NPAR)]
            hT, b_hT = hTs[0], b_hTs[0]
            actT = ar.alloc([NFF, 256], BF16)
            b_actT = [Buf("actT%d" % c) for c in range(NFF)]
            sg = [ar.alloc([256], F32) for _ in range(2)]
            b_sg = [Buf("sg%d" % i) for i in range(2)]
            tt = [ar.alloc([512], F32) for _ in range(2)]
            b_tt = [Buf("tt%d" % i) for i in range(2)]
            junk = ar.alloc([D], BF16)
            b_junk = Buf("junkA")
            st_ss = ar.alloc([8], F32)
            b_ss = [Buf("ss%d" % i) for i in range(8)]
            if with_win:
                zqt = ar.alloc([672], F32)
                b_zqt = Buf("zqt")
                uTt = ar.alloc([4, 256], BF16)
                b_uTt = Buf("uTt")

            HC = DFF // 2
            for hf in range(2):
                for k in range(8):
                    op("pool", "dma_start", out=wg[:, k, hf * HC:(hf + 1) * HC],
                       in_=W["wg%d" % f][k * 128:(k + 1) * 128, hf * HC:(hf + 1) * HC], writes=[b_wg[k][hf]], dma=True)
                    op("pool", "dma_start", out=wu[:, k, hf * HC:(hf + 1) * HC],
                       in_=W["wu%d" % f][k * 128:(k + 1) * 128, hf * HC:(hf + 1) * HC], writes=[b_wu[k][hf]], dma=True)
            for c in range(NFF):
                op("pool", "dma_start", out=wd[:, c, :], in_=W["wd%d" % f][c * 128:(c + 1) * 128, :],
                   writes=[b_wd[c]], dma=True)
            if with_win:
                for k in range(8):
                    op("pool", "dma_start", out=win[:, k, :], in_=W["w_in"][k * 128:(k + 1) * 128, :],
                       writes=[b_win[k]], dma=True)
            op("sp", "dma_start", out=gpost, in_=W["g_post%d" % f].partition_broadcast(128),
               writes=[b_gpost], dma=True)
            op("sp", "dma_start", out=gpreT, in_=W["g_pre%d" % f][0, :].rearrange("(k p) -> p k", p=128),
               writes=[b_gpreT], dma=True, allow_slow_non_contiguous=True)
            if with_win:
                op("sp", "dma_start", out=gmixT, in_=W["g_mix_pre"][0, :].rearrange("(k p) -> p k", p=128),
                   writes=[b_gmixT], dma=True, allow_slow_non_contiguous=True)
            idi = ar.alloc([128], I32)
            op("pool", "iota", idi, [[1, 128]], base=0, channel_multiplier=-1, writes=[b_idf])
            op("dve", "tensor_copy", idf, idi, reads=[b_idf], writes=[b_idf])
            op("dve", "tensor_scalar", idf, idf, 0.0, None, ALU.is_equal, reads=[b_idf], writes=[b_idf])

            def rstd_from_ss(col, n_feat, extra_scale=1.0):
                c = st_ss[:, col:col + 1]
                op("dve", "tensor_scalar", c, c, 1.0 / n_feat, EPS, ALU.mult, ALU.add,
                   reads=[b_ss[col]], writes=[b_ss[col]])
                op("act", "sqrt", c, c, reads=[b_ss[col]], writes=[b_ss[col]])
                op("dve", "reciprocal", c, c, reads=[b_ss[col]], writes=[b_ss[col]])
                if extra_scale != 1.0:
                    op("dve", "tensor_scalar", c, c, extra_scale, None, ALU.mult,
                       reads=[b_ss[col]], writes=[b_ss[col]])

            def norm_transpose(s, gT, b_gT, sscol):
                op("act", "activation", junk, xt[s], AF.Square, accum_out=st_ss[:, sscol:sscol + 1],
                   reads=[b_xt[s]], writes=[b_junk, b_ss[sscol]])
                rstd_from_ss(sscol, D)
                op("dve", "tensor_scalar", xn, xt[s], st_ss[:, sscol:sscol + 1], None, ALU.mult,
                   reads=[b_xt[s], b_ss[sscol]], writes=[b_xn])
                for k in range(8):
                    bk = k // 4
                    sl = (k % 4) * 128
                    op("pe", "transpose", banks[bk][:, sl:sl + 128], xn[:, k * 128:(k + 1) * 128], idf,
                       reads=[b_xn, b_idf], writes=[b_tp[bk]])
                for k in range(8):
                    bk = k // 4
                    sl = (k % 4) * 128
                    if bk == 0:
                        op("act", "activation", hT[:, k, s * 128:(s + 1) * 128], banks[bk][:, sl:sl + 128],
                           AF.Copy, scale=gT[:, k:k + 1], reads=[b_tp[bk], b_gT], writes=[b_hT[s][k]])
                    else:
                        op("dve", "tensor_scalar", hT[:, k, s * 128:(s + 1) * 128], banks[bk][:, sl:sl + 128],
                           gT[:, k:k + 1], None, ALU.mult, reads=[b_tp[bk], b_gT], writes=[b_hT[s][k]])

            b_tp = [Buf("tp%d" % k) for k in range(2)]
            b_pg = [Buf("pg0"), Buf("pg1")]
            b_pd = [[Buf("pd%d%d" % (s, n)) for n in range(2)] for s in range(2)]

            def load_and_norm(ti):
                r0_ = ti * 256
                for s in range(2):
                    op("sp", "dma_start", out=xt[s], in_=src_d[r0_ + s * 128:r0_ + (s + 1) * 128, :],
                       writes=[b_xt[s]], dma=True)
                if with_win and stage == "full" and ti >= 1:
                    issue_ct_loads(8)
                for s in range(2):
                    norm_transpose(s, gpreT, b_gpreT, s)

            for ti in range(ntiles):
                r0 = ti * 256
                par_ = ti % NPAR
                xt, b_xt, hT, b_hT = xts[par_], b_xts[par_], hTs[par_], b_hTs[par_]
                if NPAR == 1 or ti == 0:
                    load_and_norm(ti)
                hreads = lambda k: [b_hT[0][k], b_hT[1][k]]
                for c in range(NFF):
                    pg = banks[2 + (c % 2)]
                    bpg = b_pg[c % 2]
                    for k in range(8):
                        op("pe", "matmul", pg[:, 0:256], wg[:, k, c * 128:(c + 1) * 128], hT[:, k, :],
                           start=(k == 0), stop=(k == 7), reads=[b_wg[k][c // 11]] + hreads(k), writes=[bpg])
                    for k in range(8):
                        op("pe", "matmul", pg[:, 256:512], wu[:, k, c * 128:(c + 1) * 128], hT[:, k, :],
                           start=(k == 0), stop=(k == 7), reads=[b_wu[k][c // 11]] + hreads(k), writes=[bpg])
                    op("act", "activation", sg[c % 2], pg[:, 0:256], AF.Silu, reads=[bpg], writes=[b_sg[c % 2]])
                    op("dve", "tensor_tensor", actT[:, c, :], sg[c % 2], pg[:, 256:512], ALU.mult,
                       reads=[b_sg[c % 2], bpg], writes=[b_actT[c]])
                if NPAR == 2 and ti + 1 < ntiles:
                    pn_ = (ti + 1) % 2
                    xt, b_xt, hT, b_hT = xts[pn_], b_xts[pn_], hTs[pn_], b_hTs[pn_]
                    load_and_norm(ti + 1)
                    xt, b_xt, hT, b_hT = xts[par_], b_xts[par_], hTs[par_], b_hTs[par_]
                for s in range(2):
                    for n in range(2):
                        pd = banks[4 + 2 * s + n]
                        for c in range(NFF):
                            op("pe", "matmul", pd, actT[:, c, s * 128:(s + 1) * 128],
                               wd[:, c, n * 512:(n + 1) * 512], start=(c == 0), stop=(c == NFF - 1),
                               reads=[b_actT[c], b_wd[c]], writes=[b_pd[s][n]])
                for s in range(2):
                    for n in range(2):
                        pd = banks[4 + 2 * s + n]
                        col = 2 + 2 * s + n
                        op("act", "activation", junk[:, 0:512], pd, AF.Square,
                           accum_out=st_ss[:, col:col + 1], reads=[b_pd[s][n]], writes=[b_junk, b_ss[col]])
                    c0 = 2 + 2 * s
                    op("dve", "tensor_tensor", st_ss[:, c0:c0 + 1], st_ss[:, c0:c0 + 1], st_ss[:, c0 + 1:c0 + 2],
                       ALU.add, reads=[b_ss[c0], b_ss[c0 + 1]], writes=[b_ss[c0]])
                    rstd_from_ss(c0, D, extra_scale=0.5)
                    for n in range(2):
                        pd = banks[4 + 2 * s + n]
                        op("dve", "scalar_tensor_tensor", tt[n], pd, st_ss[:, c0:c0 + 1],
                           gpost[:, n * 512:(n + 1) * 512], ALU.mult, ALU.mult,
                           reads=[b_pd[s][n], b_ss[c0], b_gpost], writes=[b_tt[n]])
                        op("pool", "tensor_tensor", xt[s][:, n * 512:(n + 1) * 512], tt[n],
                           xt[s][:, n * 512:(n + 1) * 512], ALU.add, reads=[b_tt[n], b_xt[s]], writes=[b_xt[s]])
                    op("pool", "dma_start", out=dst_d[r0 + s * 128:r0 + (s + 1) * 128, :], in_=xt[s],
                       reads=[b_xt[s]], dma=True, final=(not with_win))
                    if stage == "A" and with_win:
                        op("pool", "dma_start", out=dbg["x1"][r0 + s * 128:r0 + (s + 1) * 128, :], in_=xt[s],
                           reads=[b_xt[s]], dma=True, final=True)
                if not with_win:
                    continue
                for s in range(2):
                    norm_transpose(s, gmixT, b_gmixT, 6 + s)
                for uc in range(4):
                    pu = banks[2 + (uc % 2)]
                    bpu = b_pg[uc % 2]
                    for k in range(8):
                        op("pe", "matmul", pu[:, 0:256], win[:, k, uc * 128:(uc + 1) * 128], hT[:, k, :],
                           start=(k == 0), stop=(k == 7), reads=[b_win[k]] + hreads(k), writes=[bpu])
                    op("act", "copy", uTt[:, uc, :], pu[:, 0:256], reads=[bpu], writes=[b_uTt])
                op("pool", "dma_start", out=uT_d[:, r0:r0 + 256].rearrange("(c p) t -> p c t", p=128), in_=uTt,
                   reads=[b_uTt], dma=True)
                if stage == "A":
                    op("pool", "dma_start", out=dbg["uT"][:, r0:r0 + 256].rearrange("(c p) t -> p c t", p=128),
                       in_=uTt, reads=[b_uTt], dma=True, final=True)
                for s in range(2):
                    for (n0, nn, bi) in ((512, 512, 4 + 2 * s), (1024, 160, 5 + 2 * s)):
                        pz = banks[bi]
                        bpz = b_pd[s][bi - 4 - 2 * s]
                        for k in range(8):
                            op("pe", "matmul", pz[:, 0:nn], hT[:, k, s * 128:(s + 1) * 128], win[:, k, n0:n0 + nn],
                               start=(k == 0), stop=(k == 7), reads=[b_win[k], b_hT[s][k]], writes=[bpz])
                        if nn == 512:
                            op("act", "copy", zqt[:, 0:512], pz[:, 0:512], reads=[bpz], writes=[b_zqt])
                        else:
                            op("dve", "tensor_copy", zqt[:, 512:672], pz[:, 0:160], reads=[bpz], writes=[b_zqt])
                    op("pool", "dma_start", out=zq_d[r0 + s * 128:r0 + (s + 1) * 128, :], in_=zqt,
                       reads=[b_zqt], dma=True)
                    if stage == "A":
                        op("pool", "dma_start", out=dbg["zq"][r0 + s * 128:r0 + (s + 1) * 128, :], in_=zqt,
                           reads=[b_zqt], dma=True, final=True)

        def dap(ap, offset, dims):
            return bass.AP(ap.tensor, offset, [list(d) for d in dims])

        def mixer_phase(seqs):
            ar = Arena(arena_t[:, :], NW)
            QSCALE = 96.0 ** -0.5
            idf = ar.alloc([128], F32); b_idf = Buf("idfB")
            idi = ar.alloc([128], I32)
            idb = ar.alloc([128], BF16); b_idb = Buf("idb")
            ones = ar.alloc([128], F32); b_ones = Buf("ones")
            op("pool", "iota", idi, [[1, 128]], base=0, channel_multiplier=-1, writes=[b_idf])
            op("dve", "tensor_copy", idf, idi, reads=[b_idf], writes=[b_idf])
            op("dve", "tensor_scalar", idf, idf, 0.0, None, ALU.is_equal, reads=[b_idf], writes=[b_idf])
            op("dve", "tensor_copy", idb, idf, reads=[b_idf], writes=[b_idb])
            op("dve", "memset", ones, 1.0, writes=[b_ones])

            wglu = ar.alloc([4, 512], BF16); b_wglu = Buf("wglu")
            for k in range(4):
                op("pool", "dma_start", out=wglu[:, k, :], in_=M["w_glu"][k * 128:(k + 1) * 128, :],
                   writes=[b_wglu], dma=True)
            smallT = ar.alloc([3, 4], F32); b_smallT = Buf("smallT")
            for i, nm in enumerate(("d_skip", "b_glu", "g_ssm_out")):
                op("sp", "dma_start", out=smallT[:, i, :], in_=M[nm][0, :].rearrange("(k p) -> p k", p=128),
                   writes=[b_smallT], dma=True, allow_slow_non_contiguous=True)
            dskipT, bgluT, gssmT = smallT[:, 0, :], smallT[:, 1, :], smallT[:, 2, :]

            mark_params = ar.off
            NPT = 72
            PT = ar.alloc([NPT, 32], F32)
            b_PT = [Buf("PT%d" % i) for i in range(NPT)]
            pi_ = [0]

            def newp():
                i = pi_[0]; pi_[0] += 1
                return i
            def P_(i):
                return PT[:, i, :]
            LR, LI, LD = newp(), newp(), newp()
            for di, dn in enumerate(("fwd", "bwd")):
                op("sp", "dma_start", out=PT[:, LR, di * 16:(di + 1) * 16],
                   in_=M["lam_re_" + dn].rearrange("(P q) -> q P", q=128), writes=[b_PT[LR]], dma=True,
                   allow_slow_non_contiguous=True)
                op("sp", "dma_start", out=PT[:, LI, di * 16:(di + 1) * 16],
                   in_=M["lam_im_" + dn].rearrange("(P q) -> q P", q=128), writes=[b_PT[LI]], dma=True,
                   allow_slow_non_contiguous=True)
                for gi in range(2):
                    op("sp", "dma_start", out=PT[gi * 64:(gi + 1) * 64, LD, di * 16:(di + 1) * 16],
                       in_=dap(M["log_dt_" + dn], gi, [[0, 64], [2, 16]]), writes=[b_PT[LD]], dma=True,
                       allow_slow_non_contiguous=True)

            def tt_(o, a, b, alu):
                op("dve", "tensor_tensor", P_(o), P_(a), P_(b), alu, reads=[b_PT[a], b_PT[b]], writes=[b_PT[o]])
            def ts_(o, a, s1, s2, o0, o1=None):
                if o1 is None:
                    op("dve", "tensor_scalar", P_(o), P_(a), s1, None, o0, reads=[b_PT[a]], writes=[b_PT[o]])
                else:
                    op("dve", "tensor_scalar", P_(o), P_(a), s1, s2, o0, o1, reads=[b_PT[a]], writes=[b_PT[o]])
            def act_(o, a, fn, **kw):
                op("act", "activation", P_(o), P_(a), fn, reads=[b_PT[a]], writes=[b_PT[o]], **kw)

            KI = ar.alloc([32], I32); b_KI = Buf("KI")
            def sin_of(o, a, shift):
                m, kf = newp(), newp()
                ts_(m, a, shift, None, ALU.add)
                ts_(kf, m, 1.0 / TWO_PI, None, ALU.mult)
                op("dve", "tensor_copy", KI, P_(kf), reads=[b_PT[kf]], writes=[b_KI])
                op("dve", "tensor_copy", P_(kf), KI, reads=[b_KI], writes=[b_PT[kf]])
                op("dve", "scalar_tensor_tensor", P_(m), P_(kf), -TWO_PI, P_(m), ALU.mult, ALU.add,
                   reads=[b_PT[kf], b_PT[m]], writes=[b_PT[m]])
                ts_(kf, m, math.pi, -TWO_PI, ALU.is_gt, ALU.mult)
                tt_(m, m, kf, ALU.add)
                ts_(kf, m, -math.pi, TWO_PI, ALU.is_lt, ALU.mult)
                tt_(m, m, kf, ALU.add)
                act_(o, m, AF.Sin)

            DT, AR_, TH, R, SN, CS = newp(), newp(), newp(), newp(), newp(), newp()
            act_(DT, LD, AF.Exp)
            tt_(AR_, LR, DT, ALU.mult)
            tt_(TH, LI, DT, ALU.mult)
            act_(R, AR_, AF.Exp)
            sin_of(SN, TH, 0.0)
            sin_of(CS, TH, math.pi / 2)
            ABR, ABI, DEN, T1, T2, COR, COI = newp(), newp(), newp(), newp(), newp(), newp(), newp()
            tt_(ABR, R, CS, ALU.mult)
            tt_(ABI, R, SN, ALU.mult)
            NRT = newp()
            ts_(NRT, ABR, -1.0, None, ALU.add)
            tt_(DEN, LR, LR, ALU.mult)
            tt_(T1, LI, LI, ALU.mult)
            tt_(DEN, DEN, T1, ALU.add)
            op("dve", "reciprocal", P_(DEN), P_(DEN), reads=[b_PT[DEN]], writes=[b_PT[DEN]])
            tt_(T1, NRT, LR, ALU.mult)
            tt_(T2, ABI, LI, ALU.mult)
            tt_(T1, T1, T2, ALU.add)
            tt_(COR, T1, DEN, ALU.mult)
            tt_(T1, ABI, LR, ALU.mult)
            tt_(T2, NRT, LI, ALU.mult)
            tt_(T1, T1, T2, ALU.subtract)
            tt_(COI, T1, DEN, ALU.mult)
            MC, MS = [CS], [SN]
            for k in range(11):
                c2, s2 = newp(), newp()
                tt_(T1, MC[k], MC[k], ALU.mult)
                tt_(T2, MS[k], MS[k], ALU.mult)
                tt_(c2, T1, T2, ALU.subtract)
                tt_(T1, MC[k], MS[k], ALU.mult)
                ts_(s2, T1, 2.0, None, ALU.mult)
                MC.append(c2); MS.append(s2)
            AW_r, AW_i, NAW_i = [None, ABR], [None, ABI], [None]
            def cmul(ar_, ai_, br_, bi_):
                o_r, o_i = newp(), newp()
                tt_(T1, ar_, br_, ALU.mult)
                tt_(T2, ai_, bi_, ALU.mult)
                tt_(o_r, T1, T2, ALU.subtract)
                tt_(T1, ar_, bi_, ALU.mult)
                tt_(T2, ai_, br_, ALU.mult)
                tt_(o_i, T1, T2, ALU.add)
                return o_r, o_i
            a2 = cmul(ABR, ABI, ABR, ABI)
            a3 = cmul(a2[0], a2[1], ABR, ABI)
            a4 = cmul(a2[0], a2[1], a2[0], a2[1])
            for (r_, i_) in (a2, a3, a4):
                AW_r.append(r_); AW_i.append(i_)
            for pw in range(1, 5):
                n_ = newp()
                ts_(n_, AW_i[pw], -1.0, None, ALU.mult)
                NAW_i.append(n_)
            R4 = newp()
            tt_(R4, R, R, ALU.mult)
            tt_(R4, R4, R4, ALU.mult)
            assert pi_[0] <= NPT, pi_[0]

            def combo(P, di, part):
                return (di * 16 + P) * 2 + part
            def w1i(q, di, part, tap):
                return ((q * 2 + di) * 2 + part) * 4 + tap
            def cab(j, pw, var):
                return (j * 4 + (pw - 1)) * 2 + var
            def ca32(j, pw, var):
                return (j * 5 + pw) * 2 + var
            def kdi(q, di, d):
                return (q * 2 + di) * 4 + d
            W1c = ar.alloc([64, 128], BF16); b_W1c = Buf("W1c")
            CAc = ar.alloc([256, 32], BF16); b_CAc = Buf("CAc")
            Kd = ar.alloc([32, 128], BF16); b_Kd = Buf("Kd")
            mark_b1 = ar.off
            BnP = ar.alloc([64, 128], F32); b_BnP = Buf("BnP")
            op("pool", "memset", BnP, 0.0, writes=[b_BnP])
            for di, dn in enumerate(("fwd", "bwd")):
                for part, pn in enumerate(("re", "im")):
                    src = M["b_%s_%s" % (pn, dn)]
                    for g in range(32):
                        P, gi = g // 2, g % 2
                        c0 = 32 * (P % 4) + gi * 16
                        op("sp", "dma_start", out=BnP[gi * 64:(gi + 1) * 64, combo(P, di, part), c0:c0 + 16],
                           in_=dap(src, g * 1024, [[16, 64], [1, 16]]), writes=[b_BnP], dma=True)
            issue_ct_loads()
            CT, b_CT = CT_g, b_CT_g
            ctmp = ar.alloc([8, 16], F32); b_ctmp = Buf("ctmp")
            CA32 = ar.alloc([320, 32], F32); b_CA32 = Buf("CA32")
            op("pool", "memset", CA32, 0.0, writes=[b_CA32])
            cw_ = [ar.alloc([16, 16], F32) for _ in range(6)]; b_cw = [Buf("cw%d" % i) for i in range(6)]
            cpr, cpi, ct0, ct1, car, cai = cw_
            b_cpr, b_cpi, b_ct0, b_ct1, b_car, b_cai = b_cw

            def bcP(tile_idx, di):
                t = PT[:, tile_idx, di * 16:(di + 1) * 16]
                return bass.AP(t.tensor, t.offset, [list(t.ap[0]), [1, 16], [0, 16]])

            for di in range(2):
                cr, ci = CT[:, di * 2, :, :], CT[:, di * 2 + 1, :, :]
                rd = [b_CT, b_PT[COR], b_PT[COI]]
                op("dve", "tensor_tensor", ct0, ci, bcP(COI, di), ALU.mult, reads=rd, writes=[b_ct0])
                op("dve", "tensor_tensor", cpr, cr, bcP(COR, di), ALU.mult, reads=rd, writes=[b_cpr])
                op("dve", "tensor_tensor", cpr, cpr, ct0, ALU.subtract, reads=[b_cpr, b_ct0], writes=[b_cpr])
                op("dve", "tensor_tensor", ct1, cr, bcP(COI, di), ALU.mult, reads=rd, writes=[b_ct1])
                op("dve", "tensor_tensor", cpi, ci, bcP(COR, di), ALU.mult, reads=rd, writes=[b_cpi])
                op("dve", "tensor_tensor", cpi, cpi, ct1, ALU.add, reads=[b_cpi, b_ct1], writes=[b_cpi])
                for pw in range(5):
                    if pw == 0:
                        src_r, b_sr = cpr, b_cpr
                        op("dve", "tensor_scalar", cai, cpi, -1.0, None, ALU.mult, reads=[b_cpi], writes=[b_cai])
                    else:
                        rdp = [b_cpr, b_cpi, b_PT[AW_r[pw]], b_PT[AW_i[pw]], b_PT[NAW_i[pw]]]
                        op("dve", "tensor_tensor", ct0, cpi, bcP(AW_i[pw], di), ALU.mult, reads=rdp, writes=[b_ct0])
                        op("dve", "tensor_tensor", car, cpr, bcP(AW_r[pw], di), ALU.mult, reads=rdp, writes=[b_car])
                        op("dve", "tensor_tensor", car, car, ct0, ALU.subtract, reads=[b_car, b_ct0], writes=[b_car])
                        op("dve", "tensor_tensor", ct1, cpi, bcP(AW_r[pw], di), ALU.mult, reads=rdp, writes=[b_ct1])
                        op("dve", "tensor_tensor", cai, cpr, bcP(NAW_i[pw], di), ALU.mult, reads=rdp, writes=[b_cai])
                        op("dve", "tensor_tensor", cai, cai, ct1, ALU.subtract, reads=[b_cai, b_ct1], writes=[b_cai])
                        src_r, b_sr = car, b_car
                    for var, (src_, bsrc_) in enumerate(((src_r, b_sr), (cai, b_cai))):
                        base = ca32(di * 16, pw, var)
                        for gi in range(2):
                            ps_ = slice(gi * 64, (gi + 1) * 64)
                            op("act" if gi == 0 else "dve", "copy" if gi == 0 else "tensor_copy",
                               CA32[ps_, base:base + 151:10, gi * 16:gi * 16 + 16], src_[ps_, :, :],
                               reads=[bsrc_], writes=[b_CA32])
            CA32v = CA32.rearrange("p (j w v) c -> p j w v c", j=32, w=5)
            CAcv = CAc.rearrange("p (j w v) c -> p j w v c", j=32, w=4)
            for j in range(32):
                op("act", "copy", CAcv[:, j, :, :, :], CA32v[:, j, 1:5, :, :], reads=[b_CA32], writes=[b_CAc])
            dg = ar.alloc([48, 128], F32); b_dg = Buf("dg")
            def dgi(P4, pw, v):
                return (P4 * 4 + pw) * 3 + v
            b_sb = [Buf("sbank%d" % i) for i in range(8)]
            wtmp = ar.alloc([128], F32); b_wtmp = Buf("wtmp")
            sbi = 0
            for q in range(4):
                for di in range(2):
                    for P4 in range(4):
                        j = di * 16 + q * 4 + P4
                        for pw in range(4):
                            if pw == 0:
                                op("dve", "tensor_copy", dg[:, dgi(P4, 0, 0), :], idf, reads=[b_idf], writes=[b_dg])
                                continue
                            for v, tl in enumerate((AW_r[pw], AW_i[pw], NAW_i[pw])):
                                op("dve", "tensor_scalar", dg[:, dgi(P4, pw, v), :], idf, PT[:, tl, j:j + 1], None, ALU.mult,
                                   reads=[b_idf, b_PT[tl]], writes=[b_dg])
                    for part in range(2):
                        for tap in range(4):
                            pw = (3 - tap) if di == 0 else tap
                            bk = banks[sbi % 8]; bbk = b_sb[sbi % 8]; sbi += 1
                            for P4 in range(4):
                                P = q * 4 + P4
                                o_ = bk[:, P4 * 128:(P4 + 1) * 128]
                                Bre, Bim = BnP[:, combo(P, di, 0), :], BnP[:, combo(P, di, 1), :]
                                rdm = [b_BnP, b_dg]
                                if pw == 0:
                                    op("pe", "matmul", o_, Bre if part == 0 else Bim, dg[:, dgi(P4, 0, 0), :], start=True, stop=True,
                                       skip_group_check=True, reads=rdm, writes=[bbk])
                                elif part == 0:
                                    op("pe", "matmul", o_, Bre, dg[:, dgi(P4, pw, 0), :], start=True, stop=False,
                                       skip_group_check=True, reads=rdm, writes=[bbk])
                                    op("pe", "matmul", o_, Bim, dg[:, dgi(P4, pw, 2), :], start=False, stop=True,
                                       skip_group_check=True, reads=rdm, writes=[bbk])
                                else:
                                    op("pe", "matmul", o_, Bre, dg[:, dgi(P4, pw, 1), :], start=True, stop=False,
                                       skip_group_check=True, reads=rdm, writes=[bbk])
                                    op("pe", "matmul", o_, Bim, dg[:, dgi(P4, pw, 0), :], start=False, stop=True,
                                       skip_group_check=True, reads=rdm, writes=[bbk])
                            op("dve", "tensor_reduce", wtmp, bk.rearrange("p (b s) -> p s b", b=4), AX.X, ALU.add,
                               reads=[bbk], writes=[b_wtmp])
                            op("act", "copy", W1c[:, w1i(q, di, part, tap), :], wtmp, reads=[b_wtmp], writes=[b_W1c])
            dsk = ar.alloc([4, 128], F32); b_dsk = Buf("dsk")
            for q in range(4):
                op("dve", "tensor_scalar", dsk[:, q, :], idf, dskipT[:, q:q + 1], None, ALU.mult,
                   reads=[b_idf, b_smallT], writes=[b_dsk])
            for q in range(4):
                for di in range(2):
                    bk = banks[sbi % 8]; bbk = b_sb[sbi % 8]; sbi += 1
                    first = True
                    for d in range(4):
                        if di == 0 and d == 0:
                            op("pe", "matmul", bk[:, 0:128], idf, dsk[:, q, :], start=first, stop=False,
                               skip_group_check=True, reads=[b_idf, b_dsk], writes=[bbk])
                            first = False
                        for P4 in range(4):
                            P = q * 4 + P4
                            j = di * 16 + P
                            o_ = bk[:, d * 128 + 32 * P4:d * 128 + 32 * P4 + 32]
                            for part in range(2):
                                op("pe", "matmul", o_, BnP[:, combo(P, di, part), :], CA32[:, ca32(j, d, part), :],
                                   start=first, stop=False, skip_group_check=True, reads=[b_BnP, b_CA32], writes=[bbk])
                                first = False
                    op("act", "copy", Kd[:, kdi(q, di, 0):kdi(q, di, 0) + 4, :], bk.rearrange("p (d c) -> p d c", d=4),
                       reads=[bbk], writes=[b_Kd])

            if bstop == 0:
                op("dve", "tensor_copy", ctmp[:, 0, :], Kd[:, 5, 0:16], reads=[b_Kd, b_W1c, b_CAc] + b_PT, writes=[b_ctmp])
                op("pool", "dma_start", out=x2_dst[0:128, 0:128], in_=ctmp, reads=[b_ctmp], dma=True, final=True)
                return
            for (s0, L) in seqs:
                S.barrier()
                ar.off = mark_b1
                nch = L // 512
                uT = ar.alloc([4, L], BF16); b_uT = Buf("uT")
                yacc = ar.alloc([4, L], F32)
                b_yacc = [[Buf("yacc%d_%d" % (q, c)) for c in range(nch)] for q in range(4)]
                for q in range(4):
                    op("sp", "dma_start", out=uT[:, q, :], in_=uT_src[q * 128:(q + 1) * 128, s0:s0 + L],
                       writes=[b_uT], dma=True)
                mark_loop = ar.off
                cosA = ar.alloc([4, 512], F32); sinA = ar.alloc([4, 512], F32)
                cosT = [cosA[:, i, :] for i in range(4)]
                sinT = [sinA[:, i, :] for i in range(4)]
                b_tab = [Buf("tab%d" % i) for i in range(4)]
                bus = [[ar.alloc([512], F32) for _ in range(2)] for _ in range(2)]
                b_bus = [[Buf("bus%d_%d" % (i, k)) for k in range(2)] for i in range(2)]
                Tall = ar.alloc([4, 512], F32)
                T_ = [Tall[:, k, :] for k in range(4)]; bT_ = [Buf("T%d" % k) for k in range(4)]
                tmpA = Tall[:, 0:2, :].rearrange("p a (b c) -> p (a b) c", b=2)
                bre = ar.alloc([512], F32); bim = ar.alloc([512], F32); b_bre = Buf("bre"); b_bim = Buf("bim")
                wre = ar.alloc([512], F32); wim = ar.alloc([512], F32); b_wre = Buf("wre"); b_wim = Buf("wim")
                XS = [[ar.alloc([514], BF16) for _ in range(4)] for _ in range(2)]
                b_XS = [[Buf("XS%d_%d" % (i, k)) for k in range(4)] for i in range(2)]
                cX = ar.alloc([4, 4], BF16); b_cX = [Buf("cX%d" % i) for i in range(4)]
                init = ar.alloc([4, 4], F32); b_init = [Buf("init%d" % i) for i in range(4)]
                wl = ar.alloc([4, 2], F32); b_wl = [Buf("wl%d" % i) for i in range(4)]
                b_y = [Buf("yb%d" % i) for i in range(4)]
                b_bx = [Buf("bx%d" % i) for i in range(4)]
                nchb = L // 2048
                it = 0
                for di in range(2):
                    for q in range(4):
                        j0 = di * 16 + q * 4
                        op("dve", "memset", cosA[:, :, 0:1], 1.0, writes=b_tab)
                        op("dve", "memset", sinA[:, :, 0:1], 0.0, writes=b_tab)
                        for k in range(9):
                            n = 1 << k

                            def bcn(tile_idx, n=n, j0=j0):
                                t = PT[:, tile_idx, j0:j0 + 4]
                                return bass.AP(t.tensor, t.offset, [list(t.ap[0]), [1, 4], [0, n]])
                            cB, sB = bcn(MC[k + 2]), bcn(MS[k + 2])
                            rd = b_tab + [b_PT[MC[k + 2]], b_PT[MS[k + 2]]]
                            op("dve", "tensor_tensor", tmpA[:, :, 0:n], sinA[:, :, 0:n], sB, ALU.mult, reads=rd, writes=[bT_[0], bT_[1]])
                            op("dve", "tensor_tensor", cosA[:, :, n:2 * n], cosA[:, :, 0:n], cB, ALU.mult, reads=rd, writes=b_tab)
                            op("dve", "tensor_tensor", cosA[:, :, n:2 * n], cosA[:, :, n:2 * n], tmpA[:, :, 0:n], ALU.subtract,
                               reads=b_tab + [bT_[0], bT_[1]], writes=b_tab)
                            op("dve", "tensor_tensor", tmpA[:, :, 0:n], cosA[:, :, 0:n], sB, ALU.mult, reads=rd, writes=[bT_[0], bT_[1]])
                            op("dve", "tensor_tensor", sinA[:, :, n:2 * n], sinA[:, :, 0:n], cB, ALU.mult, reads=rd, writes=b_tab)
                            op("dve", "tensor_tensor", sinA[:, :, n:2 * n], sinA[:, :, n:2 * n], tmpA[:, :, 0:n], ALU.add,
                               reads=b_tab + [bT_[0], bT_[1]], writes=b_tab)
                        chunks = list(range(nchb)) if di == 0 else list(range(nchb - 1, -1, -1))
                        items = [(ci_, ch, P4) for ci_, ch in enumerate(chunks) for P4 in range(4)]

                        def R_(ap, di=di):
                            return ap if di == 0 else ap[:, ::-1]

                        def emit_BX(n, di=di, q=q):
                            ci_, ch, P4 = items[n]
                            c0 = ch * 2048
                            z_ = n % 2
                            rows = slice(32 * P4, 32 * P4 + 32)
                            for part in range(2):
                                pb, bpb = banks[4 + z_ * 2 + part], b_bx[z_ * 2 + part]
                                for tap in range(4):
                                    op("pe", "matmul", pb, W1c[rows, w1i(q, di, part, tap), :],
                                       uT[rows, q, c0 + tap:c0 + 2048:4], start=(tap == 0), stop=(tap == 3),
                                       tile_position=(32 * P4, 0), reads=[b_W1c, b_uT], writes=[bpb])
                                op("act", "copy", bus[z_][part], R_(pb), reads=[bpb], writes=[b_bus[z_][part]])

                        def emit_FIR(ch, di=di, q=q):
                            c0 = ch * 2048
                            for b_ in range(4):
                                first = True
                                for tau in range(4):
                                    taps = range(0, tau + 1) if di == 0 else range(tau, 4)
                                    for tp in taps:
                                        t0_ = c0 + b_ * 512 + tp
                                        op("pe", "matmul", banks[b_][:, tau::4], Kd[:, kdi(q, di, abs(tau - tp)), :],
                                           uT[:, q, t0_:c0 + (b_ + 1) * 512:4], start=first, stop=False,
                                           skip_group_check=True, reads=[b_Kd, b_uT], writes=[b_y[b_]])
                                        first = False

                        def emit_DVE(n, di=di, q=q):
                            ci_, ch, P4 = items[n]
                            j = di * 16 + q * 4 + P4
                            cT, sT, bt = cosT[P4], sinT[P4], b_tab[P4]
                            rr = PT[:, R4, j:j + 1]
                            c512, s512 = PT[:, MC[11], j:j + 1], PT[:, MS[11], j:j + 1]
                            z_ = n % 2
                            ur, ui = bus[z_][0], bus[z_][1]
                            bur, bui = b_bus[z_][0], b_bus[z_][1]
                            op("dve", "tensor_tensor", T_[0], ur, cT, ALU.mult, reads=[bur, bt], writes=[bT_[0]])
                            op("dve", "tensor_tensor", T_[1], ui, sT, ALU.mult, reads=[bui, bt], writes=[bT_[1]])
                            op("dve", "tensor_tensor", T_[2], ui, cT, ALU.mult, reads=[bui, bt], writes=[bT_[2]])
                            op("dve", "tensor_tensor", T_[3], ur, sT, ALU.mult, reads=[bur, bt], writes=[bT_[3]])
                            op("dve", "tensor_tensor", bre, T_[0], T_[1], ALU.add, reads=[bT_[0], bT_[1]], writes=[b_bre])
                            op("dve", "tensor_tensor", bim, T_[2], T_[3], ALU.subtract, reads=[bT_[2], bT_[3]], writes=[b_bim])
                            X_, bX_ = XS[z_], b_XS[z_]
                            col = 0 if di == 0 else 513
                            if ci_ == 0:
                                i_re, i_im = 0.0, 0.0
                                ird = []
                                for k in range(4):
                                    op("pool", "memset", X_[k][:, col:col + 1], 0.0, writes=[bX_[k]])
                            else:
                                iv = init[:, P4, :]
                                wlr, wli = wl[:, P4, 0:1], wl[:, P4, 1:2]
                                op("dve", "tensor_scalar", iv[:, 2:3], wli, s512, None, ALU.mult,
                                   reads=[b_wl[P4], b_PT[MS[11]]], writes=[b_init[P4]])
                                op("dve", "scalar_tensor_tensor", iv[:, 0:1], wlr, c512, iv[:, 2:3],
                                   ALU.mult, ALU.subtract, reads=[b_wl[P4], b_PT[MC[11]], b_init[P4]], writes=[b_init[P4]])
                                op("dve", "tensor_scalar", iv[:, 3:4], wlr, s512, None, ALU.mult,
                                   reads=[b_wl[P4], b_PT[MS[11]]], writes=[b_init[P4]])
                                op("dve", "scalar_tensor_tensor", iv[:, 1:2], wli, c512, iv[:, 3:4],
                                   ALU.mult, ALU.add, reads=[b_wl[P4], b_PT[MC[11]], b_init[P4]], writes=[b_init[P4]])
                                i_re, i_im = iv[:, 0:1], iv[:, 1:2]
                                ird = [b_init[P4]]
                                for k in range(4):
                                    op("pool", "tensor_copy", X_[k][:, col:col + 1], cX[:, P4, k:k + 1],
                                       reads=[b_cX[P4]], writes=[bX_[k]])
                            rbc = rr.to_broadcast([128, 512])
                            op("dve", "tensor_tensor_scan", wre, rbc, bre, i_re, ALU.mult, ALU.add,
                               reads=[b_bre, b_PT[R4]] + ird, writes=[b_wre])
                            op("dve", "tensor_tensor_scan", wim, rbc, bim, i_im, ALU.mult, ALU.add,
                               reads=[b_bim, b_PT[R4]] + ird, writes=[b_wim])
                            if ci_ < nchb - 1:
                                op("act", "copy", wl[:, P4, 0:1], wre[:, 511:512], reads=[b_wre], writes=[b_wl[P4]])
                                op("act", "copy", wl[:, P4, 1:2], wim[:, 511:512], reads=[b_wim], writes=[b_wl[P4]])
                            Xw = [R_(X_[k][:, 1:513]) for k in range(4)]
                            op("dve", "tensor_tensor", Xw[0], wre, cT, ALU.mult, reads=[b_wre, bt], writes=[bX_[0]])
                            op("dve", "scalar_tensor_tensor", Xw[1], wim, -1.0, sT, ALU.mult, ALU.mult,
                               reads=[b_wim, bt], writes=[bX_[1]])
                            op("dve", "tensor_tensor", Xw[2], wim, cT, ALU.mult, reads=[b_wim, bt], writes=[bX_[2]])
                            op("dve", "tensor_tensor", Xw[3], wre, sT, ALU.mult, reads=[b_wre, bt], writes=[bX_[3]])
                            if ci_ < nchb - 1:
                                colc = 512 if di == 0 else 1
                                for k in range(4):
                                    op("pool", "tensor_copy", cX[:, P4, k:k + 1], X_[k][:, colc:colc + 1],
                                       reads=[bX_[k]], writes=[b_cX[P4]])

                        def emit_OUT(n, di=di, q=q):
                            ci_, ch, P4 = items[n]
                            j = di * 16 + q * 4 + P4
                            z_ = n % 2
                            X_, bX_ = XS[z_], b_XS[z_]
                            rows = slice(32 * P4, 32 * P4 + 32)
                            sh = 0 if di == 0 else 2
                            for b_ in range(4):
                                for tau in range(4):
                                    pw = tau + 1 if di == 0 else 4 - tau
                                    for k in range(4):
                                        last = (P4 == 3 and tau == 3 and k == 3)
                                        op("pe", "matmul", banks[b_][rows, tau::4], CAc[:, cab(j, pw, 0 if k < 2 else 1), :],
                                           X_[k][:, b_ * 128 + sh:b_ * 128 + sh + 128], start=False, stop=last,
                                           skip_group_check=True, tile_position=(0, 32 * P4),
                                           reads=[b_CAc, bX_[k]], writes=[b_y[b_]])

                        def emit_yevac(ch, di=di, q=q):
                            c0 = ch * 2048
                            for b_ in range(4):
                                ya = yacc[:, q, c0 + b_ * 512:c0 + (b_ + 1) * 512]
                                byq = b_yacc[q][(c0 + b_ * 512) // 512]
                                if di == 0:
                                    op("act", "copy", ya, banks[b_], reads=[b_y[b_]], writes=[byq])
                                else:
                                    op("dve", "tensor_tensor", ya, ya, banks[b_], ALU.add, reads=[b_y[b_], byq], writes=[byq])

                        emit_BX(0)
                        for n, (ci_, ch, P4) in enumerate(items):
                            if n + 1 < len(items):
                                emit_BX(n + 1)
                            if P4 == 0:
                                emit_FIR(ch)
                            emit_DVE(n)
                            emit_OUT(n)
                            if P4 == 3:
                                emit_yevac(ch)
                S.barrier()
                ar.off = mark_loop
                Y2 = ar.alloc([4, 512], F32); b_Y2 = [Buf("Y2_%d" % q) for q in range(4)]
                YG = ar.alloc([4, 512], F32); b_YG = [Buf("YG%d" % q) for q in range(4)]
                YGB = ar.alloc([4, 512], BF16); b_YGB = [Buf("YGB%d" % q) for q in range(4)]
                SQ = ar.alloc([4, 512], F32); b_SQ = [Buf("SQ%d" % q) for q in range(4)]
                GT = ar.alloc([4, 512], F32); b_GT = [Buf("GT%d" % q) for q in range(4)]
                RS = ar.alloc([512], F32); b_RS = Buf("RS")
                YS = ar.alloc([4, 512], BF16); b_YS = Buf("YS")
                b_gl = [Buf("glps%d" % i) for i in range(3)]; b_ms = Buf("msps")
                glb = (4, 5, 6)
                gi_ = 0
                for ch in range(nch):
                    t0 = ch * 512
                    for q in range(4):
                        Yq = yacc[:, q, t0:t0 + 512]
                        byq = b_yacc[q][ch]
                        op("act", "activation", Y2[:, q, :], Yq, AF.Square, reads=[byq], writes=[b_Y2[q]])
                        op("dve", "tensor_scalar", Y2[:, q, :], Y2[:, q, :], 0.044715, 1.0, ALU.mult, ALU.add,
                           reads=[b_Y2[q]], writes=[b_Y2[q]])
                        op("dve", "tensor_tensor", Y2[:, q, :], Y2[:, q, :], Yq, ALU.mult, reads=[b_Y2[q], byq], writes=[b_Y2[q]])
                        op("act", "activation", Y2[:, q, :], Y2[:, q, :], AF.Sigmoid, scale=2.0 * math.sqrt(2.0 / math.pi),
                           reads=[b_Y2[q]], writes=[b_Y2[q]])
                        op("dve", "tensor_tensor", YG[:, q, :], Yq, Y2[:, q, :], ALU.mult, reads=[byq, b_Y2[q]], writes=[b_YG[q]])
                        op("act", "copy", YGB[:, q, :], YG[:, q, :], reads=[b_YG[q]], writes=[b_YGB[q]])
                    for qo in range(4):
                        gl = banks[glb[gi_ % 3]]
                        bgl = b_gl[gi_ % 3]
                        gi_ += 1
                        for qi in range(4):
                            op("pe", "matmul", gl, wglu[:, qi, qo * 128:(qo + 1) * 128], YGB[:, qi, :],
                               start=(qi == 0), stop=(qi == 3), reads=[b_wglu, b_YGB[qi]], writes=[bgl])
                        op("act", "activation", GT[:, qo, :], gl, AF.Sigmoid, bias=bgluT[:, qo:qo + 1],
                           reads=[bgl, b_smallT], writes=[b_GT[qo]])
                        op("dve", "tensor_tensor", YG[:, qo, :], YG[:, qo, :], GT[:, qo, :], ALU.mult,
                           reads=[b_GT[qo], b_YG[qo]], writes=[b_YG[qo]])
                        op("act", "activation", SQ[:, qo, :], YG[:, qo, :], AF.Square, reads=[b_YG[qo]], writes=[b_SQ[qo]])
                    ms = banks[7]
                    for qo in range(4):
                        op("pe", "matmul", ms, ones, SQ[:, qo, :], start=(qo == 0), stop=(qo == 3),
                           reads=[b_ones, b_SQ[qo]], writes=[b_ms])
                    op("dve", "tensor_scalar", RS, ms, 1.0 / 512, EPS, ALU.mult, ALU.add, reads=[b_ms], writes=[b_RS])
                    op("act", "sqrt", RS, RS, reads=[b_RS], writes=[b_RS])
                    op("dve", "reciprocal", RS, RS, reads=[b_RS], writes=[b_RS])
                    for qo in range(4):
                        op("dve", "scalar_tensor_tensor", YS[:, qo, :], YG[:, qo, :], gssmT[:, qo:qo + 1], RS,
                           ALU.mult, ALU.mult, reads=[b_YG[qo], b_smallT, b_RS], writes=[b_YS])
                    op("pool", "dma_start", out=ys_d[:, s0 + t0:s0 + t0 + 512].rearrange("(c p) t -> p c t", p=128),
                       in_=YS, reads=[b_YS], writes=[b_ysd], dma=True)
            ar.off = mark_params
            S.barrier()
            if bstop == 1:
                op("dve", "memset", ones, 1.0, writes=[b_ones])
                op("pool", "dma_start", out=x2_dst[0:128, 0:128], in_=ones, reads=[b_ones, b_ysd], dma=True, final=True)
                return

            wuq = ar.alloc([3, 768], BF16); b_wuq = Buf("wuq")
            wukv = ar.alloc([2, 1024], BF16); b_wukv = Buf("wukv")
            wout = ar.alloc([8, 1024], BF16); b_wout = Buf("wout")
            for k in range(3):
                op("pool", "dma_start", out=wuq[:, k, :], in_=M["w_uq"][k * 128:(k + 1) * 128, :],
                   reads=[b_ysd], writes=[b_wuq], dma=True)
            for k in range(2):
                op("pool", "dma_start", out=wukv[:, k, :], in_=M["w_ukv"][k * 128:(k + 1) * 128, :],
                   reads=[b_ysd], writes=[b_wukv], dma=True)
            for k in range(8):
                op("pool", "dma_start", out=wout[:, k, :], in_=M["w_out"][k * 128:(k + 1) * 128, :],
                   reads=[b_ysd], writes=[b_wout], dma=True)
            gq = ar.alloc([384], F32); gkv = ar.alloc([256], F32); gatt = ar.alloc([512], F32); gmp = ar.alloc([D], F32)
            b_gb = Buf("gbc")
            for t_ap, nm in ((gq, "g_q"), (gkv, "g_kv"), (gatt, "g_att_out"), (gmp, "g_mix_post")):
                op("pool", "dma_start", out=t_ap, in_=M[nm].partition_broadcast(128), reads=[b_ysd], writes=[b_gb], dma=True)
            NT = max(L for _, L in seqs) // 128
            rc = ar.alloc([NT, 16], F32); rs = ar.alloc([NT, 16], F32)
            rcq = ar.alloc([NT, 16], F32); rsq = ar.alloc([NT, 16], F32)
            b_rope = Buf("rope")
            mark_b2 = ar.off
            ri = ar.alloc([NT, 16], I32); rf = ar.alloc([NT, 16], F32); rg = ar.alloc([NT, 16], F32)
            rk = ar.alloc([NT, 16], F32)
            rki = ar.alloc([NT, 16], I32)
            b_r = Buf("ropetmp")
            op("pool", "iota", ri, [[0, NT], [1, 16]], base=0, channel_multiplier=0, reads=[b_ysd], writes=[b_r])
            op("dve", "tensor_copy", rf, ri, reads=[b_r], writes=[b_r])
            op("act", "activation", rf, rf, AF.Exp, scale=-math.log(10000.0) / 16.0, reads=[b_r], writes=[b_r])
            op("pool", "iota", ri, [[128, NT], [0, 16]], base=0, channel_multiplier=1, reads=[b_r], writes=[b_r])
            op("dve", "tensor_copy", rg, ri, reads=[b_r], writes=[b_r])
            op("dve", "tensor_tensor", rg, rg, rf, ALU.mult, reads=[b_r], writes=[b_r])
            for (dst, shift) in ((rs, 0.0), (rc, math.pi / 2)):
                op("dve", "tensor_scalar", rf, rg, shift, None, ALU.add, reads=[b_r], writes=[b_r])
                op("dve", "tensor_scalar", rk, rf, 1.0 / TWO_PI, None, ALU.mult, reads=[b_r], writes=[b_r])
                op("dve", "tensor_copy", rki, rk, reads=[b_r], writes=[b_r])
                op("dve", "tensor_copy", rk, rki, reads=[b_r], writes=[b_r])
                op("dve", "scalar_tensor_tensor", rf, rk, -TWO_PI, rf, ALU.mult, ALU.add, reads=[b_r], writes=[b_r])
                op("dve", "tensor_scalar", rk, rf, math.pi, -TWO_PI, ALU.is_gt, ALU.mult, reads=[b_r], writes=[b_r])
                op("dve", "tensor_tensor", rf, rf, rk, ALU.add, reads=[b_r], writes=[b_r])
                op("dve", "tensor_scalar", rk, rf, -math.pi, TWO_PI, ALU.is_lt, ALU.mult, reads=[b_r], writes=[b_r])
                op("dve", "tensor_tensor", rf, rf, rk, ALU.add, reads=[b_r], writes=[b_r])
                op("act", "activation", dst, rf, AF.Sin, reads=[b_r], writes=[b_rope])
            op("dve", "tensor_scalar", rcq, rc, QSCALE, None, ALU.mult, reads=[b_rope], writes=[b_rope])
            op("dve", "tensor_scalar", rsq, rs, QSCALE, None, ALU.mult, reads=[b_rope], writes=[b_rope])

            def bc8(ap16):
                return bass.AP(ap16.tensor, ap16.offset, [list(ap16.ap[0]), [0, 8], [1, 16]])

            for (s0, L) in seqs:
                S.barrier()
                ar.off = mark_b2
                nkt = L // 128
                KT = ar.alloc([8, L], BF16); b_KT = [Buf("KT%d" % i) for i in range(nkt)]
                VA = ar.alloc([nkt, 8, 65], BF16); b_VA = [Buf("VA%d" % i) for i in range(nkt)]
                zq_ = [ar.alloc([672], F32) for _ in range(2)]; b_zq_ = [Buf("zq0"), Buf("zq1")]
                st2 = ar.alloc([8], F32); b_st = [Buf("st%d" % i) for i in range(8)]
                junk = ar.alloc([D], BF16)
                b_junk = Buf("junkB")
                kvn_ = [ar.alloc([384], BF16) for _ in range(2)]; b_kvn_ = [Buf("kvn0"), Buf("kvn1")]
                kvnT_ = [ar.alloc([3, 128], BF16) for _ in range(2)]; b_kvnT_ = [Buf("kvnT0"), Buf("kvnT1")]
                Kf_ = [ar.alloc([8, 96], BF16) for _ in range(2)]; b_Kf_ = [Buf("Kf0"), Buf("Kf1")]
                rt_ = [[ar.alloc([8, 16], F32) for _ in range(2)] for _ in range(2)]; b_rt_ = [Buf("rt0"), Buf("rt1")]
                QT = ar.alloc([8, 512], BF16); b_QT = Buf("QT")
                PTb = [ar.alloc([512], BF16) for _ in range(3)]; b_PTb = [Buf("PTb0"), Buf("PTb1"), Buf("PTb2")]
                att = ar.alloc([4, 512], F32); b_att = [Buf("att%d" % i) for i in range(4)]
                rec = ar.alloc([4], F32); b_rec = Buf("rec")
                yat = ar.alloc([512], BF16); b_yat = Buf("yat")
                ymT = ar.alloc([8, 128], BF16); b_ymT = Buf("ymT")
                x1t = ar.alloc([D], F32); b_x1t = Buf("x1t")
                qs_ = [ar.alloc([768], F32) for _ in range(2)]; b_qs_ = [Buf("qs0"), Buf("qs1")]
                tt2 = [ar.alloc([512], F32) for _ in range(2)]; b_tt2 = [Buf("tt2_0"), Buf("tt2_1")]
                bb = [Buf("b2bank%d" % i) for i in range(8)]
                b_S = [bb[3], bb[4], bb[7]]
                b_O = [bb[5], bb[6]]
                tpb_ = [banks[0].bitcast(BF16), banks[0].bitcast(BF16)]
                b_tpb_ = [bb[0], bb[0]]
                mmb_ = [(1, 2), (5, 6)]

                class _Par:
                    v = 0
                par = _Par()

                def rstd_col(col, n_feat):
                    c = st2[:, col:col + 1]
                    op("dve", "tensor_scalar", c, c, 1.0 / n_feat, EPS, ALU.mult, ALU.add, reads=[b_st[col]], writes=[b_st[col]])
                    op("act", "sqrt", c, c, reads=[b_st[col]], writes=[b_st[col]])
                    op("dve", "reciprocal", c, c, reads=[b_st[col]], writes=[b_st[col]])

                def norm_T(src_ap, n_feat, g_ap, col):
                    p_ = par.v
                    col = col + 5 * p_
                    b_zq, kvn, b_kvn, kvnT, b_kvnT, tpb, b_tpb = b_zq_[p_], kvn_[p_], b_kvn_[p_], kvnT_[p_], b_kvnT_[p_], tpb_[p_], b_tpb_[p_]
                    nk = n_feat // 128
                    op("act", "activation", junk[:, 0:n_feat], src_ap, AF.Square, accum_out=st2[:, col:col + 1],
                       reads=[b_zq], writes=[b_junk, b_st[col]])
                    rstd_col(col, n_feat)
                    op("dve", "scalar_tensor_tensor", kvn[:, 0:n_feat], src_ap, st2[:, col:col + 1], g_ap,
                       ALU.mult, ALU.mult, reads=[b_zq, b_st[col], b_gb], writes=[b_kvn])
                    for k in range(nk):
                        op("pe", "transpose", tpb[:, k * 128:(k + 1) * 128], kvn[:, k * 128:(k + 1) * 128], idb,
                           reads=[b_kvn, b_idb], writes=[b_tpb])
                    op("act", "copy", kvnT[:, 0:nk, :], tpb[:, 0:nk * 128].rearrange("p (a b) -> p a b", a=nk),
                       reads=[b_tpb], writes=[b_kvnT])

                for kt in range(nkt):
                    r0 = s0 + kt * 128
                    par.v = kt % 2
                    p_ = par.v
                    zq, b_zq, kvnT, b_kvnT, Kf, b_Kf, rt, b_rt = zq_[p_], b_zq_[p_], kvnT_[p_], b_kvnT_[p_], Kf_[p_], b_Kf_[p_], rt_[p_], b_rt_[p_]
                    tpb, b_tpb = tpb_[p_], b_tpb_[p_]
                    mbk = mmb_[p_]
                    b_mm = [bb[mbk[0]], bb[mbk[1]]]
                    op("sp", "dma_start", out=zq, in_=zq_src[r0:r0 + 128, :], writes=[b_zq], dma=True)
                    norm_T(zq[:, 384:640], 256, gkv, 0)
                    for n in range(2):
                        for kc in range(2):
                            op("pe", "matmul", banks[mbk[n]], kvnT[:, kc, :], wukv[:, kc, n * 512:(n + 1) * 512],
                               start=(kc == 0), stop=(kc == 1), reads=[b_kvnT, b_wukv], writes=[b_mm[n]])
                    x1_, x2_ = zq[:, 640:656], zq[:, 656:672]
                    cs_, sn_ = rc[:, kt, :], rs[:, kt, :]
                    r1, r2 = rt[0][:, 0, :], rt[1][:, 0, :]
                    op("dve", "tensor_tensor", r1, x1_, cs_, ALU.mult, reads=[b_zq, b_rope], writes=[b_rt])
                    op("dve", "tensor_tensor", r2, x2_, sn_, ALU.mult, reads=[b_zq, b_rope], writes=[b_rt])
                    op("dve", "tensor_tensor", Kf[:, 0, 64:80], r1, r2, ALU.subtract, reads=[b_rt], writes=[b_Kf])
                    op("dve", "tensor_tensor", r1, x1_, sn_, ALU.mult, reads=[b_zq, b_rope, b_Kf], writes=[b_rt])
                    op("dve", "tensor_tensor", r2, x2_, cs_, ALU.mult, reads=[b_zq, b_rope], writes=[b_rt])
                    op("dve", "tensor_tensor", Kf[:, 0, 80:96], r1, r2, ALU.add, reads=[b_rt], writes=[b_Kf])
                    for h in range(1, 8):
                        op("pool", "tensor_copy", Kf[:, h, 64:96], Kf[:, 0, 64:96], reads=[b_Kf], writes=[b_Kf])
                    for n in range(2):
                        bv = banks[mbk[n]].rearrange("p (h d) -> p h d", h=4)
                        op("act", "copy", Kf[:, 4 * n:4 * n + 4, 0:64], bv[:, :, 0:64], reads=[b_mm[n]], writes=[b_Kf])
                        op("dve", "tensor_copy", VA[:, kt, 4 * n:4 * n + 4, 0:64], bv[:, :, 64:128],
                           reads=[b_mm[n]], writes=[b_VA[kt]])
                    op("pool", "memset", VA[:, kt, :, 64:65], 1.0, writes=[b_VA[kt]])
                    for h in range(8):
                        op("pe", "transpose", tpb[0:96, h * 128:(h + 1) * 128], Kf[:, h, :], idb,
                           reads=[b_Kf, b_idb], writes=[b_tpb])
                    op("act", "copy", KT[0:96, :, kt * 128:(kt + 1) * 128],
                       tpb[0:96, :].rearrange("p (h t) -> p h t", h=8), reads=[b_tpb], writes=[b_KT[kt]])

                if bstop == 2:
                    op("dve", "memset", ones, 1.0, reads=b_KT[0:nkt] + b_VA[0:nkt], writes=[b_ones])
                    op("pool", "dma_start", out=x2_dst[0:128, 0:128], in_=ones, reads=[b_ones], dma=True, final=True)
                    return
                for qb in range(L // 512):
                    for sub in range(4):
                        tix = qb * 4 + sub
                        r0 = s0 + tix * 128
                        par.v = tix % 2
                        p_ = par.v
                        zq, b_zq, kvnT, b_kvnT, Qf, b_Qf, rt, b_rt = zq_[p_], b_zq_[p_], kvnT_[p_], b_kvnT_[p_], Kf_[p_], b_Kf_[p_], rt_[p_], b_rt_[p_]
                        tpb, b_tpb = tpb_[p_], b_tpb_[p_]
                        qs, b_qs = qs_[p_], b_qs_[p_]
                        mbk = mmb_[p_]
                        b_mm = [bb[mbk[0]], bb[mbk[1]]]
                        mm2 = pall[:, mbk[0] * 512:mbk[0] * 512 + 1024]
                        op("sp", "dma_start", out=zq, in_=zq_src[r0:r0 + 128, :], writes=[b_zq], dma=True)
                        norm_T(zq[:, 0:384], 384, gq, 1)
                        for (n0, nn) in ((0, 512), (512, 256)):
                            n = n0 // 512
                            for kc in range(3):
                                op("pe", "matmul", mm2[:, n0:n0 + nn], kvnT[:, kc, :], wuq[:, kc, n0:n0 + nn],
                                   start=(kc == 0), stop=(kc == 2), reads=[b_kvnT, b_wuq], writes=[b_mm[n]])
                        op("dve", "tensor_copy", qs[:, 0:512], banks[mbk[0]], reads=[b_mm[0]], writes=[b_qs])
                        op("dve", "tensor_copy", qs[:, 512:768], banks[mbk[1]][:, 0:256], reads=[b_mm[1]], writes=[b_qs])
                        qv = qs.rearrange("p (h d) -> p h d", h=8)
                        rdq = [b_qs]
                        op("dve", "tensor_scalar", Qf[:, :, 0:64], qv[:, :, 0:64], QSCALE, None, ALU.mult,
                           reads=rdq, writes=[b_Qf])
                        cq, sq_ = bc8(rcq[:, tix, :]), bc8(rsq[:, tix, :])
                        op("dve", "tensor_tensor", rt[0], qv[:, :, 64:80], cq, ALU.mult, reads=rdq + [b_rope], writes=[b_rt])
                        op("dve", "tensor_tensor", rt[1], qv[:, :, 80:96], sq_, ALU.mult, reads=rdq + [b_rope], writes=[b_rt])
                        op("dve", "tensor_tensor", Qf[:, :, 64:80], rt[0], rt[1], ALU.subtract, reads=[b_rt], writes=[b_Qf])
                        op("dve", "tensor_tensor", rt[0], qv[:, :, 64:80], sq_, ALU.mult, reads=rdq + [b_rope, b_Qf], writes=[b_rt])
                        op("dve", "tensor_tensor", rt[1], qv[:, :, 80:96], cq, ALU.mult, reads=rdq + [b_rope], writes=[b_rt])
                        op("dve", "tensor_tensor", Qf[:, :, 80:96], rt[0], rt[1], ALU.add, reads=[b_rt], writes=[b_Qf])
                        for h in range(8):
                            op("pe", "transpose", tpb[0:96, h * 128:(h + 1) * 128], Qf[:, h, :], idb,
                               reads=[b_Qf, b_idb], writes=[b_tpb])
                        op("dve", "tensor_copy", QT[0:96, :, sub * 128:(sub + 1) * 128],
                           tpb[0:96, :].rearrange("p (h t) -> p h t", h=8), reads=[b_tpb], writes=[b_QT])
                    if bstop == 3:
                        op("dve", "memset", ones, 1.0, reads=[b_QT], writes=[b_ones])
                        op("pool", "dma_start", out=x2_dst[0:128, 0:128], in_=ones, reads=[b_ones], dma=True, final=True)
                        return
                    items = [(h, kt) for h in range(8) for kt in range(nkt)]
                    sbank = (3, 4, 7)

                    def emit_S(i):
                        h, kt = items[i]
                        op("pe", "matmul", banks[sbank[i % 3]], KT[0:96, h, kt * 128:(kt + 1) * 128], QT[0:96, h, :],
                           start=True, stop=True, reads=[b_KT[kt], b_QT], writes=[b_S[i % 3]])

                    emit_S(0)
                    if len(items) > 1:
                        emit_S(1)
                    for i, (h, kt) in enumerate(items):
                        if i + 2 < len(items):
                            emit_S(i + 2)
                        Ob = banks[5 + h % 2]
                        bO = b_O[h % 2]
                        z_ = i % 3
                        op("act", "activation", PTb[z_], banks[sbank[z_]], AF.Exp, reads=[b_S[z_]], writes=[b_PTb[z_]])
                        for sub in range(4):
                            op("pe", "matmul", Ob[:, sub * 65:(sub + 1) * 65], PTb[z_][:, sub * 128:(sub + 1) * 128],
                               VA[:, kt, h, :], start=(kt == 0 and sub == 0), stop=(kt == nkt - 1),
                               skip_group_check=True, reads=[b_PTb[z_], b_VA[kt]], writes=[bO])
                        if kt == nkt - 1:
                            Ov = Ob[:, 0:260].rearrange("p (s d) -> p s d", s=4)
                            op("dve", "reciprocal", rec, Ov[:, :, 64], reads=[bO], writes=[b_rec])
                            for sub in range(4):
                                op("dve", "tensor_scalar", att[:, sub, h * 64:(h + 1) * 64], Ov[:, sub, 0:64],
                                   rec[:, sub:sub + 1], None, ALU.mult, reads=[bO, b_rec], writes=[b_att[sub]])
                    if bstop == 4:
                        op("dve", "memset", ones, 1.0, reads=b_att, writes=[b_ones])
                        op("pool", "dma_start", out=x2_dst[0:128, 0:128], in_=ones, reads=[b_ones], dma=True, final=True)
                        return
                    tpb, b_tpb = tpb_[0], b_tpb_[0]
                    b_mm = [bb[1], bb[2]]
                    for sub in range(4):
                        tix = qb * 4 + sub
                        r0 = s0 + tix * 128
                        op("act", "activation", junk[:, 0:512], att[:, sub, :], AF.Square, accum_out=st2[:, 2:3],
                           reads=[b_att[sub]], writes=[b_junk, b_st[2]])
                        rstd_col(2, 512)
                        op("dve", "scalar_tensor_tensor", yat, att[:, sub, :], st2[:, 2:3], gatt, ALU.mult, ALU.mult,
                           reads=[b_att[sub], b_st[2], b_gb], writes=[b_yat])
                        for k in range(4):
                            op("pe", "transpose", tpb[:, k * 128:(k + 1) * 128], yat[:, k * 128:(k + 1) * 128], idb,
                               reads=[b_yat, b_idb], writes=[b_tpb])
                        op("act", "copy", ymT[:, 4:8, :], tpb[:, 0:512].rearrange("p (a b) -> p a b", a=4),
                           reads=[b_tpb], writes=[b_ymT])
                        op("sp", "dma_start", out=ymT[:, 0:4, :],
                           in_=ys_d[:, r0:r0 + 128].rearrange("(c p) t -> p c t", p=128),
                           reads=[b_ysd], writes=[b_ymT], dma=True)
                        op("sp", "dma_start", out=x1t, in_=x1_src[r0:r0 + 128, :], writes=[b_x1t], dma=True)
                        for n in range(2):
                            for k in range(8):
                                op("pe", "matmul", banks[1 + n], ymT[:, k, :], wout[:, k, n * 512:(n + 1) * 512],
                                   start=(k == 0), stop=(k == 7), reads=[b_ymT, b_wout], writes=[b_mm[n]])
                        for n in range(2):
                            op("act", "activation", junk[:, 0:512], banks[1 + n], AF.Square,
                               accum_out=st2[:, 3 + n:4 + n], reads=[b_mm[n]], writes=[b_junk, b_st[3 + n]])
                        op("dve", "tensor_tensor", st2[:, 3:4], st2[:, 3:4], st2[:, 4:5], ALU.add,
                           reads=[b_st[3], b_st[4]], writes=[b_st[3]])
                        rstd_col(3, D)
                        for n in range(2):
                            op("dve", "scalar_tensor_tensor", tt2[n], banks[1 + n], st2[:, 3:4],
                               gmp[:, n * 512:(n + 1) * 512], ALU.mult, ALU.mult,
                               reads=[b_mm[n], b_st[3], b_gb], writes=[b_tt2[n]])
                            op("pool", "tensor_tensor", x1t[:, n * 512:(n + 1) * 512], tt2[n],
                               x1t[:, n * 512:(n + 1) * 512], ALU.add, reads=[b_tt2[n], b_x1t], writes=[b_x1t])
                        op("pool", "dma_start", out=x2_dst[r0:r0 + 128, :], in_=x1t, reads=[b_x1t],
                           writes=[b_x2d], dma=True, final=(stage == "B"))

        if stage == "A":
            ffn_phase(1, x_in, x1_d, True, ntiles=2)
        elif stage == "B":
            mixer_phase([(0, LB)])
        elif stage == "C":
            ffn_phase(2, x_in, y_out, False, ntiles=3)
        else:
            ffn_phase(1, x_in, x1_d, True)
            S.barrier()
            mixer_phase([(0, LP), (LP, LS)])
            S.barrier()
            ffn_phase(2, x2_d, y_out, False)
        S.emit(st)
    return nc


_NC_CACHE = {}


def kernel(**inputs):
    xp = np.asarray(inputs["x_prompt"], dtype=np.float32)
    xs = np.asarray(inputs["x_sample"], dtype=np.float32)
    if "nc" not in _NC_CACHE:
        _NC_CACHE["nc"] = build("full")
    nc = _NC_CACHE["nc"]
    shared = {}
    shared.update(ffn_inputs(inputs))
    shared.update(mixer_inputs(inputs))
    in_maps = []
    for b in range(8):
        m = dict(shared)
        m["x"] = np.ascontiguousarray(np.concatenate([xp[b], xs[b]], axis=0))
        in_maps.append(m)
    res = run_bass_kernel_spmd(nc, in_maps, core_ids=list(range(8)))
    y_prompt = np.stack([res.results[b]["y"][:LP] for b in range(8)], axis=0).astype(np.float32)
    y_sample = np.stack([res.results[b]["y"][LP:] for b in range(8)], axis=0).astype(np.float32)
    return (y_prompt, y_sample)


def ffn_inputs(d):
    m = {}
    _names = ("g_ffn1_pre", "w_ffn1_gate", "w_ffn1_up", "w_ffn1_down", "g_ffn1_post",
              "g_ffn2_pre", "w_ffn2_gate", "w_ffn2_up", "w_ffn2_down", "g_ffn2_post", "g_mix_pre", "w_in")
    assert all(n in d for n in _names)
    for f in (1, 2):
        m["g_ffn%d_pre" % f] = np.ascontiguousarray(d["g_ffn%d_pre" % f], dtype=np.float32).reshape(1, -1)
        m["w_ffn%d_gate" % f] = np.ascontiguousarray(d["w_ffn%d_gate" % f][0])
        m["w_ffn%d_up" % f] = np.ascontiguousarray(d["w_ffn%d_up" % f][0])
        m["w_ffn%d_down" % f] = np.ascontiguousarray(d["w_ffn%d_down" % f][0])
        m["g_ffn%d_post" % f] = np.ascontiguousarray(d["g_ffn%d_post" % f]).reshape(1, -1)
    m["g_mix_pre"] = np.ascontiguousarray(d["g_mix_pre"]).reshape(1, -1)
    m["w_in"] = np.ascontiguousarray(d["w_in"][0])
    return m


def mixer_inputs(d):
    m = {}
    for nm in ("w_glu", "w_uq", "w_ukv", "w_out"):
        m[nm] = np.ascontiguousarray(d[nm][0])
    for nm in ("d_skip", "b_glu", "g_ssm_out", "g_q", "g_kv", "g_att_out", "g_mix_post"):
        m[nm] = np.ascontiguousarray(d[nm]).reshape(1, -1)
    for dn in ("fwd", "bwd"):
        for nm in ("lam_re_", "lam_im_", "log_dt_", "b_re_", "b_im_", "c_re_", "c_im_"):
            m[nm + dn] = np.ascontiguousarray(d[nm + dn]).reshape(-1)
    return m
```

```python
import math
from contextlib import ExitStack

import numpy as np
import concourse.bass as bass
import concourse.mybir as mybir
from concourse.bass_utils import run_bass_kernel_spmd

F32 = mybir.dt.float32
BF16 = mybir.dt.bfloat16
I32 = mybir.dt.int32
ALU = mybir.AluOpType
AF = mybir.ActivationFunctionType
AX = mybir.AxisListType

D = 1024
DFF = 2816
NFF = DFF // 128
DIN = 1184
LP = 4096
LS = 2048
NTOK = LP + LS
EPS = 1e-6
TWO_PI = 2.0 * math.pi


class Buf:
    __slots__ = ("name", "last_w", "readers")

    def __init__(self, name):
        self.name = name
        self.last_w = None
        self.readers = []


class Op:
    __slots__ = ("eng", "name", "args", "kw", "dma", "lane", "laneval", "deps", "sig", "sigval")


class Sched:
    ENGS = ("pe", "act", "dve", "pool", "sp")

    def __init__(self, nc, lanes_per_queue=8):
        self.nc = nc
        self.ops = {e: [] for e in self.ENGS}
        self.nl = lanes_per_queue
        self.lane_rr = {e: 0 for e in self.ENGS}
        self.lane_last = {}
        self.lane_cnt = {}
        self.out_dmas = []

    def op(self, eng, name, *args, reads=(), writes=(), dma=False, final=False, **kw):
        o = Op()
        o.eng = eng
        o.name = name
        o.args = args
        o.kw = kw
        o.dma = dma
        o.sig = False
        o.sigval = 0
        o.deps = []
        o.lane = None
        o.laneval = 0
        deps = o.deps
        for b in reads:
            w = b.last_w
            if w is not None and (w.dma or not (w.eng == eng == "pe")):
                deps.append(w)
        for b in writes:
            w = b.last_w
            if w is not None and (w.dma or not (w.eng == eng == "pe")):
                deps.append(w)
            for r in b.readers:
                if r.dma or not (r.eng == eng == "pe"):
                    deps.append(r)
        if dma:
            l = self.lane_rr[eng]
            self.lane_rr[eng] = (l + 1) % self.nl
            key = (eng, l)
            prev = self.lane_last.get(key)
            if prev is not None:
                deps.append(prev)
            self.lane_last[key] = o
            c = self.lane_cnt.get(key, 0) + 1
            self.lane_cnt[key] = c
            o.lane = key
            o.laneval = 16 * c
            o.sig = True
            if final:
                self.out_dmas.append(o)
        for d in deps:
            d.sig = True
        for b in reads:
            b.readers.append(o)
        for b in writes:
            b.last_w = o
            b.readers = []
        self.ops[eng].append(o)
        return o

    def barrier(self):
        lasts = []
        for e in self.ENGS:
            for o in reversed(self.ops[e]):
                if not o.dma:
                    lasts.append(o)
                    break
        lasts += list(self.lane_last.values())
        for e in self.ENGS:
            o = Op()
            o.eng = e
            o.name = "nop"
            o.args = ()
            o.kw = {}
            o.dma = False
            o.sig = False
            o.sigval = 0
            o.lane = None
            o.laneval = 0
            o.deps = list(lasts)
            for d in o.deps:
                d.sig = True
            self.ops[e].append(o)

    def emit(self, stack):
        nc = self.nc
        sems = {e: stack.enter_context(nc.semaphore("s_" + e)) for e in self.ENGS}
        lsems = {}
        for key in self.lane_cnt:
            lsems[key] = stack.enter_context(nc.semaphore("l_%s%d" % key))
        for e in self.ENGS:
            c = 0
            for o in self.ops[e]:
                if not o.dma and o.sig:
                    c += 1
                    o.sigval = c
        fw = {}
        for o in self.out_dmas:
            if o.lane not in fw or fw[o.lane] < o.laneval:
                fw[o.lane] = o.laneval
        final_waits = [(lsems[k], v) for k, v in fw.items()]

        def replay(e, h):
            waited = {}
            for o in self.ops[e]:
                for d in o.deps:
                    if d.dma:
                        s, v = lsems[d.lane], d.laneval
                        k = ("l",) + d.lane
                    else:
                        s, v = sems[d.eng], d.sigval
                        k = d.eng
                    if waited.get(k, 0) >= v:
                        continue
                    waited[k] = v
                    h.wait_ge(s, v)
                ins = getattr(h, o.name)(*o.args, **o.kw)
                if o.dma:
                    ins.then_inc(lsems[o.lane], 16)
                elif o.sig:
                    ins.then_inc(sems[e], 1)
            if e == "sp":
                for s, v in final_waits:
                    h.wait_ge(s, v)

        with nc.Block() as block:
            @block.tensor
            def _(h):
                replay("pe", h)

            @block.scalar
            def _(h):
                replay("act", h)

            @block.vector
            def _(h):
                replay("dve", h)

            @block.gpsimd
            def _(h):
                replay("pool", h)

            @block.sync
            def _(h):
                replay("sp", h)


class Arena:
    def __init__(self, base_ap, nwords):
        self.base = base_ap
        self.n = nwords
        self.off = 0
        self.maxoff = 0

    def alloc(self, free_shape, dt):
        n = 1
        for s in free_shape:
            n *= s
        nw = n if dt in (F32, I32) else (n + 1) // 2
        assert self.off + nw <= self.n, ("arena overflow", self.off, nw, self.n)
        a = self.base[:, self.off:self.off + nw]
        self.off += nw
        self.maxoff = max(self.maxoff, self.off)
        if dt != F32:
            a = a.bitcast(dt)
            if dt == BF16 and n != 2 * nw:
                a = a[:, 0:n]
        if len(free_shape) == 2:
            a = a.rearrange("p (a b) -> p a b", a=free_shape[0])
        elif len(free_shape) == 3:
            a = a.rearrange("p (a b c) -> p a b c", a=free_shape[0], b=free_shape[1])
        return a


def build(stage="full", LB=1024, bstop=9):
    nc = bass.Bass("TRN2", target_bir_lowering=False)

    def din(name, shape):
        return nc.dram_tensor(name, list(shape), F32, kind="ExternalInput").ap()

    x_in = din("x", [NTOK, D]) if stage != "B" else None
    W = {}
    for f in ((1, 2) if stage != "B" else ()):
        W["g_pre%d" % f] = din("g_ffn%d_pre" % f, [1, D])
        W["wg%d" % f] = din("w_ffn%d_gate" % f, [D, DFF])
        W["wu%d" % f] = din("w_ffn%d_up" % f, [D, DFF])
        W["wd%d" % f] = din("w_ffn%d_down" % f, [DFF, D])
        W["g_post%d" % f] = din("g_ffn%d_post" % f, [1, D])
    if stage != "B":
        W["g_mix_pre"] = din("g_mix_pre", [1, D])
        W["w_in"] = din("w_in", [D, DIN])
    y_out = nc.dram_tensor("y", [NTOK, D], F32, kind="ExternalOutput").ap()
    x1_d = nc.dram_tensor("x1_scr", [NTOK, D], F32, kind="Internal").ap()
    x2_d = nc.dram_tensor("x2_scr", [NTOK, D], F32, kind="Internal").ap()
    zq_d = nc.dram_tensor("zq_scr", [NTOK, 672], F32, kind="Internal").ap()
    uT_d = nc.dram_tensor("uT_scr", [512, NTOK], BF16, kind="Internal").ap()
    M = {}
    for nm, shp in (("w_glu", [512, 512]), ("w_uq", [384, 768]), ("w_ukv", [256, 1024]), ("w_out", [1024, 1024]),
                    ("d_skip", [1, 512]), ("b_glu", [1, 512]), ("g_ssm_out", [1, 512]), ("g_q", [1, 384]),
                    ("g_kv", [1, 256]), ("g_att_out", [1, 512]), ("g_mix_post", [1, 1024])):
        M[nm] = din(nm, shp)
    for dn in ("fwd", "bwd"):
        M["lam_re_" + dn] = din("lam_re_" + dn, [2048])
        M["lam_im_" + dn] = din("lam_im_" + dn, [2048])
        M["log_dt_" + dn] = din("log_dt_" + dn, [32])
        for pn in ("re", "im"):
            M["b_%s_%s" % (pn, dn)] = din("b_%s_%s" % (pn, dn), [32768])
            M["c_%s_%s" % (pn, dn)] = din("c_%s_%s" % (pn, dn), [32768])
    ys_d = nc.dram_tensor("ys_scr", [512, NTOK], BF16, kind="Internal").ap()
    b_ysd = Buf("ysd")
    b_x2d = Buf("x2d")
    if stage == "B":
        zq_src = din("t_zq", [LB, 672])
        uT_src = nc.dram_tensor("t_uT", [512, LB], BF16, kind="ExternalInput").ap()
        x1_src = din("t_x1", [LB, D])
        x2_dst = nc.dram_tensor("t_x2", [LB, D], F32, kind="ExternalOutput").ap()
    else:
        zq_src, uT_src, x1_src, x2_dst = zq_d, uT_d, x1_d, x2_d
    dbg = {}
    if stage == "A":
        dbg["x1"] = nc.dram_tensor("dbg_x1", [NTOK, D], F32, kind="ExternalOutput").ap()
        dbg["zq"] = nc.dram_tensor("dbg_zq", [NTOK, 672], F32, kind="ExternalOutput").ap()
        dbg["uT"] = nc.dram_tensor("dbg_uT", [512, NTOK], BF16, kind="ExternalOutput").ap()

    S = Sched(nc)
    op = S.op
    with ExitStack() as st:
        NW = 53000 - 1024
        arena_t = st.enter_context(nc.sbuf_tensor("arena", [128, 53000], F32))
        CT_g = arena_t[:, 53000 - 1024:53000].rearrange("p (a b c) -> p a b c", a=4, b=16)
        b_CT_g = Buf("CT")
        pall = st.enter_context(nc.psum_tensor("pall", [128, 4096], F32))
        banks = [pall[:, i * 512:(i + 1) * 512] for i in range(8)]

        ct_pending = []
        for di, dn in enumerate(("fwd", "bwd")):
            for part, pn in enumerate(("re", "im")):
                for gi in range(2):
                    for P in range(16):
                        ct_pending.append((di, dn, part, pn, gi, P))

        def issue_ct_loads(n=None):
            k = len(ct_pending) if n is None else min(n, len(ct_pending))
            for _ in range(k):
                di, dn, part, pn, gi, P = ct_pending.pop(0)
                src = M["c_%s_%s" % (pn, dn)]
                op("sp", "dma_start", out=CT_g[gi * 64:(gi + 1) * 64, di * 2 + part, P, :],
                   in_=bass.AP(src.tensor, gi * 1024 + P * 2048, [[1, 64], [64, 16]]),
                   writes=[b_CT_g], dma=True, allow_slow_non_contiguous=True)

        def ffn_phase(f, src_d, dst_d, with_win, ntiles=NTOK // 256):
            ar = Arena(arena_t[:, :], NW)
            wg = ar.alloc([8, DFF], BF16)
            wu = ar.alloc([8, DFF], BF16)
            wd = ar.alloc([NFF, D], BF16)
            b_wg = [[Buf("wg%d_%d" % (k, h)) for h in range(2)] for k in range(8)]
            b_wu = [[Buf("wu%d_%d" % (k, h)) for h in range(2)] for k in range(8)]
            b_wd = [Buf("wd%d" % k) for k in range(NFF)]
            if with_win:
                win = ar.alloc([8, DIN], BF16)
                b_win = [Buf("win%d" % k) for k in range(8)]
            gpost = ar.alloc([D], F32)
            b_gpost = Buf("gpost")
            gpreT = ar.alloc([8], F32)
            b_gpreT = Buf("gpreT")
            gmixT = ar.alloc([8], F32)
            b_gmixT = Buf("gmixT")
            idf = ar.alloc([128], F32)
            b_idf = Buf("idf")
            NPAR = 1 if with_win else 2
            xts = [[ar.alloc([D], F32) for _ in range(2)] for _ in range(NPAR)]
            b_xts = [[Buf("xt%d_%d" % (p, i)) for i in range(2)] for p in range(NPAR)]
            xt, b_xt = xts[0], b_xts[0]
            xn = ar.alloc([D], F32)
            b_xn = Buf("xn")
            hTs = [ar.alloc([8, 256], BF16) for _ in range(NPAR)]
            b_hTs = [[[Buf("hT%d_%d_%d" % (p, s, k)) for k in range(8)] for s in range(2)] for p in range(NPAR)]
            hT, b_hT = hTs[0], b_hTs[0]
            actT = ar.alloc([NFF, 256], BF16)
            b_actT = [Buf("actT%d" % c) for c in range(NFF)]
            sg = [ar.alloc([256], F32) for _ in range(2)]
            b_sg = [Buf("sg%d" % i) for i in range(2)]
            tt = [ar.alloc([512], F32) for _ in range(2)]
            b_tt = [Buf("tt%d" % i) for i in range(2)]
            junk = ar.alloc([D], BF16)
            b_junk = Buf("junkA")
            st_ss = ar.alloc([8], F32)
            b_ss = [Buf("ss%d" % i) for i in range(8)]
            if with_win:
                zqt = ar.alloc([672], F32)
                b_zqt = Buf("zqt")
                uTt = ar.alloc([4, 256], BF16)
                b_uTt = Buf("uTt")

            HC = DFF // 2
            for hf in range(2):
                for k in range(8):
                    op("pool", "dma_start", out=wg[:, k, hf * HC:(hf + 1) * HC],
                       in_=W["wg%d" % f][k * 128:(k + 1) * 128, hf * HC:(hf + 1) * HC], writes=[b_wg[k][hf]], dma=True)
                    op("pool", "dma_start", out=wu[:, k, hf * HC:(hf + 1) * HC],
                       in_=W["wu%d" % f][k * 128:(k + 1) * 128, hf * HC:(hf + 1) * HC], writes=[b_wu[k][hf]], dma=True)
            for c in range(NFF):
                op("pool", "dma_start", out=wd[:, c, :], in_=W["wd%d" % f][c * 128:(c + 1) * 128, :],
                   writes=[b_wd[c]], dma=True)
            if with_win:
                for k in range(8):
                    op("pool", "dma_start", out=win[:, k, :], in_=W["w_in"][k * 128:(k + 1) * 128, :],
                       writes=[b_win[k]], dma=True)
            op("sp", "dma_start", out=gpost, in_=W["g_post%d" % f].partition_broadcast(128),
               writes=[b_gpost], dma=True)
            op("sp", "dma_start", out=gpreT, in_=W["g_pre%d" % f][0, :].rearrange("(k p) -> p k", p=128),
               writes=[b_gpreT], dma=True, allow_slow_non_contiguous=True)
            if with_win:
                op("sp", "dma_start", out=gmixT, in_=W["g_mix_pre"][0, :].rearrange("(k p) -> p k", p=128),
                   writes=[b_gmixT], dma=True, allow_slow_non_contiguous=True)
            idi = ar.alloc([128], I32)
            op("pool", "iota", idi, [[1, 128]], base=0, channel_multiplier=-1, writes=[b_idf])
            op("dve", "tensor_copy", idf, idi, reads=[b_idf], writes=[b_idf])
            op("dve", "tensor_scalar", idf, idf, 0.0, None, ALU.is_equal, reads=[b_idf], writes=[b_idf])

            def rstd_from_ss(col, n_feat, extra_scale=1.0):
                c = st_ss[:, col:col + 1]
                op("dve", "tensor_scalar", c, c, 1.0 / n_feat, EPS, ALU.mult, ALU.add,
                   reads=[b_ss[col]], writes=[b_ss[col]])
                op("act", "sqrt", c, c, reads=[b_ss[col]], writes=[b_ss[col]])
                op("dve", "reciprocal", c, c, reads=[b_ss[col]], writes=[b_ss[col]])
                if extra_scale != 1.0:
                    op("dve", "tensor_scalar", c, c, extra_scale, None, ALU.mult,
                       reads=[b_ss[col]], writes=[b_ss[col]])

            def norm_transpose(s, gT, b_gT, sscol):
                op("act", "activation", junk, xt[s], AF.Square, accum_out=st_ss[:, sscol:sscol + 1],
                   reads=[b_xt[s]], writes=[b_junk, b_ss[sscol]])
                rstd_from_ss(sscol, D)
                op("dve", "tensor_scalar", xn, xt[s], st_ss[:, sscol:sscol + 1], None, ALU.mult,
                   reads=[b_xt[s], b_ss[sscol]], writes=[b_xn])
                for k in range(8):
                    bk = k // 4
                    sl = (k % 4) * 128
                    op("pe", "transpose", banks[bk][:, sl:sl + 128], xn[:, k * 128:(k + 1) * 128], idf,
                       reads=[b_xn, b_idf], writes=[b_tp[bk]])
                for k in range(8):
                    bk = k // 4
                    sl = (k % 4) * 128
                    if bk == 0:
                        op("act", "activation", hT[:, k, s * 128:(s + 1) * 128], banks[bk][:, sl:sl + 128],
                           AF.Copy, scale=gT[:, k:k + 1], reads=[b_tp[bk], b_gT], writes=[b_hT[s][k]])
                    else:
                        op("dve", "tensor_scalar", hT[:, k, s * 128:(s + 1) * 128], banks[bk][:, sl:sl + 128],
                           gT[:, k:k + 1], None, ALU.mult, reads=[b_tp[bk], b_gT], writes=[b_hT[s][k]])

            b_tp = [Buf("tp%d" % k) for k in range(2)]
            b_pg = [Buf("pg0"), Buf("pg1")]
            b_pd = [[Buf("pd%d%d" % (s, n)) for n in range(2)] for s in range(2)]

            def load_and_norm(ti):
                r0_ = ti * 256
                for s in range(2):
                    op("sp", "dma_start", out=xt[s], in_=src_d[r0_ + s * 128:r0_ + (s + 1) * 128, :],
                       writes=[b_xt[s]], dma=True)
                if with_win and stage == "full" and ti >= 1:
                    issue_ct_loads(8)
                for s in range(2):
                    norm_transpose(s, gpreT, b_gpreT, s)

            for ti in range(ntiles):
                r0 = ti * 256
                par_ = ti % NPAR
                xt, b_xt, hT, b_hT = xts[par_], b_xts[par_], hTs[par_], b_hTs[par_]
                if NPAR == 1 or ti == 0:
                    load_and_norm(ti)
                hreads = lambda k: [b_hT[0][k], b_hT[1][k]]
                for c in range(NFF):
                    pg = banks[2 + (c % 2)]
                    bpg = b_pg[c % 2]
                    for k in range(8):
                        op("pe", "matmul", pg[:, 0:256], wg[:, k, c * 128:(c + 1) * 128], hT[:, k, :],
                           start=(k == 0), stop=(k == 7), reads=[b_wg[k][c // 11]] + hreads(k), writes=[bpg])
                    for k in range(8):
                        op("pe", "matmul", pg[:, 256:512], wu[:, k, c * 128:(c + 1) * 128], hT[:, k, :],
                           start=(k == 0), stop=(k == 7), reads=[b_wu[k][c // 11]] + hreads(k), writes=[bpg])
                    op("act", "activation", sg[c % 2], pg[:, 0:256], AF.Silu, reads=[bpg], writes=[b_sg[c % 2]])
                    op("dve", "tensor_tensor", actT[:, c, :], sg[c % 2], pg[:, 256:512], ALU.mult,
                       reads=[b_sg[c % 2], bpg], writes=[b_actT[c]])
                if NPAR == 2 and ti + 1 < ntiles:
                    pn_ = (ti + 1) % 2
                    xt, b_xt, hT, b_hT = xts[pn_], b_xts[pn_], hTs[pn_], b_hTs[pn_]
                    load_and_norm(ti + 1)
                    xt, b_xt, hT, b_hT = xts[par_], b_xts[par_], hTs[par_], b_hTs[par_]
                for s in range(2):
                    for n in range(2):
                        pd = banks[4 + 2 * s + n]
                        for c in range(NFF):
                            op("pe", "matmul", pd, actT[:, c, s * 128:(s + 1) * 128],
                               wd[:, c, n * 512:(n + 1) * 512], start=(c == 0), stop=(c == NFF - 1),
                               reads=[b_actT[c], b_wd[c]], writes=[b_pd[s][n]])
                for s in range(2):
                    for n in range(2):
                        pd = banks[4 + 2 * s + n]
                        col = 2 + 2 * s + n
                        op("act", "activation", junk[:, 0:512], pd, AF.Square,
                           accum_out=st_ss[:, col:col + 1], reads=[b_pd[s][n]], writes=[b_junk, b_ss[col]])
                    c0 = 2 + 2 * s
                    op("dve", "tensor_tensor", st_ss[:, c0:c0 + 1], st_ss[:, c0:c0 + 1], st_ss[:, c0 + 1:c0 + 2],
                       ALU.add, reads=[b_ss[c0], b_ss[c0 + 1]], writes=[b_ss[c0]])
                    rstd_from_ss(c0, D, extra_scale=0.5)
                    for n in range(2):
                        pd = banks[4 + 2 * s + n]
                        op("dve", "scalar_tensor_tensor", tt[n], pd, st_ss[:, c0:c0 + 1],
                           gpost[:, n * 512:(n + 1) * 512], ALU.mult, ALU.mult,
                           reads=[b_pd[s][n], b_ss[c0], b_gpost], writes=[b_tt[n]])
                        op("pool", "tensor_tensor", xt[s][:, n * 512:(n + 1) * 512], tt[n],
                           xt[s][:, n * 512:(n + 1) * 512], ALU.add, reads=[b_tt[n], b_xt[s]], writes=[b_xt[s]])
                    op("pool", "dma_start", out=dst_d[r0 + s * 128:r0 + (s + 1) * 128, :], in_=xt[s],
                       reads=[b_xt[s]], dma=True, final=(not with_win))
                    if stage == "A" and with_win:
                        op("pool", "dma_start", out=dbg["x1"][r0 + s * 128:r0 + (s + 1) * 128, :], in_=xt[s],
                           reads=[b_xt[s]], dma=True, final=True)
                if not with_win:
                    continue
                for s in range(2):
                    norm_transpose(s, gmixT, b_gmixT, 6 + s)
                for uc in range(4):
                    pu = banks[2 + (uc % 2)]
                    bpu = b_pg[uc % 2]
                    for k in range(8):
                        op("pe", "matmul", pu[:, 0:256], win[:, k, uc * 128:(uc + 1) * 128], hT[:, k, :],
                           start=(k == 0), stop=(k == 7), reads=[b_win[k]] + hreads(k), writes=[bpu])
                    op("act", "copy", uTt[:, uc, :], pu[:, 0:256], reads=[bpu], writes=[b_uTt])
                op("pool", "dma_start", out=uT_d[:, r0:r0 + 256].rearrange("(c p) t -> p c t", p=128), in_=uTt,
                   reads=[b_uTt], dma=True)
                if stage == "A":
                    op("pool", "dma_start", out=dbg["uT"][:, r0:r0 + 256].rearrange("(c p) t -> p c t", p=128),
                       in_=uTt, reads=[b_uTt], dma=True, final=True)
                for s in range(2):
                    for (n0, nn, bi) in ((512, 512, 4 + 2 * s), (1024, 160, 5 + 2 * s)):
                        pz = banks[bi]
                        bpz = b_pd[s][bi - 4 - 2 * s]
                        for k in range(8):
                            op("pe", "matmul", pz[:, 0:nn], hT[:, k, s * 128:(s + 1) * 128], win[:, k, n0:n0 + nn],
                               start=(k == 0), stop=(k == 7), reads=[b_win[k], b_hT[s][k]], writes=[bpz])
                        if nn == 512:
                            op("act", "copy", zqt[:, 0:512], pz[:, 0:512], reads=[bpz], writes=[b_zqt])
                        else:
                            op("dve", "tensor_copy", zqt[:, 512:672], pz[:, 0:160], reads=[bpz], writes=[b_zqt])
                    op("pool", "dma_start", out=zq_d[r0 + s * 128:r0 + (s + 1) * 128, :], in_=zqt,
                       reads=[b_zqt], dma=True)
                    if stage == "A":
                        op("pool", "dma_start", out=dbg["zq"][r0 + s * 128:r0 + (s + 1) * 128, :], in_=zqt,
                           reads=[b_zqt], dma=True, final=True)

        def dap(ap, offset, dims):
            return bass.AP(ap.tensor, offset, [list(d) for d in dims])

        def mixer_phase(seqs):
            ar = Arena(arena_t[:, :], NW)
            QSCALE = 96.0 ** -0.5
            idf = ar.alloc([128], F32); b_idf = Buf("idfB")
            idi = ar.alloc([128], I32)
            idb = ar.alloc([128], BF16); b_idb = Buf("idb")
            ones = ar.alloc([128], F32); b_ones = Buf("ones")
            op("pool", "iota", idi, [[1, 128]], base=0, channel_multiplier=-1, writes=[b_idf])
            op("dve", "tensor_copy", idf, idi, reads=[b_idf], writes=[b_idf])
            op("dve", "tensor_scalar", idf, idf, 0.0, None, ALU.is_equal, reads=[b_idf], writes=[b_idf])
            op("dve", "tensor_copy", idb, idf, reads=[b_idf], writes=[b_idb])
            op("dve", "memset", ones, 1.0, writes=[b_ones])

            wglu = ar.alloc([4, 512], BF16); b_wglu = Buf("wglu")
            for k in range(4):
                op("pool", "dma_start", out=wglu[:, k, :], in_=M["w_glu"][k * 128:(k + 1) * 128, :],
                   writes=[b_wglu], dma=True)
            smallT = ar.alloc([3, 4], F32); b_smallT = Buf("smallT")
            for i, nm in enumerate(("d_skip", "b_glu", "g_ssm_out")):
                op("sp", "dma_start", out=smallT[:, i, :], in_=M[nm][0, :].rearrange("(k p) -> p k", p=128),
                   writes=[b_smallT], dma=True, allow_slow_non_contiguous=True)
            dskipT, bgluT, gssmT = smallT[:, 0, :], smallT[:, 1, :], smallT[:, 2, :]

            mark_params = ar.off
            NPT = 72
            PT = ar.alloc([NPT, 32], F32)
            b_PT = [Buf("PT%d" % i) for i in range(NPT)]
            pi_ = [0]

            def newp():
                i = pi_[0]; pi_[0] += 1
                return i
            def P_(i):
                return PT[:, i, :]
            LR, LI, LD = newp(), newp(), newp()
            for di, dn in enumerate(("fwd", "bwd")):
                op("sp", "dma_start", out=PT[:, LR, di * 16:(di + 1) * 16],
                   in_=M["lam_re_" + dn].rearrange("(P q) -> q P", q=128), writes=[b_PT[LR]], dma=True,
                   allow_slow_non_contiguous=True)
                op("sp", "dma_start", out=PT[:, LI, di * 16:(di + 1) * 16],
                   in_=M["lam_im_" + dn].rearrange("(P q) -> q P", q=128), writes=[b_PT[LI]], dma=True,
                   allow_slow_non_contiguous=True)
                for gi in range(2):
                    op("sp", "dma_start", out=PT[gi * 64:(gi + 1) * 64, LD, di * 16:(di + 1) * 16],
                       in_=dap(M["log_dt_" + dn], gi, [[0, 64], [2, 16]]), writes=[b_PT[LD]], dma=True,
                       allow_slow_non_contiguous=True)

            def tt_(o, a, b, alu):
                op("dve", "tensor_tensor", P_(o), P_(a), P_(b), alu, reads=[b_PT[a], b_PT[b]], writes=[b_PT[o]])
            def ts_(o, a, s1, s2, o0, o1=None):
                if o1 is None:
                    op("dve", "tensor_scalar", P_(o), P_(a), s1, None, o0, reads=[b_PT[a]], writes=[b_PT[o]])
                else:
                    op("dve", "tensor_scalar", P_(o), P_(a), s1, s2, o0, o1, reads=[b_PT[a]], writes=[b_PT[o]])
            def act_(o, a, fn, **kw):
                op("act", "activation", P_(o), P_(a), fn, reads=[b_PT[a]], writes=[b_PT[o]], **kw)

            KI = ar.alloc([32], I32); b_KI = Buf("KI")
            def sin_of(o, a, shift):
                m, kf = newp(), newp()
                ts_(m, a, shift, None, ALU.add)
                ts_(kf, m, 1.0 / TWO_PI, None, ALU.mult)
                op("dve", "tensor_copy", KI, P_(kf), reads=[b_PT[kf]], writes=[b_KI])
                op("dve", "tensor_copy", P_(kf), KI, reads=[b_KI], writes=[b_PT[kf]])
                op("dve", "scalar_tensor_tensor", P_(m), P_(kf), -TWO_PI, P_(m), ALU.mult, ALU.add,
                   reads=[b_PT[kf], b_PT[m]], writes=[b_PT[m]])
                ts_(kf, m, math.pi, -TWO_PI, ALU.is_gt, ALU.mult)
                tt_(m, m, kf, ALU.add)
                ts_(kf, m, -math.pi, TWO_PI, ALU.is_lt, ALU.mult)
                tt_(m, m, kf, ALU.add)
                act_(o, m, AF.Sin)

            DT, AR_, TH, R, SN, CS = newp(), newp(), newp(), newp(), newp(), newp()
            act_(DT, LD, AF.Exp)
            tt_(AR_, LR, DT, ALU.mult)
            tt_(TH, LI, DT, ALU.mult)
            act_(R, AR_, AF.Exp)
            sin_of(SN, TH, 0.0)
            sin_of(CS, TH, math.pi / 2)
            ABR, ABI, DEN, T1, T2, COR, COI = newp(), newp(), newp(), newp(), newp(), newp(), newp()
            tt_(ABR, R, CS, ALU.mult)
            tt_(ABI, R, SN, ALU.mult)
            NRT = newp()
            ts_(NRT, ABR, -1.0, None, ALU.add)
            tt_(DEN, LR, LR, ALU.mult)
            tt_(T1, LI, LI, ALU.mult)
            tt_(DEN, DEN, T1, ALU.add)
            op("dve", "reciprocal", P_(DEN), P_(DEN), reads=[b_PT[DEN]], writes=[b_PT[DEN]])
            tt_(T1, NRT, LR, ALU.mult)
            tt_(T2, ABI, LI, ALU.mult)
            tt_(T1, T1, T2, ALU.add)
            tt_(COR, T1, DEN, ALU.mult)
            tt_(T1, ABI, LR, ALU.mult)
            tt_(T2, NRT, LI, ALU.mult)
            tt_(T1, T1, T2, ALU.subtract)
            tt_(COI, T1, DEN, ALU.mult)
            MC, MS = [CS], [SN]
            for k in range(11):
                c2, s2 = newp(), newp()
                tt_(T1, MC[k], MC[k], ALU.mult)
                tt_(T2, MS[k], MS[k], ALU.mult)
                tt_(c2, T1, T2, ALU.subtract)
                tt_(T1, MC[k], MS[k], ALU.mult)
                ts_(s2, T1, 2.0, None, ALU.mult)
                MC.append(c2); MS.append(s2)
            AW_r, AW_i, NAW_i = [None, ABR], [None, ABI], [None]
            def cmul(ar_, ai_, br_, bi_):
                o_r, o_i = newp(), newp()
                tt_(T1, ar_, br_, ALU.mult)
                tt_(T2, ai_, bi_, ALU.mult)
                tt_(o_r, T1, T2, ALU.subtract)
                tt_(T1, ar_, bi_, ALU.mult)
                tt_(T2, ai_, br_, ALU.mult)
                tt_(o_i, T1, T2, ALU.add)
                return o_r, o_i
            a2 = cmul(ABR, ABI, ABR, ABI)
            a3 = cmul(a2[0], a2[1], ABR, ABI)
            a4 = cmul(a2[0], a2[1], a2[0], a2[1])
            for (r_, i_) in (a2, a3, a4):
                AW_r.append(r_); AW_i.append(i_)
            for pw in range(1, 5):
                n_ = newp()
                ts_(n_, AW_i[pw], -1.0, None, ALU.mult)
                NAW_i.append(n_)
            R4 = newp()
            tt_(R4, R, R, ALU.mult)
            tt_(R4, R4, R4, ALU.mult)
            assert pi_[0] <= NPT, pi_[0]

            def combo(P, di, part):
                return (di * 16 + P) * 2 + part
            def w1i(q, di, part, tap):
                return ((q * 2 + di) * 2 + part) * 4 + tap
            def cab(j, pw, var):
                return (j * 4 + (pw - 1)) * 2 + var
            def ca32(j, pw, var):
                return (j * 5 + pw) * 2 + var
            def kdi(q, di, d):
                return (q * 2 + di) * 4 + d
            W1c = ar.alloc([64, 128], BF16); b_W1c = Buf("W1c")
            CAc = ar.alloc([256, 32], BF16); b_CAc = Buf("CAc")
            Kd = ar.alloc([32, 128], BF16); b_Kd = Buf("Kd")
            mark_b1 = ar.off
            BnP = ar.alloc([64, 128], F32); b_BnP = [Buf("BnP%d" % i) for i in range(64)]
            op("pool", "memset", BnP, 0.0, writes=b_BnP)
            for di, dn in enumerate(("fwd", "bwd")):
                for part, pn in enumerate(("re", "im")):
                    src = M["b_%s_%s" % (pn, dn)]
                    for g in range(32):
                        P, gi = g // 2, g % 2
                        c0 = 32 * (P % 4) + gi * 16
                        op("sp", "dma_start", out=BnP[gi * 64:(gi + 1) * 64, combo(P, di, part), c0:c0 + 16],
                           in_=dap(src, g * 1024, [[16, 64], [1, 16]]), writes=[b_BnP[combo(P, di, part)]], dma=True)
            issue_ct_loads()
            CT, b_CT = CT_g, b_CT_g
            ctmp = ar.alloc([8, 16], F32); b_ctmp = Buf("ctmp")
            CA32 = ar.alloc([320, 32], F32); b_CA32 = Buf("CA32")
            op("pool", "memset", CA32, 0.0, writes=[b_CA32])
            cw_ = [ar.alloc([16, 16], F32) for _ in range(6)]; b_cw = [Buf("cw%d" % i) for i in range(6)]
            cpr, cpi, ct0, ct1, car, cai = cw_
            b_cpr, b_cpi, b_ct0, b_ct1, b_car, b_cai = b_cw

            def bcP(tile_idx, di):
                t = PT[:, tile_idx, di * 16:(di + 1) * 16]
                return bass.AP(t.tensor, t.offset, [list(t.ap[0]), [1, 16], [0, 16]])

            for di in range(2):
                cr, ci = CT[:, di * 2, :, :], CT[:, di * 2 + 1, :, :]
                rd = [b_CT, b_PT[COR], b_PT[COI]]
                op("dve", "tensor_tensor", ct0, ci, bcP(COI, di), ALU.mult, reads=rd, writes=[b_ct0])
                op("dve", "tensor_tensor", cpr, cr, bcP(COR, di), ALU.mult, reads=rd, writes=[b_cpr])
                op("dve", "tensor_tensor", cpr, cpr, ct0, ALU.subtract, reads=[b_cpr, b_ct0], writes=[b_cpr])
                op("dve", "tensor_tensor", ct1, cr, bcP(COI, di), ALU.mult, reads=rd, writes=[b_ct1])
                op("dve", "tensor_tensor", cpi, ci, bcP(COR, di), ALU.mult, reads=rd, writes=[b_cpi])
                op("dve", "tensor_tensor", cpi, cpi, ct1, ALU.add, reads=[b_cpi, b_ct1], writes=[b_cpi])
                for pw in range(5):
                    if pw == 0:
                        src_r, b_sr = cpr, b_cpr
                        op("dve", "tensor_scalar", cai, cpi, -1.0, None, ALU.mult, reads=[b_cpi], writes=[b_cai])
                    else:
                        rdp = [b_cpr, b_cpi, b_PT[AW_r[pw]], b_PT[AW_i[pw]], b_PT[NAW_i[pw]]]
                        op("dve", "tensor_tensor", ct0, cpi, bcP(AW_i[pw], di), ALU.mult, reads=rdp, writes=[b_ct0])
                        op("dve", "tensor_tensor", car, cpr, bcP(AW_r[pw], di), ALU.mult, reads=rdp, writes=[b_car])
                        op("dve", "tensor_tensor", car, car, ct0, ALU.subtract, reads=[b_car, b_ct0], writes=[b_car])
                        op("dve", "tensor_tensor", ct1, cpi, bcP(AW_r[pw], di), ALU.mult, reads=rdp, writes=[b_ct1])
                        op("dve", "tensor_tensor", cai, cpr, bcP(NAW_i[pw], di), ALU.mult, reads=rdp, writes=[b_cai])
                        op("dve", "tensor_tensor", cai, cai, ct1, ALU.subtract, reads=[b_cai, b_ct1], writes=[b_cai])
                        src_r, b_sr = car, b_car
                    for var, (src_, bsrc_) in enumerate(((src_r, b_sr), (cai, b_cai))):
                        base = ca32(di * 16, pw, var)
                        for gi in range(2):
                            ps_ = slice(gi * 64, (gi + 1) * 64)
                            op("act" if gi == 0 else "dve", "copy" if gi == 0 else "tensor_copy",
                               CA32[ps_, base:base + 151:10, gi * 16:gi * 16 + 16], src_[ps_, :, :],
                               reads=[bsrc_], writes=[b_CA32])
            CA32v = CA32.rearrange("p (j w v) c -> p j w v c", j=32, w=5)
            CAcv = CAc.rearrange("p (j w v) c -> p j w v c", j=32, w=4)
            for j in range(32):
                op("act", "copy", CAcv[:, j, :, :, :], CA32v[:, j, 1:5, :, :], reads=[b_CA32], writes=[b_CAc])
            dg = ar.alloc([48, 128], F32); b_dg = [Buf("dg%d" % i) for i in range(48)]
            def dgi(P4, pw, v):
                return (P4 * 4 + pw) * 3 + v
            b_sb = [Buf("sbank%d" % i) for i in range(8)]
            wtmps = [ar.alloc([128], F32) for _ in range(2)]; b_wtmps = [Buf("wtmp0"), Buf("wtmp1")]
            sbi = 0
            for q in range(4):
                for di in range(2):
                    for P4 in range(4):
                        j = di * 16 + q * 4 + P4
                        for pw in range(4):
                            if pw == 0:
                                op("dve", "tensor_copy", dg[:, dgi(P4, 0, 0), :], idf, reads=[b_idf], writes=[b_dg[dgi(P4, 0, 0)]])
                                continue
                            for v, tl in enumerate((AW_r[pw], AW_i[pw], NAW_i[pw])):
                                op("dve", "tensor_scalar", dg[:, dgi(P4, pw, v), :], idf, PT[:, tl, j:j + 1], None, ALU.mult,
                                   reads=[b_idf, b_PT[tl]], writes=[b_dg[dgi(P4, pw, v)]])
                    for part in range(2):
                        for tap in range(4):
                            pw = (3 - tap) if di == 0 else tap
                            bk = banks[sbi % 8]; bbk = b_sb[sbi % 8]; sbi += 1
                            for P4 in range(4):
                                P = q * 4 + P4
                                o_ = bk[:, P4 * 128:(P4 + 1) * 128]
                                Bre, Bim = BnP[:, combo(P, di, 0), :], BnP[:, combo(P, di, 1), :]
                                bBr, bBi = b_BnP[combo(P, di, 0)], b_BnP[combo(P, di, 1)]
                                if pw == 0:
                                    op("pe", "matmul", o_, Bre if part == 0 else Bim, dg[:, dgi(P4, 0, 0), :], start=True, stop=True,
                                       skip_group_check=True, reads=[bBr, bBi, b_dg[dgi(P4, 0, 0)]], writes=[bbk])
                                elif part == 0:
                                    op("pe", "matmul", o_, Bre, dg[:, dgi(P4, pw, 0), :], start=True, stop=False,
                                       skip_group_check=True, reads=[bBr, b_dg[dgi(P4, pw, 0)]], writes=[bbk])
                                    op("pe", "matmul", o_, Bim, dg[:, dgi(P4, pw, 2), :], start=False, stop=True,
                                       skip_group_check=True, reads=[bBi, b_dg[dgi(P4, pw, 2)]], writes=[bbk])
                                else:
                                    op("pe", "matmul", o_, Bre, dg[:, dgi(P4, pw, 1), :], start=True, stop=False,
                                       skip_group_check=True, reads=[bBr, b_dg[dgi(P4, pw, 1)]], writes=[bbk])
                                    op("pe", "matmul", o_, Bim, dg[:, dgi(P4, pw, 0), :], start=False, stop=True,
                                       skip_group_check=True, reads=[bBi, b_dg[dgi(P4, pw, 0)]], writes=[bbk])
                            wtmp, b_wtmp = wtmps[sbi % 2], b_wtmps[sbi % 2]
                            op("dve", "tensor_reduce", wtmp, bk.rearrange("p (b s) -> p s b", b=4), AX.X, ALU.add,
                               reads=[bbk], writes=[b_wtmp])
                            op("act", "copy", W1c[:, w1i(q, di, part, tap), :], wtmp, reads=[b_wtmp], writes=[b_W1c])
            dsk = ar.alloc([4, 128], F32); b_dsk = Buf("dsk")
            for q in range(4):
                op("dve", "tensor_scalar", dsk[:, q, :], idf, dskipT[:, q:q + 1], None, ALU.mult,
                   reads=[b_idf, b_smallT], writes=[b_dsk])
            for q in range(4):
                for di in range(2):
                    bk = banks[sbi % 8]; bbk = b_sb[sbi % 8]; sbi += 1
                    first = True
                    for d in range(4):
                        if di == 0 and d == 0:
                            op("pe", "matmul", bk[:, 0:128], idf, dsk[:, q, :], start=first, stop=False,
                               skip_group_check=True, reads=[b_idf, b_dsk], writes=[bbk])
                            first = False
                        for P4 in range(4):
                            P = q * 4 + P4
                            j = di * 16 + P
                            o_ = bk[:, d * 128 + 32 * P4:d * 128 + 32 * P4 + 32]
                            for part in range(2):
                                op("pe", "matmul", o_, BnP[:, combo(P, di, part), :], CA32[:, ca32(j, d, part), :],
                                   start=first, stop=False, skip_group_check=True, reads=[b_BnP[combo(P, di, part)], b_CA32], writes=[bbk])
                                first = False
                    op("act", "copy", Kd[:, kdi(q, di, 0):kdi(q, di, 0) + 4, :], bk.rearrange("p (d c) -> p d c", d=4),
                       reads=[bbk], writes=[b_Kd])

            if bstop == 0:
                op("dve", "tensor_copy", ctmp[:, 0, :], Kd[:, 5, 0:16], reads=[b_Kd, b_W1c, b_CAc] + b_PT, writes=[b_ctmp])
                op("pool", "dma_start", out=x2_dst[0:128, 0:128], in_=ctmp, reads=[b_ctmp], dma=True, final=True)
                return
            for (s0, L) in seqs:
                S.barrier()
                ar.off = mark_b1
                nch = L // 512
                uT = ar.alloc([4, L], BF16); b_uT = Buf("uT")
                yacc = ar.alloc([4, L], F32)
                b_yacc = [[Buf("yacc%d_%d" % (q, c)) for c in range(nch)] for q in range(4)]
                for q in range(4):
                    op("sp", "dma_start", out=uT[:, q, :], in_=uT_src[q * 128:(q + 1) * 128, s0:s0 + L],
                       writes=[b_uT], dma=True)
                mark_loop = ar.off
                cosA = ar.alloc([4, 512], F32); sinA = ar.alloc([4, 512], F32)
                cosT = [cosA[:, i, :] for i in range(4)]
                sinT = [sinA[:, i, :] for i in range(4)]
                b_tab = [Buf("tab%d" % i) for i in range(4)]
                bus = [[ar.alloc([512], F32) for _ in range(2)] for _ in range(2)]
                b_bus = [[Buf("bus%d_%d" % (i, k)) for k in range(2)] for i in range(2)]
                Tall = ar.alloc([4, 512], F32)
                T_ = [Tall[:, k, :] for k in range(4)]; bT_ = [Buf("T%d" % k) for k in range(4)]
                tmpA = Tall[:, 0:2, :].rearrange("p a (b c) -> p (a b) c", b=2)
                bre = ar.alloc([512], F32); bim = ar.alloc([512], F32); b_bre = Buf("bre"); b_bim = Buf("bim")
                wre = ar.alloc([512], F32); wim = ar.alloc([512], F32); b_wre = Buf("wre"); b_wim = Buf("wim")
                XS = [[ar.alloc([514], BF16) for _ in range(4)] for _ in range(2)]
                b_XS = [[Buf("XS%d_%d" % (i, k)) for k in range(4)] for i in range(2)]
                cX = ar.alloc([4, 4], BF16); b_cX = [Buf("cX%d" % i) for i in range(4)]
                init = ar.alloc([4, 4], F32); b_init = [Buf("init%d" % i) for i in range(4)]
                wl = ar.alloc([4, 2], F32); b_wl = [Buf("wl%d" % i) for i in range(4)]
                b_y = [Buf("yb%d" % i) for i in range(4)]
                b_bx = [Buf("bx%d" % i) for i in range(4)]
                nchb = L // 2048
                it = 0
                for di in range(2):
                    for q in range(4):
                        j0 = di * 16 + q * 4
                        op("dve", "memset", cosA[:, :, 0:1], 1.0, writes=b_tab)
                        op("dve", "memset", sinA[:, :, 0:1], 0.0, writes=b_tab)
                        for k in range(9):
                            n = 1 << k

                            def bcn(tile_idx, n=n, j0=j0):
                                t = PT[:, tile_idx, j0:j0 + 4]
                                return bass.AP(t.tensor, t.offset, [list(t.ap[0]), [1, 4], [0, n]])
                            cB, sB = bcn(MC[k + 2]), bcn(MS[k + 2])
                            rd = b_tab + [b_PT[MC[k + 2]], b_PT[MS[k + 2]]]
                            op("dve", "tensor_tensor", tmpA[:, :, 0:n], sinA[:, :, 0:n], sB, ALU.mult, reads=rd, writes=[bT_[0], bT_[1]])
                            op("dve", "tensor_tensor", cosA[:, :, n:2 * n], cosA[:, :, 0:n], cB, ALU.mult, reads=rd, writes=b_tab)
                            op("dve", "tensor_tensor", cosA[:, :, n:2 * n], cosA[:, :, n:2 * n], tmpA[:, :, 0:n], ALU.subtract,
                               reads=b_tab + [bT_[0], bT_[1]], writes=b_tab)
                            op("dve", "tensor_tensor", tmpA[:, :, 0:n], cosA[:, :, 0:n], sB, ALU.mult, reads=rd, writes=[bT_[0], bT_[1]])
                            op("dve", "tensor_tensor", sinA[:, :, n:2 * n], sinA[:, :, 0:n], cB, ALU.mult, reads=rd, writes=b_tab)
                            op("dve", "tensor_tensor", sinA[:, :, n:2 * n], sinA[:, :, n:2 * n], tmpA[:, :, 0:n], ALU.add,
                               reads=b_tab + [bT_[0], bT_[1]], writes=b_tab)
                        chunks = list(range(nchb)) if di == 0 else list(range(nchb - 1, -1, -1))
                        items = [(ci_, ch, P4) for ci_, ch in enumerate(chunks) for P4 in range(4)]

                        def R_(ap, di=di):
                            return ap if di == 0 else ap[:, ::-1]

                        def emit_BX(n, di=di, q=q):
                            ci_, ch, P4 = items[n]
                            c0 = ch * 2048
                            z_ = n % 2
                            rows = slice(32 * P4, 32 * P4 + 32)
                            for part in range(2):
                                pb, bpb = banks[4 + z_ * 2 + part], b_bx[z_ * 2 + part]
                                for tap in range(4):
                                    op("pe", "matmul", pb, W1c[rows, w1i(q, di, part, tap), :],
                                       uT[rows, q, c0 + tap:c0 + 2048:4], start=(tap == 0), stop=(tap == 3),
                                       tile_position=(32 * P4, 0), reads=[b_W1c, b_uT], writes=[bpb])
                                op("act", "copy", bus[z_][part], R_(pb), reads=[bpb], writes=[b_bus[z_][part]])

                        def emit_FIR(ch, di=di, q=q):
                            c0 = ch * 2048
                            for b_ in range(4):
                                first = True
                                for tau in range(4):
                                    taps = range(0, tau + 1) if di == 0 else range(tau, 4)
                                    for tp in taps:
                                        t0_ = c0 + b_ * 512 + tp
                                        op("pe", "matmul", banks[b_][:, tau::4], Kd[:, kdi(q, di, abs(tau - tp)), :],
                                           uT[:, q, t0_:c0 + (b_ + 1) * 512:4], start=first, stop=False,
                                           skip_group_check=True, reads=[b_Kd, b_uT], writes=[b_y[b_]])
                                        first = False

                        def emit_DVE(n, di=di, q=q):
                            ci_, ch, P4 = items[n]
                            j = di * 16 + q * 4 + P4
                            cT, sT, bt = cosT[P4], sinT[P4], b_tab[P4]
                            rr = PT[:, R4, j:j + 1]
                            c512, s512 = PT[:, MC[11], j:j + 1], PT[:, MS[11], j:j + 1]
                            z_ = n % 2
                            ur, ui = bus[z_][0], bus[z_][1]
                            bur, bui = b_bus[z_][0], b_bus[z_][1]
                            op("dve", "tensor_tensor", T_[0], ur, cT, ALU.mult, reads=[bur, bt], writes=[bT_[0]])
                            op("dve", "tensor_tensor", T_[1], ui, sT, ALU.mult, reads=[bui, bt], writes=[bT_[1]])
                            op("dve", "tensor_tensor", T_[2], ui, cT, ALU.mult, reads=[bui, bt], writes=[bT_[2]])
                            op("dve", "tensor_tensor", T_[3], ur, sT, ALU.mult, reads=[bur, bt], writes=[bT_[3]])
                            op("dve", "tensor_tensor", bre, T_[0], T_[1], ALU.add, reads=[bT_[0], bT_[1]], writes=[b_bre])
                            op("dve", "tensor_tensor", bim, T_[2], T_[3], ALU.subtract, reads=[bT_[2], bT_[3]], writes=[b_bim])
                            X_, bX_ = XS[z_], b_XS[z_]
                            col = 0 if di == 0 else 513
                            if ci_ == 0:
                                i_re, i_im = 0.0, 0.0
                                ird = []
                                for k in range(4):
                                    op("pool", "memset", X_[k][:, col:col + 1], 0.0, writes=[bX_[k]])
                            else:
                                iv = init[:, P4, :]
                                wlr, wli = wl[:, P4, 0:1], wl[:, P4, 1:2]
                                op("dve", "tensor_scalar", iv[:, 2:3], wli, s512, None, ALU.mult,
                                   reads=[b_wl[P4], b_PT[MS[11]]], writes=[b_init[P4]])
                                op("dve", "scalar_tensor_tensor", iv[:, 0:1], wlr, c512, iv[:, 2:3],
                                   ALU.mult, ALU.subtract, reads=[b_wl[P4], b_PT[MC[11]], b_init[P4]], writes=[b_init[P4]])
                                op("dve", "tensor_scalar", iv[:, 3:4], wlr, s512, None, ALU.mult,
                                   reads=[b_wl[P4], b_PT[MS[11]]], writes=[b_init[P4]])
                                op("dve", "scalar_tensor_tensor", iv[:, 1:2], wli, c512, iv[:, 3:4],
                                   ALU.mult, ALU.add, reads=[b_wl[P4], b_PT[MC[11]], b_init[P4]], writes=[b_init[P4]])
                                i_re, i_im = iv[:, 0:1], iv[:, 1:2]
                                ird = [b_init[P4]]
                                for k in range(4):
                                    op("pool", "tensor_copy", X_[k][:, col:col + 1], cX[:, P4, k:k + 1],
                                       reads=[b_cX[P4]], writes=[bX_[k]])
                            rbc = rr.to_broadcast([128, 512])
                            op("dve", "tensor_tensor_scan", wre, rbc, bre, i_re, ALU.mult, ALU.add,
                               reads=[b_bre, b_PT[R4]] + ird, writes=[b_wre])
                            op("dve", "tensor_tensor_scan", wim, rbc, bim, i_im, ALU.mult, ALU.add,
                               reads=[b_bim, b_PT[R4]] + ird, writes=[b_wim])
                            if ci_ < nchb - 1:
                                op("act", "copy", wl[:, P4, 0:1], wre[:, 511:512], reads=[b_wre], writes=[b_wl[P4]])
                                op("act", "copy", wl[:, P4, 1:2], wim[:, 511:512], reads=[b_wim], writes=[b_wl[P4]])
                            Xw = [R_(X_[k][:, 1:513]) for k in range(4)]
                            op("dve", "tensor_tensor", Xw[0], wre, cT, ALU.mult, reads=[b_wre, bt], writes=[bX_[0]])
                            op("dve", "scalar_tensor_tensor", Xw[1], wim, -1.0, sT, ALU.mult, ALU.mult,
                               reads=[b_wim, bt], writes=[bX_[1]])
                            op("dve", "tensor_tensor", Xw[2], wim, cT, ALU.mult, reads=[b_wim, bt], writes=[bX_[2]])
                            op("dve", "tensor_tensor", Xw[3], wre, sT, ALU.mult, reads=[b_wre, bt], writes=[bX_[3]])
                            if ci_ < nchb - 1:
                                colc = 512 if di == 0 else 1
                                for k in range(4):
                                    op("pool", "tensor_copy", cX[:, P4, k:k + 1], X_[k][:, colc:colc + 1],
                                       reads=[bX_[k]], writes=[b_cX[P4]])

                        def emit_OUT(n, di=di, q=q):
                            ci_, ch, P4 = items[n]
                            j = di * 16 + q * 4 + P4
                            z_ = n % 2
                            X_, bX_ = XS[z_], b_XS[z_]
                            rows = slice(32 * P4, 32 * P4 + 32)
                            sh = 0 if di == 0 else 2
                            for b_ in range(4):
                                for tau in range(4):
                                    pw = tau + 1 if di == 0 else 4 - tau
                                    for k in range(4):
                                        last = (P4 == 3 and tau == 3 and k == 3)
                                        op("pe", "matmul", banks[b_][rows, tau::4], CAc[:, cab(j, pw, 0 if k < 2 else 1), :],
                                           X_[k][:, b_ * 128 + sh:b_ * 128 + sh + 128], start=False, stop=last,
                                           skip_group_check=True, tile_position=(0, 32 * P4),
                                           reads=[b_CAc, bX_[k]], writes=[b_y[b_]])

                        def emit_yevac(ch, di=di, q=q):
                            c0 = ch * 2048
                            for b_ in range(4):
                                ya = yacc[:, q, c0 + b_ * 512:c0 + (b_ + 1) * 512]
                                byq = b_yacc[q][(c0 + b_ * 512) // 512]
                                if di == 0:
                                    op("act", "copy", ya, banks[b_], reads=[b_y[b_]], writes=[byq])
                                else:
                                    op("dve", "tensor_tensor", ya, ya, banks[b_], ALU.add, reads=[b_y[b_], byq], writes=[byq])

                        emit_BX(0)
                        for n, (ci_, ch, P4) in enumerate(items):
                            if n + 1 < len(items):
                                emit_BX(n + 1)
                            if P4 == 0:
                                emit_FIR(ch)
                            emit_DVE(n)
                            emit_OUT(n)
                            if P4 == 3:
                                emit_yevac(ch)
                S.barrier()
                ar.off = mark_loop
                Y2 = ar.alloc([4, 512], F32); b_Y2 = [Buf("Y2_%d" % q) for q in range(4)]
                YG = ar.alloc([4, 512], F32); b_YG = [Buf("YG%d" % q) for q in range(4)]
                YGB = ar.alloc([4, 512], BF16); b_YGB = [Buf("YGB%d" % q) for q in range(4)]
                SQ = ar.alloc([4, 512], F32); b_SQ = [Buf("SQ%d" % q) for q in range(4)]
                GT = ar.alloc([4, 512], F32); b_GT = [Buf("GT%d" % q) for q in range(4)]
                RS = ar.alloc([512], F32); b_RS = Buf("RS")
                YS = ar.alloc([4, 512], BF16); b_YS = Buf("YS")
                b_gl = [Buf("glps%d" % i) for i in range(3)]; b_ms = Buf("msps")
                glb = (4, 5, 6)
                gi_ = 0
                for ch in range(nch):
                    t0 = ch * 512
                    for q in range(4):
                        Yq = yacc[:, q, t0:t0 + 512]
                        byq = b_yacc[q][ch]
                        op("act", "activation", Y2[:, q, :], Yq, AF.Square, reads=[byq], writes=[b_Y2[q]])
                        op("dve", "tensor_scalar", Y2[:, q, :], Y2[:, q, :], 0.044715, 1.0, ALU.mult, ALU.add,
                           reads=[b_Y2[q]], writes=[b_Y2[q]])
                        op("dve", "tensor_tensor", Y2[:, q, :], Y2[:, q, :], Yq, ALU.mult, reads=[b_Y2[q], byq], writes=[b_Y2[q]])
                        op("act", "activation", Y2[:, q, :], Y2[:, q, :], AF.Sigmoid, scale=2.0 * math.sqrt(2.0 / math.pi),
                           reads=[b_Y2[q]], writes=[b_Y2[q]])
                        op("dve", "tensor_tensor", YG[:, q, :], Yq, Y2[:, q, :], ALU.mult, reads=[byq, b_Y2[q]], writes=[b_YG[q]])
                        op("act", "copy", YGB[:, q, :], YG[:, q, :], reads=[b_YG[q]], writes=[b_YGB[q]])
                    for qo in range(4):
                        gl = banks[glb[gi_ % 3]]
                        bgl = b_gl[gi_ % 3]
                        gi_ += 1
                        for qi in range(4):
                            op("pe", "matmul", gl, wglu[:, qi, qo * 128:(qo + 1) * 128], YGB[:, qi, :],
                               start=(qi == 0), stop=(qi == 3), reads=[b_wglu, b_YGB[qi]], writes=[bgl])
                        op("act", "activation", GT[:, qo, :], gl, AF.Sigmoid, bias=bgluT[:, qo:qo + 1],
                           reads=[bgl, b_smallT], writes=[b_GT[qo]])
                        op("dve", "tensor_tensor", YG[:, qo, :], YG[:, qo, :], GT[:, qo, :], ALU.mult,
                           reads=[b_GT[qo], b_YG[qo]], writes=[b_YG[qo]])
                        op("act", "activation", SQ[:, qo, :], YG[:, qo, :], AF.Square, reads=[b_YG[qo]], writes=[b_SQ[qo]])
                    ms = banks[7]
                    for qo in range(4):
                        op("pe", "matmul", ms, ones, SQ[:, qo, :], start=(qo == 0), stop=(qo == 3),
                           reads=[b_ones, b_SQ[qo]], writes=[b_ms])
                    op("dve", "tensor_scalar", RS, ms, 1.0 / 512, EPS, ALU.mult, ALU.add, reads=[b_ms], writes=[b_RS])
                    op("act", "sqrt", RS, RS, reads=[b_RS], writes=[b_RS])
                    op("dve", "reciprocal", RS, RS, reads=[b_RS], writes=[b_RS])
                    for qo in range(4):
                        op("dve", "scalar_tensor_tensor", YS[:, qo, :], YG[:, qo, :], gssmT[:, qo:qo + 1], RS,
                           ALU.mult, ALU.mult, reads=[b_YG[qo], b_smallT, b_RS], writes=[b_YS])
                    op("pool", "dma_start", out=ys_d[:, s0 + t0:s0 + t0 + 512].rearrange("(c p) t -> p c t", p=128),
                       in_=YS, reads=[b_YS], writes=[b_ysd], dma=True)
            ar.off = mark_params
            S.barrier()
            if bstop == 1:
                op("dve", "memset", ones, 1.0, writes=[b_ones])
                op("pool", "dma_start", out=x2_dst[0:128, 0:128], in_=ones, reads=[b_ones, b_ysd], dma=True, final=True)
                return

            wuq = ar.alloc([3, 768], BF16); b_wuq = Buf("wuq")
            wukv = ar.alloc([2, 1024], BF16); b_wukv = Buf("wukv")
            wout = ar.alloc([8, 1024], BF16); b_wout = Buf("wout")
            for k in range(3):
                op("pool", "dma_start", out=wuq[:, k, :], in_=M["w_uq"][k * 128:(k + 1) * 128, :],
                   reads=[b_ysd], writes=[b_wuq], dma=True)
            for k in range(2):
                op("pool", "dma_start", out=wukv[:, k, :], in_=M["w_ukv"][k * 128:(k + 1) * 128, :],
                   reads=[b_ysd], writes=[b_wukv], dma=True)
            for k in range(8):
                op("pool", "dma_start", out=wout[:, k, :], in_=M["w_out"][k * 128:(k + 1) * 128, :],
                   reads=[b_ysd], writes=[b_wout], dma=True)
            gq = ar.alloc([384], F32); gkv = ar.alloc([256], F32); gatt = ar.alloc([512], F32); gmp = ar.alloc([D], F32)
            b_gb = Buf("gbc")
            for t_ap, nm in ((gq, "g_q"), (gkv, "g_kv"), (gatt, "g_att_out"), (gmp, "g_mix_post")):
                op("pool", "dma_start", out=t_ap, in_=M[nm].partition_broadcast(128), reads=[b_ysd], writes=[b_gb], dma=True)
            NT = max(L for _, L in seqs) // 128
            rc = ar.alloc([NT, 16], F32); rs = ar.alloc([NT, 16], F32)
            rcq = ar.alloc([NT, 16], F32); rsq = ar.alloc([NT, 16], F32)
            b_rope = Buf("rope")
            mark_b2 = ar.off
            ri = ar.alloc([NT, 16], I32); rf = ar.alloc([NT, 16], F32); rg = ar.alloc([NT, 16], F32)
            rk = ar.alloc([NT, 16], F32)
            rki = ar.alloc([NT, 16], I32)
            b_r = Buf("ropetmp")
            op("pool", "iota", ri, [[0, NT], [1, 16]], base=0, channel_multiplier=0, reads=[b_ysd], writes=[b_r])
            op("dve", "tensor_copy", rf, ri, reads=[b_r], writes=[b_r])
            op("act", "activation", rf, rf, AF.Exp, scale=-math.log(10000.0) / 16.0, reads=[b_r], writes=[b_r])
            op("pool", "iota", ri, [[128, NT], [0, 16]], base=0, channel_multiplier=1, reads=[b_r], writes=[b_r])
            op("dve", "tensor_copy", rg, ri, reads=[b_r], writes=[b_r])
            op("dve", "tensor_tensor", rg, rg, rf, ALU.mult, reads=[b_r], writes=[b_r])
            for (dst, shift) in ((rs, 0.0), (rc, math.pi / 2)):
                op("dve", "tensor_scalar", rf, rg, shift, None, ALU.add, reads=[b_r], writes=[b_r])
                op("dve", "tensor_scalar", rk, rf, 1.0 / TWO_PI, None, ALU.mult, reads=[b_r], writes=[b_r])
                op("dve", "tensor_copy", rki, rk, reads=[b_r], writes=[b_r])
                op("dve", "tensor_copy", rk, rki, reads=[b_r], writes=[b_r])
                op("dve", "scalar_tensor_tensor", rf, rk, -TWO_PI, rf, ALU.mult, ALU.add, reads=[b_r], writes=[b_r])
                op("dve", "tensor_scalar", rk, rf, math.pi, -TWO_PI, ALU.is_gt, ALU.mult, reads=[b_r], writes=[b_r])
                op("dve", "tensor_tensor", rf, rf, rk, ALU.add, reads=[b_r], writes=[b_r])
                op("dve", "tensor_scalar", rk, rf, -math.pi, TWO_PI, ALU.is_lt, ALU.mult, reads=[b_r], writes=[b_r])
                op("dve", "tensor_tensor", rf, rf, rk, ALU.add, reads=[b_r], writes=[b_r])
                op("act", "activation", dst, rf, AF.Sin, reads=[b_r], writes=[b_rope])
            op("dve", "tensor_scalar", rcq, rc, QSCALE, None, ALU.mult, reads=[b_rope], writes=[b_rope])
            op("dve", "tensor_scalar", rsq, rs, QSCALE, None, ALU.mult, reads=[b_rope], writes=[b_rope])

            def bc8(ap16):
                return bass.AP(ap16.tensor, ap16.offset, [list(ap16.ap[0]), [0, 8], [1, 16]])

            for (s0, L) in seqs:
                S.barrier()
                ar.off = mark_b2
                nkt = L // 128
                KT = ar.alloc([8, L], BF16); b_KT = [Buf("KT%d" % i) for i in range(nkt)]
                VA = ar.alloc([nkt, 8, 65], BF16); b_VA = [Buf("VA%d" % i) for i in range(nkt)]
                zq_ = [ar.alloc([672], F32) for _ in range(2)]; b_zq_ = [Buf("zq0"), Buf("zq1")]
                st2 = ar.alloc([8], F32); b_st = [Buf("st%d" % i) for i in range(8)]
                junk = ar.alloc([D], BF16)
                b_junk = Buf("junkB")
                kvn_ = [ar.alloc([384], BF16) for _ in range(2)]; b_kvn_ = [Buf("kvn0"), Buf("kvn1")]
                kvnT_ = [ar.alloc([3, 128], BF16) for _ in range(2)]; b_kvnT_ = [Buf("kvnT0"), Buf("kvnT1")]
                Kf_ = [ar.alloc([8, 96], BF16) for _ in range(2)]; b_Kf_ = [Buf("Kf0"), Buf("Kf1")]
                rt_ = [[ar.alloc([8, 16], F32) for _ in range(2)] for _ in range(2)]; b_rt_ = [Buf("rt0"), Buf("rt1")]
                QT = ar.alloc([8, 512], BF16); b_QT = Buf("QT")
                PTb = [ar.alloc([512], BF16) for _ in range(3)]; b_PTb = [Buf("PTb0"), Buf("PTb1"), Buf("PTb2")]
                att = ar.alloc([4, 512], F32); b_att = [Buf("att%d" % i) for i in range(4)]
                rec = ar.alloc([4], F32); b_rec = Buf("rec")
                yat = ar.alloc([512], BF16); b_yat = Buf("yat")
                ymT = ar.alloc([8, 128], BF16); b_ymT = Buf("ymT")
                x1t = ar.alloc([D], F32); b_x1t = Buf("x1t")
                qs_ = [ar.alloc([768], F32) for _ in range(2)]; b_qs_ = [Buf("qs0"), Buf("qs1")]
                tt2 = [ar.alloc([512], F32) for _ in range(2)]; b_tt2 = [Buf("tt2_0"), Buf("tt2_1")]
                bb = [Buf("b2bank%d" % i) for i in range(8)]
                b_S = [bb[3], bb[4], bb[7]]
                b_O = [bb[5], bb[6]]
                tpb_ = [banks[0].bitcast(BF16), banks[0].bitcast(BF16)]
                b_tpb_ = [bb[0], bb[0]]
                mmb_ = [(1, 2), (5, 6)]

                class _Par:
                    v = 0
                par = _Par()

                def rstd_col(col, n_feat):
                    c = st2[:, col:col + 1]
                    op("dve", "tensor_scalar", c, c, 1.0 / n_feat, EPS, ALU.mult, ALU.add, reads=[b_st[col]], writes=[b_st[col]])
                    op("act", "sqrt", c, c, reads=[b_st[col]], writes=[b_st[col]])
                    op("dve", "reciprocal", c, c, reads=[b_st[col]], writes=[b_st[col]])

                def norm_T(src_ap, n_feat, g_ap, col):
                    p_ = par.v
                    col = col + 5 * p_
                    b_zq, kvn, b_kvn, kvnT, b_kvnT, tpb, b_tpb = b_zq_[p_], kvn_[p_], b_kvn_[p_], kvnT_[p_], b_kvnT_[p_], tpb_[p_], b_tpb_[p_]
                    nk = n_feat // 128
                    op("act", "activation", junk[:, 0:n_feat], src_ap, AF.Square, accum_out=st2[:, col:col + 1],
                       reads=[b_zq], writes=[b_junk, b_st[col]])
                    rstd_col(col, n_feat)
                    op("dve", "scalar_tensor_tensor", kvn[:, 0:n_feat], src_ap, st2[:, col:col + 1], g_ap,
                       ALU.mult, ALU.mult, reads=[b_zq, b_st[col], b_gb], writes=[b_kvn])
                    for k in range(nk):
                        op("pe", "transpose", tpb[:, k * 128:(k + 1) * 128], kvn[:, k * 128:(k + 1) * 128], idb,
                           reads=[b_kvn, b_idb], writes=[b_tpb])
                    op("act", "copy", kvnT[:, 0:nk, :], tpb[:, 0:nk * 128].rearrange("p (a b) -> p a b", a=nk),
                       reads=[b_tpb], writes=[b_kvnT])

                for kt in range(nkt):
                    r0 = s0 + kt * 128
                    par.v = kt % 2
                    p_ = par.v
                    zq, b_zq, kvnT, b_kvnT, Kf, b_Kf, rt, b_rt = zq_[p_], b_zq_[p_], kvnT_[p_], b_kvnT_[p_], Kf_[p_], b_Kf_[p_], rt_[p_], b_rt_[p_]
                    tpb, b_tpb = tpb_[p_], b_tpb_[p_]
                    mbk = mmb_[p_]
                    b_mm = [bb[mbk[0]], bb[mbk[1]]]
                    op("sp", "dma_start", out=zq, in_=zq_src[r0:r0 + 128, :], writes=[b_zq], dma=True)
                    norm_T(zq[:, 384:640], 256, gkv, 0)
                    for n in range(2):
                        for kc in range(2):
                            op("pe", "matmul", banks[mbk[n]], kvnT[:, kc, :], wukv[:, kc, n * 512:(n + 1) * 512],
                               start=(kc == 0), stop=(kc == 1), reads=[b_kvnT, b_wukv], writes=[b_mm[n]])
                    x1_, x2_ = zq[:, 640:656], zq[:, 656:672]
                    cs_, sn_ = rc[:, kt, :], rs[:, kt, :]
                    r1, r2 = rt[0][:, 0, :], rt[1][:, 0, :]
                    op("dve", "tensor_tensor", r1, x1_, cs_, ALU.mult, reads=[b_zq, b_rope], writes=[b_rt])
                    op("dve", "tensor_tensor", r2, x2_, sn_, ALU.mult, reads=[b_zq, b_rope], writes=[b_rt])
                    op("dve", "tensor_tensor", Kf[:, 0, 64:80], r1, r2, ALU.subtract, reads=[b_rt], writes=[b_Kf])
                    op("dve", "tensor_tensor", r1, x1_, sn_, ALU.mult, reads=[b_zq, b_rope, b_Kf], writes=[b_rt])
                    op("dve", "tensor_tensor", r2, x2_, cs_, ALU.mult, reads=[b_zq, b_rope], writes=[b_rt])
                    op("dve", "tensor_tensor", Kf[:, 0, 80:96], r1, r2, ALU.add, reads=[b_rt], writes=[b_Kf])
                    for h in range(1, 8):
                        op("pool", "tensor_copy", Kf[:, h, 64:96], Kf[:, 0, 64:96], reads=[b_Kf], writes=[b_Kf])
                    for n in range(2):
                        bv = banks[mbk[n]].rearrange("p (h d) -> p h d", h=4)
                        op("act", "copy", Kf[:, 4 * n:4 * n + 4, 0:64], bv[:, :, 0:64], reads=[b_mm[n]], writes=[b_Kf])
                        op("dve", "tensor_copy", VA[:, kt, 4 * n:4 * n + 4, 0:64], bv[:, :, 64:128],
                           reads=[b_mm[n]], writes=[b_VA[kt]])
                    op("pool", "memset", VA[:, kt, :, 64:65], 1.0, writes=[b_VA[kt]])
                    for h in range(8):
                        op("pe", "transpose", tpb[0:96, h * 128:(h + 1) * 128], Kf[:, h, :], idb,
                           reads=[b_Kf, b_idb], writes=[b_tpb])
                    op("act", "copy", KT[0:96, :, kt * 128:(kt + 1) * 128],
                       tpb[0:96, :].rearrange("p (h t) -> p h t", h=8), reads=[b_tpb], writes=[b_KT[kt]])

                if bstop == 2:
                    op("dve", "memset", ones, 1.0, reads=b_KT[0:nkt] + b_VA[0:nkt], writes=[b_ones])
                    op("pool", "dma_start", out=x2_dst[0:128, 0:128], in_=ones, reads=[b_ones], dma=True, final=True)
                    return
                for qb in range(L // 512):
                    for sub in range(4):
                        tix = qb * 4 + sub
                        r0 = s0 + tix * 128
                        par.v = tix % 2
                        p_ = par.v
                        zq, b_zq, kvnT, b_kvnT, Qf, b_Qf, rt, b_rt = zq_[p_], b_zq_[p_], kvnT_[p_], b_kvnT_[p_], Kf_[p_], b_Kf_[p_], rt_[p_], b_rt_[p_]
                        tpb, b_tpb = tpb_[p_], b_tpb_[p_]
                        qs, b_qs = qs_[p_], b_qs_[p_]
                        mbk = mmb_[p_]
                        b_mm = [bb[mbk[0]], bb[mbk[1]]]
                        mm2 = pall[:, mbk[0] * 512:mbk[0] * 512 + 1024]
                        op("sp", "dma_start", out=zq, in_=zq_src[r0:r0 + 128, :], writes=[b_zq], dma=True)
                        norm_T(zq[:, 0:384], 384, gq, 1)
                        for (n0, nn) in ((0, 512), (512, 256)):
                            n = n0 // 512
                            for kc in range(3):
                                op("pe", "matmul", mm2[:, n0:n0 + nn], kvnT[:, kc, :], wuq[:, kc, n0:n0 + nn],
                                   start=(kc == 0), stop=(kc == 2), reads=[b_kvnT, b_wuq], writes=[b_mm[n]])
                        op("dve", "tensor_copy", qs[:, 0:512], banks[mbk[0]], reads=[b_mm[0]], writes=[b_qs])
                        op("dve", "tensor_copy", qs[:, 512:768], banks[mbk[1]][:, 0:256], reads=[b_mm[1]], writes=[b_qs])
                        qv = qs.rearrange("p (h d) -> p h d", h=8)
                        rdq = [b_qs]
                        op("dve", "tensor_scalar", Qf[:, :, 0:64], qv[:, :, 0:64], QSCALE, None, ALU.mult,
                           reads=rdq, writes=[b_Qf])
                        cq, sq_ = bc8(rcq[:, tix, :]), bc8(rsq[:, tix, :])
                        op("dve", "tensor_tensor", rt[0], qv[:, :, 64:80], cq, ALU.mult, reads=rdq + [b_rope], writes=[b_rt])
                        op("dve", "tensor_tensor", rt[1], qv[:, :, 80:96], sq_, ALU.mult, reads=rdq + [b_rope], writes=[b_rt])
                        op("dve", "tensor_tensor", Qf[:, :, 64:80], rt[0], rt[1], ALU.subtract, reads=[b_rt], writes=[b_Qf])
                        op("dve", "tensor_tensor", rt[0], qv[:, :, 64:80], sq_, ALU.mult, reads=rdq + [b_rope, b_Qf], writes=[b_rt])
                        op("dve", "tensor_tensor", rt[1], qv[:, :, 80:96], cq, ALU.mult, reads=rdq + [b_rope], writes=[b_rt])
                        op("dve", "tensor_tensor", Qf[:, :, 80:96], rt[0], rt[1], ALU.add, reads=[b_rt], writes=[b_Qf])
                        for h in range(8):
                            op("pe", "transpose", tpb[0:96, h * 128:(h + 1) * 128], Qf[:, h, :], idb,
                               reads=[b_Qf, b_idb], writes=[b_tpb])
                        op("dve", "tensor_copy", QT[0:96, :, sub * 128:(sub + 1) * 128],
                           tpb[0:96, :].rearrange("p (h t) -> p h t", h=8), reads=[b_tpb], writes=[b_QT])
                    if bstop == 3:
                        op("dve", "memset", ones, 1.0, reads=[b_QT], writes=[b_ones])
                        op("pool", "dma_start", out=x2_dst[0:128, 0:128], in_=ones, reads=[b_ones], dma=True, final=True)
                        return
                    items = [(h, kt) for h in range(8) for kt in range(nkt)]
                    sbank = (3, 4, 7)

                    def emit_S(i):
                        h, kt = items[i]
                        op("pe", "matmul", banks[sbank[i % 3]], KT[0:96, h, kt * 128:(kt + 1) * 128], QT[0:96, h, :],
                           start=True, stop=True, reads=[b_KT[kt], b_QT], writes=[b_S[i % 3]])

                    emit_S(0)
                    if len(items) > 1:
                        emit_S(1)
                    for i, (h, kt) in enumerate(items):
                        if i + 2 < len(items):
                            emit_S(i + 2)
                        Ob = banks[5 + h % 2]
                        bO = b_O[h % 2]
                        z_ = i % 3
                        op("act", "activation", PTb[z_], banks[sbank[z_]], AF.Exp, reads=[b_S[z_]], writes=[b_PTb[z_]])
                        for sub in range(4):
                            op("pe", "matmul", Ob[:, sub * 65:(sub + 1) * 65], PTb[z_][:, sub * 128:(sub + 1) * 128],
                               VA[:, kt, h, :], start=(kt == 0 and sub == 0), stop=(kt == nkt - 1),
                               skip_group_check=True, reads=[b_PTb[z_], b_VA[kt]], writes=[bO])
                        if kt == nkt - 1:
                            Ov = Ob[:, 0:260].rearrange("p (s d) -> p s d", s=4)
                            op("dve", "reciprocal", rec, Ov[:, :, 64], reads=[bO], writes=[b_rec])
                            for sub in range(4):
                                op("dve", "tensor_scalar", att[:, sub, h * 64:(h + 1) * 64], Ov[:, sub, 0:64],
                                   rec[:, sub:sub + 1], None, ALU.mult, reads=[bO, b_rec], writes=[b_att[sub]])
                    if bstop == 4:
                        op("dve", "memset", ones, 1.0, reads=b_att, writes=[b_ones])
                        op("pool", "dma_start", out=x2_dst[0:128, 0:128], in_=ones, reads=[b_ones], dma=True, final=True)
                        return
                    tpb, b_tpb = tpb_[0], b_tpb_[0]
                    b_mm = [bb[1], bb[2]]
                    for sub in range(4):
                        tix = qb * 4 + sub
                        r0 = s0 + tix * 128
                        op("act", "activation", junk[:, 0:512], att[:, sub, :], AF.Square, accum_out=st2[:, 2:3],
                           reads=[b_att[sub]], writes=[b_junk, b_st[2]])
                        rstd_col(2, 512)
                        op("dve", "scalar_tensor_tensor", yat, att[:, sub, :], st2[:, 2:3], gatt, ALU.mult, ALU.mult,
                           reads=[b_att[sub], b_st[2], b_gb], writes=[b_yat])
                        for k in range(4):
                            op("pe", "transpose", tpb[:, k * 128:(k + 1) * 128], yat[:, k * 128:(k + 1) * 128], idb,
                               reads=[b_yat, b_idb], writes=[b_tpb])
                        op("act", "copy", ymT[:, 4:8, :], tpb[:, 0:512].rearrange("p (a b) -> p a b", a=4),
                           reads=[b_tpb], writes=[b_ymT])
                        op("sp", "dma_start", out=ymT[:, 0:4, :],
                           in_=ys_d[:, r0:r0 + 128].rearrange("(c p) t -> p c t", p=128),
                           reads=[b_ysd], writes=[b_ymT], dma=True)
                        op("sp", "dma_start", out=x1t, in_=x1_src[r0:r0 + 128, :], writes=[b_x1t], dma=True)
                        for n in range(2):
                            for k in range(8):
                                op("pe", "matmul", banks[1 + n], ymT[:, k, :], wout[:, k, n * 512:(n + 1) * 512],
                                   start=(k == 0), stop=(k == 7), reads=[b_ymT, b_wout], writes=[b_mm[n]])
                        for n in range(2):
                            op("act", "activation", junk[:, 0:512], banks[1 + n], AF.Square,
                               accum_out=st2[:, 3 + n:4 + n], reads=[b_mm[n]], writes=[b_junk, b_st[3 + n]])
                        op("dve", "tensor_tensor", st2[:, 3:4], st2[:, 3:4], st2[:, 4:5], ALU.add,
                           reads=[b_st[3], b_st[4]], writes=[b_st[3]])
                        rstd_col(3, D)
                        for n in range(2):
                            op("dve", "scalar_tensor_tensor", tt2[n], banks[1 + n], st2[:, 3:4],
                               gmp[:, n * 512:(n + 1) * 512], ALU.mult, ALU.mult,
                               reads=[b_mm[n], b_st[3], b_gb], writes=[b_tt2[n]])
                            op("pool", "tensor_tensor", x1t[:, n * 512:(n + 1) * 512], tt2[n],
                               x1t[:, n * 512:(n + 1) * 512], ALU.add, reads=[b_tt2[n], b_x1t], writes=[b_x1t])
                        op("pool", "dma_start", out=x2_dst[r0:r0 + 128, :], in_=x1t, reads=[b_x1t],
                           writes=[b_x2d], dma=True, final=(stage == "B"))

        if stage == "A":
            ffn_phase(1, x_in, x1_d, True, ntiles=2)
        elif stage == "B":
            mixer_phase([(0, LB)])
        elif stage == "C":
            ffn_phase(2, x_in, y_out, False, ntiles=3)
        else:
            ffn_phase(1, x_in, x1_d, True)
            S.barrier()
            mixer_phase([(0, LP), (LP, LS)])
            S.barrier()
            ffn_phase(2, x2_d, y_out, False)
        S.emit(st)
    return nc


_NC_CACHE = {}


def kernel(**inputs):
    xp = np.asarray(inputs["x_prompt"], dtype=np.float32)
    xs = np.asarray(inputs["x_sample"], dtype=np.float32)
    if "nc" not in _NC_CACHE:
        _NC_CACHE["nc"] = build("full")
    nc = _NC_CACHE["nc"]
    shared = {}
    shared.update(ffn_inputs(inputs))
    shared.update(mixer_inputs(inputs))
    in_maps = []
    for b in range(8):
        m = dict(shared)
        m["x"] = np.ascontiguousarray(np.concatenate([xp[b], xs[b]], axis=0))
        in_maps.append(m)
    res = run_bass_kernel_spmd(nc, in_maps, core_ids=list(range(8)))
    y_prompt = np.stack([res.results[b]["y"][:LP] for b in range(8)], axis=0).astype(np.float32)
    y_sample = np.stack([res.results[b]["y"][LP:] for b in range(8)], axis=0).astype(np.float32)
    return (y_prompt, y_sample)


def ffn_inputs(d):
    m = {}
    _names = ("g_ffn1_pre", "w_ffn1_gate", "w_ffn1_up", "w_ffn1_down", "g_ffn1_post",
              "g_ffn2_pre", "w_ffn2_gate", "w_ffn2_up", "w_ffn2_down", "g_ffn2_post", "g_mix_pre", "w_in")
    assert all(n in d for n in _names)
    for f in (1, 2):
        m["g_ffn%d_pre" % f] = np.ascontiguousarray(d["g_ffn%d_pre" % f], dtype=np.float32).reshape(1, -1)
        m["w_ffn%d_gate" % f] = np.ascontiguousarray(d["w_ffn%d_gate" % f][0])
        m["w_ffn%d_up" % f] = np.ascontiguousarray(d["w_ffn%d_up" % f][0])
        m["w_ffn%d_down" % f] = np.ascontiguousarray(d["w_ffn%d_down" % f][0])
        m["g_ffn%d_post" % f] = np.ascontiguousarray(d["g_ffn%d_post" % f]).reshape(1, -1)
    m["g_mix_pre"] = np.ascontiguousarray(d["g_mix_pre"]).reshape(1, -1)
    m["w_in"] = np.ascontiguousarray(d["w_in"][0])
    return m


def mixer_inputs(d):
    m = {}
    for nm in ("w_glu", "w_uq", "w_ukv", "w_out"):
        m[nm] = np.ascontiguousarray(d[nm][0])
    for nm in ("d_skip", "b_glu", "g_ssm_out", "g_q", "g_kv", "g_att_out", "g_mix_post"):
        m[nm] = np.ascontiguousarray(d[nm]).reshape(1, -1)
    for dn in ("fwd", "bwd"):
        for nm in ("lam_re_", "lam_im_", "log_dt_", "b_re_", "b_im_", "c_re_", "c_im_"):
            m[nm + dn] = np.ascontiguousarray(d[nm + dn]).reshape(-1)
    return m
```
